# Optimizing a Trainium2 kernel written in Bass

```python
import math
import jax, jax.numpy as jnp
from jax import lax
import numpy as np

D_MODEL = 2048
BATCH = 4
SEQ = 8192
DEPTH = 1

CHUNK = 64
Q_BLOCK = 128
ROPE_THETA = 10000.0
EPS = 1e-6

DA_HEADS = 8
DA_HEAD_DIM = 128
DA_V_DIM = 2 * DA_HEAD_DIM
DA_QK_WIDTH = DA_HEADS * 2 * DA_HEAD_DIM
DA_WIDTH = DA_HEADS * DA_V_DIM

RW_HEAD_DIM = 64
RW_HEADS = D_MODEL // RW_HEAD_DIM
RW_WIDTH = RW_HEADS * RW_HEAD_DIM
RW_DECAY_RANK = max(32, int(round(math.sqrt(D_MODEL) * 1.8 / 32)) * 32)
RW_AAA_RANK = max(32, int(round(math.sqrt(D_MODEL) * 1.8 / 32)) * 32)
RW_GATE_RANK = max(32, int(round(0.6 * D_MODEL ** 0.8 / 32)) * 32)
RW_GN_EPS = 1e-5 * RW_HEAD_DIM
RW_TOTAL = 3 * RW_WIDTH + RW_DECAY_RANK + RW_AAA_RANK + RW_GATE_RANK

D_FF = -(-8 * D_MODEL // (3 * 256)) * 256

OFF_DA_Q = 0
OFF_DA_K = OFF_DA_Q + DA_QK_WIDTH
OFF_DA_V = OFF_DA_K + DA_QK_WIDTH
OFF_RW = OFF_DA_V + DA_WIDTH
OFF_GATE = OFF_RW + RW_TOTAL
N_IN = OFF_GATE + 2 * D_MODEL

kernel_name = "hybrid_diffattn_rwkv7_gated_block"


def _rmsnorm(x, g, eps=EPS):
    xf = x.astype(jnp.float32)
    y = xf * lax.rsqrt(jnp.mean(xf * xf, axis=-1, keepdims=True) + eps)
    return (y * g.astype(jnp.float32)).astype(x.dtype)


def _rope(t, pos):
    half = t.shape[-1] // 2
    inv = ROPE_THETA ** (-jnp.arange(half, dtype=jnp.float32) / half)
    ang = pos.astype(jnp.float32)[:, None] * inv[None, :]
    cos = jnp.cos(ang)[:, None, None, :]
    sin = jnp.sin(ang)[:, None, None, :]
    tf = t.astype(jnp.float32)
    t1, t2 = tf[..., :half], tf[..., half:]
    out = jnp.concatenate([t1 * cos - t2 * sin, t2 * cos + t1 * sin], axis=-1)
    return out.astype(t.dtype)


def _diff_attention(q, k, v, lam, subln_g, lambda_init):
    B, S = q.shape[0], q.shape[1]
    n_blocks = S // Q_BLOCK
    scale = DA_HEAD_DIM ** -0.5
    key_chunk = jnp.arange(S) // CHUNK
    qb = jnp.moveaxis(q.reshape(B, n_blocks, Q_BLOCK, DA_HEADS, 2, DA_HEAD_DIM), 1, 0)
    neg = jnp.finfo(jnp.float32).min

    def one_block(args):
        q_blk, i = args
        q_chunk = (i * Q_BLOCK + jnp.arange(Q_BLOCK)) // CHUNK
        mask = key_chunk[None, :] <= q_chunk[:, None]
        s = jnp.einsum('bqhcd,bkhcd->bhcqk', q_blk, k).astype(jnp.float32) * scale
        s = jnp.where(mask, s, neg)
        p = jax.nn.softmax(s, axis=-1)
        p = p[:, :, 0] - lam * p[:, :, 1]
        return jnp.einsum('bhqk,bkhe->bqhe', p.astype(v.dtype), v)

    o = lax.map(one_block, (qb, jnp.arange(n_blocks)))
    o = jnp.moveaxis(o, 0, 1).reshape(B, S, DA_HEADS, DA_V_DIM)
    o = _rmsnorm(o, subln_g, eps=1e-5) * (1.0 - lambda_init)
    return o.reshape(B, S, DA_WIDTH)


def _rwkv7_step(state, inp):
    r_t, w_t, k_t, v_t, a_t, b_t = inp
    sa = jnp.einsum('bhij,bhj->bhi', state, a_t)
    state = (state * w_t[:, :, None, :] + sa[..., None] * b_t[:, :, None, :]
             + v_t[..., None] * k_t[:, :, None, :])
    y = jnp.einsum('bhij,bhj->bhi', state, r_t)
    return state, y


def _rwkv7(p, mu, w0, w_up, a0, a_up, g_up, k_k, k_a, r_k, ln_w, ln_b):
    B, S, _ = p.shape
    prev = jnp.pad(p, ((0, 0), (1, 0), (0, 0)))[:, :S]
    xs = p + mu * (prev - p)
    o0, o1, o2, o3 = RW_WIDTH, 2 * RW_WIDTH, 3 * RW_WIDTH, 3 * RW_WIDTH + RW_DECAY_RANK
    o4 = o3 + RW_AAA_RANK
    r, k, v = xs[..., :o0], xs[..., o0:o1], xs[..., o1:o2]
    wl, al, gl = xs[..., o2:o3], xs[..., o3:o4], xs[..., o4:]
    w = -jax.nn.softplus(-(w0 + jnp.tanh(wl) @ w_up)) - 0.5
    decay = jnp.exp(-jnp.exp(w.astype(jnp.float32)))
    a = jax.nn.sigmoid(a0 + al @ a_up)
    g = jax.nn.sigmoid(gl) @ g_up

    def heads(t):
        return t.reshape(B, S, RW_HEADS, RW_HEAD_DIM).astype(jnp.float32)

    kk = heads(k * k_k)
    kk = kk / jnp.maximum(jnp.linalg.norm(kk, axis=-1, keepdims=True), 1e-12)
    k = k * (1.0 + (a - 1.0) * k_a)
    rh, kh, vh, ah, wh = heads(r), heads(k), heads(v), heads(a), heads(decay)

    def tm(t):
        return jnp.swapaxes(t, 0, 1)

    state0 = jnp.zeros((B, RW_HEADS, RW_HEAD_DIM, RW_HEAD_DIM), jnp.float32)
    _, y = lax.scan(_rwkv7_step, state0,
                    (tm(rh), tm(wh), tm(kh), tm(vh), tm(-kk), tm(kk * ah)))
    y = tm(y)
    mean = jnp.mean(y, axis=-1, keepdims=True)
    var = jnp.mean(jnp.square(y - mean), axis=-1, keepdims=True)
    y = ((y - mean) * lax.rsqrt(var + RW_GN_EPS)).reshape(B, S, RW_WIDTH)
    y = y * ln_w.astype(jnp.float32) + ln_b.astype(jnp.float32)
    bonus = jnp.sum(rh * kh * r_k.astype(jnp.float32), axis=-1, keepdims=True) * vh
    y = y + bonus.reshape(B, S, RW_WIDTH)
    return (y * g.astype(jnp.float32)).astype(p.dtype)


def setup_inputs(seed: int = 0) -> dict:
    key = jax.random.key(seed)
    ks = jax.random.split(key, 28)
    L = DEPTH

    def nrm(k, shape, scale):
        return jax.random.normal(k, shape, jnp.float32) * scale

    return {
        "x": nrm(ks[0], (BATCH, SEQ, D_MODEL), 1.0),
        "w_in": nrm(ks[1], (L, D_MODEL, N_IN), D_MODEL ** -0.5),
        "b_gate": nrm(ks[2], (L, 2 * D_MODEL), 0.01),
        "g_mix_pre": 1.0 + nrm(ks[3], (L, D_MODEL), 0.05),
        "g_mix_post": 1.0 + nrm(ks[4], (L, D_MODEL), 0.05),
        "g_ffn_pre": 1.0 + nrm(ks[5], (L, D_MODEL), 0.05),
        "g_ffn_post": 1.0 + nrm(ks[6], (L, D_MODEL), 0.05),
        "da_lambda_q": nrm(ks[7], (L, 2, DA_HEAD_DIM), 0.1),
        "da_lambda_k": nrm(ks[8], (L, 2, DA_HEAD_DIM), 0.1),
        "da_subln_g": 1.0 + nrm(ks[9], (L, DA_V_DIM), 0.05),
        "rw_mu": jax.random.uniform(ks[10], (L, RW_TOTAL), jnp.float32),
        "rw_w0": jax.random.uniform(ks[11], (L, RW_WIDTH), jnp.float32, minval=-6.0, maxval=0.0),
        "rw_w_up": nrm(ks[12], (L, RW_DECAY_RANK, RW_WIDTH), 0.5 * RW_DECAY_RANK ** -0.5),
        "rw_a0": nrm(ks[13], (L, RW_WIDTH), 0.1),
        "rw_a_up": nrm(ks[14], (L, RW_AAA_RANK, RW_WIDTH), 0.5 * RW_AAA_RANK ** -0.5),
        "rw_g_up": nrm(ks[15], (L, RW_GATE_RANK, RW_WIDTH), RW_GATE_RANK ** -0.5),
        "rw_k_k": 0.85 + nrm(ks[16], (L, RW_WIDTH), 0.05),
        "rw_k_a": 1.0 + nrm(ks[17], (L, RW_WIDTH), 0.05),
        "rw_r_k": nrm(ks[18], (L, RW_HEADS, RW_HEAD_DIM), 0.1),
        "rw_ln_w": 1.0 + nrm(ks[19], (L, RW_WIDTH), 0.05),
        "rw_ln_b": nrm(ks[20], (L, RW_WIDTH), 0.01),
        "w_branch_a": nrm(ks[21], (L, DA_WIDTH, D_MODEL), DA_WIDTH ** -0.5),
        "w_branch_b": nrm(ks[22], (L, RW_WIDTH, D_MODEL), RW_WIDTH ** -0.5),
        "w_out": nrm(ks[23], (L, D_MODEL, D_MODEL), D_MODEL ** -0.5),
        "w_ffn_in": nrm(ks[24], (L, D_MODEL, 2 * D_FF), D_MODEL ** -0.5),
        "w_ffn_out": nrm(ks[25], (L, D_FF, D_MODEL), D_FF ** -0.5),
    }


def reference(x, w_in, b_gate, g_mix_pre, g_mix_post, g_ffn_pre, g_ffn_post,
              da_lambda_q, da_lambda_k, da_subln_g,
              rw_mu, rw_w0, rw_w_up, rw_a0, rw_a_up, rw_g_up, rw_k_k, rw_k_a, rw_r_k,
              rw_ln_w, rw_ln_b, w_branch_a, w_branch_b, w_out, w_ffn_in, w_ffn_out):
    B, S, _ = x.shape
    pos = jnp.arange(S, dtype=jnp.int32)
    for l in range(DEPTH):
        lambda_init = 0.8 - 0.6 * math.exp(-0.3 * l)
        h = _rmsnorm(x, g_mix_pre[l])
        proj = h @ w_in[l]
        q = _rope(proj[..., OFF_DA_Q:OFF_DA_K].reshape(B, S, DA_HEADS, 2, DA_HEAD_DIM), pos)
        k = _rope(proj[..., OFF_DA_K:OFF_DA_V].reshape(B, S, DA_HEADS, 2, DA_HEAD_DIM), pos)
        v = proj[..., OFF_DA_V:OFF_RW].reshape(B, S, DA_HEADS, DA_V_DIM)
        lq = da_lambda_q[l].astype(jnp.float32)
        lk = da_lambda_k[l].astype(jnp.float32)
        lam = jnp.exp(jnp.sum(lq[0] * lk[0])) - jnp.exp(jnp.sum(lq[1] * lk[1])) + lambda_init
        o_a = _diff_attention(q, k, v, lam, da_subln_g[l], lambda_init)
        o_b = _rwkv7(proj[..., OFF_RW:OFF_GATE], rw_mu[l], rw_w0[l], rw_w_up[l], rw_a0[l],
                     rw_a_up[l], rw_g_up[l], rw_k_k[l], rw_k_a[l], rw_r_k[l],
                     rw_ln_w[l], rw_ln_b[l])
        gates = jax.nn.sigmoid(proj[..., OFF_GATE:] + b_gate[l])
        mix = (gates[..., :D_MODEL] * (o_a @ w_branch_a[l])
               + gates[..., D_MODEL:] * (o_b @ w_branch_b[l])) @ w_out[l]
        x = x + _rmsnorm(mix, g_mix_post[l])
        h = _rmsnorm(x, g_ffn_pre[l])
        gu = h @ w_ffn_in[l]
        f = jax.nn.silu(gu[..., :D_FF]) * gu[..., D_FF:]
        x = x + _rmsnorm(f @ w_ffn_out[l], g_ffn_post[l])
    return x
```

```python
import math
from contextlib import ExitStack
import numpy as np
import ml_dtypes
import concourse.bass as bass
import concourse.mybir as mybir
from concourse.bass_utils import run_bass_kernel_spmd

F32 = mybir.dt.float32
BF16 = mybir.dt.bfloat16
AF = mybir.ActivationFunctionType
ALU = mybir.AluOpType
AX = mybir.AxisListType

S = 8192
D = 2048
SO = 4096
NS_DMA = 32
ENGS = ("pe", "act", "dve", "pool", "sp")
SAME_ENG_SYNC = True
DEBUG = {}


class Buf:
    __slots__ = ("t", "w", "r")

    def __init__(self, t):
        self.t = t
        self.w = {}
        self.r = {}


class Prog:
    def __init__(self, nc):
        self.nc = nc
        self.ops = {e: [] for e in ENGS}
        self.known = {e: {} for e in ENGS}
        self.ndma = 0
        self.ndma_sw = 0
        self.dma_seq = [0] * NS_DMA
        self.stack = ExitStack()
        self.ncc = 0
        self.scopes = []

    def sb(self, name, shape, dt):
        st = self.scopes[-1] if self.scopes else self.stack
        return Buf(st.enter_context(self.nc.sbuf_tensor(name, list(shape), dt)))

    def push_scope(self):
        self.scopes.append(ExitStack())

    def pop_scope(self):
        self.scopes.pop().close()

    def ps(self, name):
        return Buf(self.stack.enter_context(self.nc.psum_tensor(name, [128, 512], F32)))

    def dram(self, name, shape, dt):
        return Buf(self.nc.dram_tensor(name, list(shape), dt))

    def get_pid(self, e):
        if getattr(self, "_pid", None) is None:
            self._pid = e.partition_id() % 2
        return self._pid

    def op(self, eng, fn, r=(), w=(), acc=False, dma=False, cc=False):
        deps = {}

        def add(d):
            for k, i in d.items():
                if deps.get(k, 0) < i:
                    deps[k] = i

        for b in r:
            add(b.w)
        for b in w:
            add(b.w)
            add(b.r)
        ops = self.ops[eng]
        my_idx = len(ops) + 1
        if dma:
            half = NS_DMA // 2
            if eng == "pool":
                slot = self.ndma_sw % half
                self.ndma_sw += 1
            else:
                slot = half + self.ndma % half
                self.ndma += 1
            seq = self.dma_seq[slot] + 1
            self.dma_seq[slot] = seq
            if seq > 1:
                add({("d", slot): seq - 1})
            ev = (("d", slot), seq)
        elif cc:
            self.ncc += 1
            ev = (("c", 0), self.ncc)
        else:
            ev = (eng, my_idx)
        kn = self.known[eng]
        waits = []
        for k, i in deps.items():
            if k == eng and (eng == "pe" or not SAME_ENG_SYNC):
                continue
            if kn.get(k, 0) >= i:
                continue
            kn[k] = i
            waits.append((k, i))
            if not isinstance(k, tuple):
                self.ops[k][i - 1][2] = True
        ops.append([waits, fn, False, ev if (dma or cc) else None, self._snap(fn)])
        for b in r:
            if b.r.get(ev[0], 0) < ev[1]:
                b.r[ev[0]] = ev[1]
        for b in w:
            if acc:
                b.w[ev[0]] = ev[1]
            else:
                b.w = {ev[0]: ev[1]}
            b.r = {}
        return ev

    @staticmethod
    def _snap(fn):
        out = []
        for c in (fn.__closure__ or ()):
            try:
                out.append(id(c.cell_contents))
            except ValueError:
                out.append(None)
        return out

    def barrier(self):
        latest = {}
        for e in ENGS:
            n = len(self.ops[e])
            while n > 0 and (self.ops[e][n - 1][1] is None or self.ops[e][n - 1][3] is not None):
                n -= 1
            if n > 0:
                latest[e] = n
        for s in range(NS_DMA):
            if self.dma_seq[s] > 0:
                latest[("d", s)] = self.dma_seq[s]
        if self.ncc:
            latest[("c", 0)] = self.ncc
        for e in ENGS:
            kn = self.known[e]
            waits = []
            for k, i in latest.items():
                if k == e:
                    continue
                if kn.get(k, 0) >= i:
                    continue
                kn[k] = i
                waits.append((k, i))
                if not isinstance(k, tuple):
                    self.ops[k][i - 1][2] = True
            self.ops[e].append([waits, None, False, None, None])

    def emit(self):
        nc = self.nc
        st = self.stack
        self.csem = st.enter_context(nc.semaphore("ccs"))
        self.sem = {e: st.enter_context(nc.semaphore("s_" + e)) for e in ENGS}
        self.dsem = [st.enter_context(nc.semaphore("d_%d" % i)) for i in range(NS_DMA)]
        self.val = {}
        for e in ENGS:
            c = 0
            v = []
            for o in self.ops[e]:
                if o[2]:
                    c += 1
                v.append(c)
            self.val[e] = v
        block = st.enter_context(nc.Block())

        @block.tensor
        def _(e):
            self._emit("pe", e)

        @block.scalar
        def _(e):
            self._emit("act", e)

        @block.vector
        def _(e):
            self._emit("dve", e)

        @block.gpsimd
        def _(e):
            self._emit("pool", e)

        @block.sync
        def _(e):
            self._emit("sp", e)

    def _emit(self, name, e):
        for waits, fn, flagged, ev, snap in self.ops[name]:
            if fn is not None and snap != self._snap(fn):
                names = fn.__code__.co_freevars
                bad = [n for n, a, b in zip(names, snap, self._snap(fn)) if a != b]
                raise RuntimeError("late-bound closure variable(s) %s in op at line %d" % (bad, fn.__code__.co_firstlineno))
            for k, i in waits:
                if isinstance(k, tuple):
                    if k[0] == "d":
                        e.wait_ge(self.dsem[k[1]], 16 * i)
                    else:
                        e.wait_ge(self.csem, i)
                else:
                    e.wait_ge(self.sem[k], self.val[k][i - 1])
            if fn is None:
                continue
            ins = fn(e)
            if ev is not None:
                if ev[0][0] == "d":
                    ins.then_inc(self.dsem[ev[0][1]], 16)
                else:
                    ins.then_inc(self.csem)
            elif flagged:
                ins.then_inc(self.sem[name], 1)


class PsumRing:
    def __init__(self, P, n=8, bufs=None):
        self.b = bufs if bufs is not None else [P.ps("psb%d" % i) for i in range(n)]
        self.i = 0

    def get(self):
        b = self.b[self.i % len(self.b)]
        self.i += 1
        return b


class Ring:
    def __init__(self, P, name, n, shape, dt):
        self.b = [P.sb("%s%d" % (name, i), shape, dt) for i in range(n)]
        self.i = 0

    def get(self):
        b = self.b[self.i % len(self.b)]
        self.i += 1
        return b


def bf(psbuf):
    return psbuf.t[:, :].bitcast(BF16)


def build(stop_after=99, nsb=8, ncore=8):
    nc = bass.Bass("TRN2", target_bir_lowering=False)
    P = Prog(nc)

    def din(name, shape, dt=F32):
        return Buf(nc.dram_tensor(name, list(shape), dt, kind="ExternalInput"))

    S_att = nsb * 1024
    SO_ = S_att // 2
    x_in = din("x", [S_att, D])
    xo_in = din("xo", [SO_, D])
    wg_in = din("wg", [D, 4096])
    wa_in = din("wa", [D, D])
    wb_in = din("wb", [D, D])
    wo_in = din("wo", [D, D])
    wf1_in = din("wf1", [D, 11264])
    wf2_in = din("wf2", [5632, D])
    pv_in = din("pvec", [128, 80])
    w1_in = din("w1", [D, 6592])
    gpre_in = din("gpre", [1, D])
    cos_in = din("cosT", [128, S])
    sin_in = din("sinT", [128, S])
    rmat_in = din("rmat", [128, 128])
    ident_in = din("ident", [128, 128])
    lqk_in = din("lqk", [1, 512])
    rwp_in = din("rwp", [128, 8, 10])
    mul_in = din("mul", [128, 4])
    wup_in = din("wup", [96, 1024])
    aup_in = din("aup", [96, 1024])
    gup_in = din("gup", [256, 1024])
    mSU_in = din("mSU", [64, 512])
    mIU_in = din("mIU", [64, 512])
    mSL_in = din("mSL", [64, 512])
    I8_in = din("I8", [64, 512])
    scanm_in = din("scanm", [128, 512])
    bones_in = din("bones", [128, 128])
    gsub_in = din("gsub", [1, 256])
    out_t = Buf(nc.dram_tensor("out", [SO_, D], F32, kind="ExternalOutput"))
    dbg = {}

    def dout(name, shape, dt):
        dbg[name] = Buf(nc.dram_tensor(name, list(shape), dt, kind="ExternalOutput"))
        return dbg[name]

    w1b = P.dram("w1b", [D, 6592], BF16)
    if DEBUG.get("p1"):
        qkT = dout("qkT", [16, 128, nsb * 1024], BF16)
        rwT = dout("rwT", [3520, nsb * 1024], BF16)
        vtm = dout("vtm", [nsb * 1024, 1024], BF16)
        hTd = dout("hTd", [128, 16, 512], BF16)
    else:
        qkT = P.dram("qkT", [16, 128, S], BF16)
        rwT = P.dram("rwT", [3520, S], BF16)
        vtm = P.dram("vtm", [S, 1024], BF16)

    identb = P.sb("identb", [128, 128], BF16)
    identf = P.sb("identf", [128, 128], F32)
    rmatb = P.sb("rmatb", [128, 128], BF16)
    gpre_b = P.sb("gpre_b", [128, D], F32)
    P.op("pool", lambda e: e.dma_start(out=identb.t[:, :], in_=ident_in.t[:, :]), w=[identb], dma=True)
    P.op("sp", lambda e: e.dma_start(out=identf.t[:, :], in_=ident_in.t[:, :]), w=[identf], dma=True)
    P.op("pool", lambda e: e.dma_start(out=rmatb.t[:, :], in_=rmat_in.t[:, :]), w=[rmatb], dma=True)
    P.op("sp", lambda e: e.dma_start(out=gpre_b.t[:, :], in_=gpre_in.t[0:1, :].partition_broadcast(128)),
         w=[gpre_b], dma=True)

    def convert(dst, src, rows, rstep=256):
        for r0 in range(0, rows, rstep):
            P.op("pool", lambda e, r0=r0: e.dma_start(out=dst.t[r0:r0 + rstep, :], in_=src.t[r0:r0 + rstep, :]),
                 r=[src], w=[dst], acc=True, dma=True)

    convert(w1b, w1_in, D)

    psr = PsumRing(P, 8)
    epsb = P.sb("epsb", [128, 4], F32)
    P.op("pool", lambda e: e.memset(epsb.t[:, 0:1], 1e-6), w=[epsb])
    P.op("pool", lambda e: e.memset(epsb.t[:, 1:2], 1e-5), w=[epsb], acc=True)

    def phase1():
        st2 = ExitStack()
        hg = Ring(P, "hg", 4, [128, 16, 512], BF16)
        xt_r = Ring(P, "xt", 3, [128, D], F32)
        xn_r = Ring(P, "xn", 2, [128, D], BF16)
        junk = P.sb("junk", [128, D], BF16)
        ss_r = Ring(P, "ss", 4, [128, 2], F32)
        wt_r = Ring(P, "wt", 3, [128, 16, 512], BF16)
        qb_r = Ring(P, "qb", 3, [128, 512], BF16)
        t1_r = Ring(P, "t1", 3, [128, 512], F32)
        t2_r = Ring(P, "t2", 3, [128, 512], F32)
        ob_r = Ring(P, "ob", 4, [128, 512], BF16)
        cs_r = Ring(P, "cs", 2, [128, 512], F32)
        sn_r = Ring(P, "sn", 2, [128, 512], F32)

        def build_group(G, hbuf):
            for tt in range(4):
                t0 = G * 512 + tt * 128
                xt = xt_r.get()
                P.op("sp", lambda e, xt=xt, t0=t0: e.dma_start(out=xt.t[:, :], in_=x_in.t[t0:t0 + 128, :]),
                     w=[xt], dma=True)
                ss = ss_r.get()
                P.op("pool", lambda e, ss=ss: e.memset(ss.t[:, :], 0.0), w=[ss])
                P.op("act", lambda e, xt=xt, ss=ss: e.activation(out=junk.t[:, :], in_=xt.t[:, :], func=AF.Square,
                                                                 accum_out=ss.t[:, 0:1]),
                     r=[xt], w=[junk, ss])
                P.op("act", lambda e, ss=ss: e.activation(out=ss.t[:, 1:2], in_=ss.t[:, 0:1], func=AF.Sqrt,
                                                          bias=epsb.t[:, 0:1], scale=1.0 / D), r=[ss, epsb], w=[ss])
                P.op("dve", lambda e, ss=ss: e.reciprocal(out=ss.t[:, 0:1], in_=ss.t[:, 1:2]), r=[ss], w=[ss])
                xn = xn_r.get()
                P.op("dve", lambda e, xt=xt, ss=ss, xn=xn: e.scalar_tensor_tensor(
                    out=xn.t[:, :], in0=xt.t[:, :], scalar=ss.t[:, 0:1], in1=gpre_b.t[:, :],
                    op0=ALU.mult, op1=ALU.mult), r=[xt, ss, gpre_b], w=[xn])
                for half in range(2):
                    pb = psr.get()
                    for j in range(8):
                        kc = half * 8 + j
                        P.op("pe", lambda e, pb=pb, xn=xn, kc=kc, j=j: e.transpose(
                            out=bf(pb)[:, j * 128:(j + 1) * 128], in_=xn.t[:, kc * 128:(kc + 1) * 128],
                            identity=identb.t[:, :]), r=[xn, identb], w=[pb], acc=(j > 0))
                    P.op("act", lambda e, pb=pb, hbuf=hbuf, half=half, tt=tt: e.activation(
                        out=hbuf.t[:, half * 8:half * 8 + 8, tt * 128:(tt + 1) * 128],
                        in_=bf(pb).rearrange("p (j t) -> p j t", j=8), func=AF.Copy),
                        r=[pb], w=[hbuf], acc=True)

        blocks = []
        for c0 in range(0, 6592, 512):
            n = min(512, 6592 - c0)
            blocks.append((c0, n))

        for sbk in range(nsb):
            hb = [hg.get(), hg.get()]
            for g in range(2):
                build_group(sbk * 2 + g, hb[g])
            if DEBUG.get("p1") and sbk == 0:
                P.op("sp", lambda e, hb0=hb[0]: e.dma_start(out=dbg["hTd"].t[:, :, :], in_=hb0.t[:, :, :]), r=[hb[0]], w=[dbg["hTd"]], dma=True)
            for (c0, n) in blocks:
                wt = wt_r.get()
                P.op("sp", lambda e, wt=wt, c0=c0, n=n: e.dma_start(
                    out=wt.t[:, :, 0:n], in_=w1b.t[:, c0:c0 + n].rearrange("(kc p) c -> p kc c", p=128)),
                    r=[w1b], w=[wt], dma=True)
                if 2048 <= c0 < 3072:
                    for g in range(2):
                        for tt in range(4):
                            t0 = (sbk * 2 + g) * 512 + tt * 128
                            pb = psr.get()
                            for kc in range(16):
                                P.op("pe", lambda e, pb=pb, hbg=hb[g], tt=tt, kc=kc, wt=wt: e.matmul(
                                    pb.t[:, 0:512], lhsT=hbg.t[:, kc, tt * 128:(tt + 1) * 128], rhs=wt.t[:, kc, 0:512],
                                    start=(kc == 0), stop=(kc == 15)), r=[hb[g], wt], w=[pb], acc=(kc > 0))
                            ob = ob_r.get()
                            P.op("act", lambda e, pb=pb, ob=ob: e.activation(out=ob.t[:, :], in_=pb.t[:, :], func=AF.Copy),
                                 r=[pb], w=[ob])
                            P.op("pool", lambda e, ob=ob, t0=t0, c0=c0: e.dma_start(
                                out=vtm.t[t0:t0 + 128, c0 - 2048:c0 - 2048 + 512], in_=ob.t[:, :]),
                                r=[ob], w=[vtm], acc=True, dma=True)
                    continue
                for g in range(2):
                    tk0 = (sbk * 2 + g) * 512
                    is_qk = c0 < 2048
                    if is_qk:
                        cs = cs_r.get()
                        sn = sn_r.get()
                        P.op("sp", lambda e, cs=cs, tk0=tk0: e.dma_start(out=cs.t[:, :], in_=cos_in.t[:, tk0:tk0 + 512]),
                             w=[cs], dma=True)
                        P.op("sp", lambda e, sn=sn, tk0=tk0: e.dma_start(out=sn.t[:, :], in_=sin_in.t[:, tk0:tk0 + 512]),
                             w=[sn], dma=True)
                    for cc in range((n + 127) // 128):
                        m = min(128, n - cc * 128)
                        col = c0 + cc * 128
                        pb = psr.get()
                        for kc in range(16):
                            P.op("pe", lambda e, pb=pb, hbg=hb[g], cc=cc, kc=kc, wt=wt, m=m: e.matmul(
                                pb.t[0:m, 0:512], lhsT=wt.t[:, kc, cc * 128:cc * 128 + m], rhs=hbg.t[:, kc, :],
                                start=(kc == 0), stop=(kc == 15)), r=[hb[g], wt], w=[pb], acc=(kc > 0))
                        if is_qk:
                            qb = qb_r.get()
                            P.op("act", lambda e, pb=pb, qb=qb: e.activation(out=qb.t[:, :], in_=pb.t[:, :], func=AF.Copy),
                                 r=[pb], w=[qb])
                            pr = psr.get()
                            P.op("pe", lambda e, pr=pr, qb=qb: e.matmul(pr.t[:, 0:512], lhsT=rmatb.t[:, :], rhs=qb.t[:, :],
                                                                         start=True, stop=True), r=[qb, rmatb], w=[pr])
                            t1 = t1_r.get()
                            t2 = t2_r.get()
                            P.op("pool", lambda e, t1=t1, qb=qb, cs=cs: e.tensor_tensor(
                                out=t1.t[:, :], in0=qb.t[:, :], in1=cs.t[:, :], op=ALU.mult), r=[qb, cs], w=[t1])
                            P.op("dve", lambda e, t2=t2, pr=pr, sn=sn: e.tensor_tensor(
                                out=t2.t[:, :], in0=pr.t[:, :], in1=sn.t[:, :], op=ALU.mult), r=[pr, sn], w=[t2])
                            ob = ob_r.get()
                            P.op("dve", lambda e, t1=t1, t2=t2, ob=ob: e.tensor_tensor(
                                out=ob.t[:, :], in0=t1.t[:, :], in1=t2.t[:, :], op=ALU.add), r=[t1, t2], w=[ob])
                            ch = col // 128
                            P.op("pool", lambda e, ob=ob, ch=ch, tk0=tk0: e.dma_start(
                                out=qkT.t[ch, :, tk0:tk0 + 512], in_=ob.t[:, :]), r=[ob], w=[qkT], acc=True, dma=True)
                        else:
                            ob = ob_r.get()
                            P.op("act", lambda e, pb=pb, ob=ob, m=m: e.activation(out=ob.t[0:m, :], in_=pb.t[0:m, :], func=AF.Copy),
                                 r=[pb], w=[ob])
                            row = col - 3072
                            P.op("pool", lambda e, ob=ob, row=row, m=m, tk0=tk0: e.dma_start(
                                out=rwT.t[row:row + m, tk0:tk0 + 512], in_=ob.t[0:m, :]), r=[ob], w=[rwT], acc=True, dma=True)

    P.push_scope()
    phase1()
    P.barrier()
    P.pop_scope()
    NBLK = S_att // 512
    if DEBUG.get("p2"):
        oT = dout("oT", [NBLK, 2048, 512], BF16)
    else:
        oT = P.dram("oT", [NBLK, 2048, 512], BF16)

    def phase_attn():
        LAMBDA_INIT = 0.8 - 0.6 * math.exp(-0.3 * 0)
        NG = S_att // 512
        NKB = S_att // 128
        kT = [P.sb("kT%d" % c, [128, S_att], BF16) for c in range(2)]
        v1 = P.sb("v1", [128, NKB, 257], BF16)
        qg_r = Ring(P, "qg", 3, [128, 512], BF16)
        pT_r = Ring(P, "pT", 4, [128, 512], BF16)
        o_c = [P.sb("o_c%d" % c, [128, 4, 257], F32) for c in range(2)]
        lqk = P.sb("lqk_sb", [128, 512], F32)
        lam = P.sb("lam_sb", [128, 8], F32)
        gsb = P.sb("gsb", [128, 256], F32)
        sm = Ring(P, "sm", 4, [128, 8], F32)
        ta_r = Ring(P, "ta", 2, [128, 256], F32)
        to_r = Ring(P, "to", 2, [128, 256], F32)
        jk = P.sb("jk2", [128, 256], BF16)
        obf_r = Ring(P, "obf", 2, [128, 256], BF16)
        ost_r = Ring(P, "ost", 2, [128, 2, 512], BF16)
        acc = psr.b[0:4]
        st_b = psr.b[4:7]
        misc = psr.b[7]
        sti = [0]

        P.op("sp", lambda e: e.dma_start(out=lqk.t[:, :], in_=lqk_in.t[0:1, :].partition_broadcast(128)), w=[lqk], dma=True)
        P.op("sp", lambda e: e.dma_start(out=gsb.t[:, :], in_=gsub_in.t[0:1, :].partition_broadcast(128)), w=[gsb], dma=True)
        P.op("dve", lambda e: e.tensor_tensor(out=lqk.t[:, 0:256], in0=lqk.t[:, 0:256], in1=lqk.t[:, 256:512], op=ALU.mult),
             r=[lqk], w=[lqk])
        P.op("dve", lambda e: e.tensor_reduce(out=lam.t[:, 0:2], in_=lqk.t[:, 0:256].rearrange("p (a b) -> p a b", a=2),
                                              axis=AX.X, op=ALU.add), r=[lqk], w=[lam])
        P.op("act", lambda e: e.activation(out=lam.t[:, 2:4], in_=lam.t[:, 0:2], func=AF.Exp), r=[lam], w=[lam])
        P.op("dve", lambda e: e.scalar_tensor_tensor(out=lam.t[:, 4:5], in0=lam.t[:, 3:4], scalar=-LAMBDA_INIT, in1=lam.t[:, 2:3],
                                                     op0=ALU.add, op1=ALU.subtract), r=[lam], w=[lam])
        P.op("dve", lambda e: e.tensor_scalar(out=gsb.t[:, :], in0=gsb.t[:, :], scalar1=1.0 - LAMBDA_INIT, scalar2=None,
                                              op0=ALU.mult), r=[gsb], w=[gsb])
        for hd in range(4):
            for c in range(2):
                P.op("sp", lambda e, c=c, hd=hd: e.dma_start(out=kT[c].t[:, :], in_=qkT.t[8 + hd * 2 + c, :, 0:S_att]),
                     r=[qkT], w=[kT[c]], dma=True)
            P.op("sp", lambda e, hd=hd: e.dma_start(
                out=v1.t[:, :, 0:256], in_=vtm.t[0:S_att, hd * 256:(hd + 1) * 256].rearrange("(kb p) e -> p kb e", p=128)),
                r=[vtm], w=[v1], dma=True)
            P.op("pool", lambda e: e.memset(v1.t[:, :, 256:257], 1.0), w=[v1], acc=True)
            for j in range(NG):
                for c in range(2):
                    qg = qg_r.get()
                    P.op("sp", lambda e, qg=qg, c=c, hd=hd, j=j: e.dma_start(
                        out=qg.t[:, :], in_=qkT.t[hd * 2 + c, :, j * 512:(j + 1) * 512]), r=[qkT], w=[qg], dma=True)
                    for kb in range(4 * j + 4):
                        m = max(kb - 4 * j, 0)
                        c0 = m * 128
                        sps = st_b[sti[0] % 3]
                        sti[0] += 1
                        P.op("pe", lambda e, sps=sps, c=c, kb=kb, qg=qg, c0=c0: e.matmul(
                            sps.t[:, c0:512], lhsT=kT[c].t[:, kb * 128:(kb + 1) * 128], rhs=qg.t[:, c0:512],
                            start=True, stop=True), r=[kT[c], qg], w=[sps])
                        pT = pT_r.get()
                        P.op("act", lambda e, sps=sps, pT=pT, c0=c0: e.activation(
                            out=pT.t[:, c0:512], in_=sps.t[:, c0:512], func=AF.Exp, scale=128 ** -0.5), r=[sps], w=[pT])
                        if kb >= 4 * j:
                            P.op("pool", lambda e, pT=pT, c0=c0: e.memset(pT.t[64:128, c0:c0 + 64], 0.0), w=[pT], acc=True)
                        for qs in range(m, 4):
                            P.op("pe", lambda e, qs=qs, pT=pT, kb=kb, j=j: e.matmul(
                                acc[qs].t[:, 0:257], lhsT=pT.t[:, qs * 128:(qs + 1) * 128], rhs=v1.t[:, kb, :],
                                start=(kb == 0), stop=(kb == 4 * j + qs)), r=[pT, v1], w=[acc[qs]], acc=(kb > 0))
                    for qs in range(4):
                        P.op("act", lambda e, qs=qs, c=c: e.activation(out=o_c[c].t[:, qs, :], in_=acc[qs].t[:, 0:257], func=AF.Copy),
                             r=[acc[qs]], w=[o_c[c]], acc=True)
                ost = ost_r.get()
                for qs in range(4):
                    s_ = sm.get()
                    P.op("dve", lambda e, s_=s_, qs=qs: e.reciprocal(out=s_.t[:, 0:1], in_=o_c[0].t[:, qs, 256:257]), r=[o_c[0]], w=[s_])
                    P.op("dve", lambda e, s_=s_, qs=qs: e.reciprocal(out=s_.t[:, 1:2], in_=o_c[1].t[:, qs, 256:257]), r=[o_c[1]], w=[s_])
                    P.op("dve", lambda e, s_=s_: e.tensor_tensor(out=s_.t[:, 2:3], in0=s_.t[:, 1:2], in1=lam.t[:, 4:5], op=ALU.mult),
                         r=[s_, lam], w=[s_])
                    ta = ta_r.get()
                    P.op("dve", lambda e, s_=s_, ta=ta, qs=qs: e.tensor_scalar(out=ta.t[:, :], in0=o_c[0].t[:, qs, 0:256],
                                                                         scalar1=s_.t[:, 0:1], scalar2=None, op0=ALU.mult),
                         r=[s_, o_c[0]], w=[ta])
                    to = to_r.get()
                    P.op("dve", lambda e, s_=s_, ta=ta, to=to, qs=qs: e.scalar_tensor_tensor(
                        out=to.t[:, :], in0=o_c[1].t[:, qs, 0:256], scalar=s_.t[:, 2:3], in1=ta.t[:, :],
                        op0=ALU.mult, op1=ALU.add), r=[s_, o_c[1], ta], w=[to])
                    P.op("pool", lambda e, s_=s_: e.memset(s_.t[:, 3:4], 0.0), w=[s_], acc=True)
                    P.op("act", lambda e, s_=s_, to=to: e.activation(out=jk.t[:, :], in_=to.t[:, :], func=AF.Square,
                                                                     accum_out=s_.t[:, 3:4]), r=[to, s_], w=[jk, s_])
                    P.op("act", lambda e, s_=s_: e.activation(out=s_.t[:, 4:5], in_=s_.t[:, 3:4], func=AF.Sqrt,
                                                              bias=epsb.t[:, 1:2], scale=1.0 / 256), r=[s_, epsb], w=[s_])
                    P.op("dve", lambda e, s_=s_: e.reciprocal(out=s_.t[:, 5:6], in_=s_.t[:, 4:5]), r=[s_], w=[s_])
                    obf = obf_r.get()
                    P.op("dve", lambda e, s_=s_, to=to, obf=obf: e.scalar_tensor_tensor(
                        out=obf.t[:, :], in0=to.t[:, :], scalar=s_.t[:, 5:6], in1=gsb.t[:, :], op0=ALU.mult, op1=ALU.mult),
                        r=[to, s_, gsb], w=[obf])
                    for ec in range(2):
                        P.op("pe", lambda e, obf=obf, ec=ec: e.transpose(
                            out=bf(misc)[:, ec * 128:(ec + 1) * 128], in_=obf.t[:, ec * 128:(ec + 1) * 128], identity=identb.t[:, :]),
                            r=[obf, identb], w=[misc], acc=(ec > 0))
                    P.op("act", lambda e, ost=ost, qs=qs: e.activation(
                        out=ost.t[:, :, qs * 128:(qs + 1) * 128], in_=bf(misc)[:, 0:256].rearrange("p (a t) -> p a t", a=2),
                        func=AF.Copy), r=[misc], w=[ost], acc=True)
                for ec in range(2):
                    P.op("sp", lambda e, ost=ost, ec=ec, hd=hd, j=j: e.dma_start(
                        out=oT.t[j, hd * 256 + ec * 128:hd * 256 + (ec + 1) * 128, :], in_=ost.t[:, ec, :]),
                        r=[ost], w=[oT], acc=True, dma=True)

    P.push_scope()
    phase_attn()
    P.barrier()
    P.pop_scope()

    if stop_after <= 2:
        P.emit()
        return nc, P, dbg

    def phase_rwkv():
        v3 = lambda ap: ap.rearrange("p (c t) -> p c t", c=8)
        y3 = lambda ap: ap.rearrange("p (c i) -> p c i", c=8)
        psr_all = psr
        psr7 = PsumRing(P, bufs=psr_all.b[0:7])
        pS = psr_all.b[7]
        NB = S_att // 512
        C0 = math.exp(-0.5)
        rwp = P.sb("rwp_sb", [128, 8, 10], F32)
        mul = P.sb("mul_sb", [128, 4], F32)
        wupb = P.sb("wupb", [96, 1024], BF16)
        aupb = P.sb("aupb", [96, 1024], BF16)
        gupb = P.sb("gupb", [128, 2, 1024], BF16)
        mSU = P.sb("mSU_sb", [64, 512], F32)
        mIU = P.sb("mIU_sb", [64, 512], F32)
        mSL = P.sb("mSL_sb", [64, 512], F32)
        I8 = P.sb("I8_sb", [64, 512], BF16)
        scanm = P.sb("scanm_sb", [128, 512], F32)
        bones = P.sb("bones_sb", [128, 128], BF16)
        gn_eps = P.sb("gn_eps", [64, 1], F32)
        for (dst, src, q) in ((rwp, rwp_in, "sp"), (mul, mul_in, "sp"), (mSU, mSU_in, "sp"), (mIU, mIU_in, "sp"),
                              (mSL, mSL_in, "sp"), (scanm, scanm_in, "sp"), (wupb, wup_in, "pool"), (aupb, aup_in, "pool"),
                              (I8, I8_in, "pool"), (bones, bones_in, "pool")):
            nd = len(dst.t.shape)
            if nd == 3:
                P.op(q, lambda e, dst=dst, src=src: e.dma_start(out=dst.t[:, :, :], in_=src.t[:, :, :]), w=[dst], dma=True)
            else:
                P.op(q, lambda e, dst=dst, src=src: e.dma_start(out=dst.t[:, :], in_=src.t[:, :]), w=[dst], dma=True)
        P.op("pool", lambda e: e.dma_start(out=gupb.t[:, :, :], in_=gup_in.t[:, :].rearrange("(k p) c -> p k c", p=128)),
             w=[gupb], dma=True)
        P.op("pool", lambda e: e.memset(gn_eps.t[:, :], 64e-5), w=[gn_eps])

        S32 = P.sb("S32", [128, 8, 64], F32)
        STb = P.sb("STb", [128, 8, 64], BF16)
        P.op("pool", lambda e: e.memset(S32.t[:, :, :], 0.0), w=[S32])
        P.op("pool", lambda e: e.memset(STb.t[:, :, :], 0.0), w=[STb])

        AR = P.sb("AR", [128, 8, 8, 128], BF16)
        Kt = P.sb("Kt", [128, 8, 512], BF16)
        Bt = P.sb("Bt", [128, 8, 512], BF16)
        VF = P.sb("VF", [128, 8, 512], BF16)
        KH = P.sb("KH", [128, 8, 512], BF16)
        BH = P.sb("BH", [128, 8, 512], BF16)
        GF = P.sb("GF", [128, 8, 512], BF16)
        BON = P.sb("BON", [128, 8, 512], BF16)
        YT = P.sb("YT", [128, 8, 512], BF16)
        pCs = P.sb("pCs", [128, 8, 8], F32)
        Lw = P.sb("Lw", [128, 513], BF16)
        La = P.sb("La", [128, 513], BF16)
        Lg = P.sb("Lg", [128, 2, 513], BF16)
        tanw = P.sb("tanw", [128, 512], BF16)
        xsal = P.sb("xsal", [128, 512], BF16)
        sigg = P.sb("sigg", [128, 2, 512], BF16)
        f32r = {}

        def T32(name, n=1):
            if name not in f32r:
                f32r[name] = Ring(P, "rw_" + name, n, [128, 512], F32)
            return f32r[name].get()

        Lr_r = Ring(P, "Lr", 1, [128, 513], BF16)
        Lk_r = Ring(P, "Lk", 1, [128, 513], BF16)
        Lv_r = Ring(P, "Lv", 1, [128, 513], BF16)
        sq_r = Ring(P, "sqr", 2, [128, 512], BF16)
        ost_r = Ring(P, "rwost", 2, [128, 512], BF16)
        tm_r = {k: Ring(P, "tm" + k, 2, [64, 1024], BF16) for k in ("v", "k", "b")}
        m_r = {k: Ring(P, "m" + k, 2, [64, 512], BF16) for k in ("ak", "ab", "rk", "rb", "mt")}
        x_r = [Ring(P, "xr%d" % h, 2, [64, 512], BF16) for h in range(2)]
        y_r = [Ring(P, "yr%d" % h, 2, [64, 512], BF16) for h in range(2)]
        t_r = [Ring(P, "tr%d" % h, 2, [64, 512], BF16) for h in range(2)]
        tt_r = [Ring(P, "ttr%d" % h, 2, [64, 512], BF16) for h in range(2)]
        w1s_r = Ring(P, "w1s", 1, [64, 512], F32)
        wtm_r = Ring(P, "wtm", 2, [64, 512], BF16)
        utm_r = Ring(P, "utm", 2, [64, 512], BF16)
        y1s_r = Ring(P, "y1s", 1, [64, 512], F32)
        ytm_r = Ring(P, "ytm", 2, [64, 512], F32)
        ysq_r = Ring(P, "ysq", 2, [64, 512], F32)
        yh_r = Ring(P, "yh", 2, [64, 1024], BF16)
        st_r = Ring(P, "gnst", 4, [64, 48], F32)

        def shift_mix(L, n, mu_ap, parts, out_fn):
            d = T32("d")
            P.op("dve", lambda e: e.tensor_tensor(out=d.t[0:parts, :], in0=L[0], in1=L[1], op=ALU.subtract), r=[L[2]], w=[d])
            xs = T32("xsl")
            P.op("dve", lambda e: e.scalar_tensor_tensor(out=xs.t[0:parts, :], in0=d.t[0:parts, :], scalar=mu_ap, in1=L[1],
                                                         op0=ALU.mult, op1=ALU.add), r=[d, L[2], mul, rwp], w=[xs])
            return xs

        def load_shift(q, Lb, rows, row0, t0, view3=None):
            if t0 == 0:
                P.op("pool", lambda e: e.memset(Lb.t[:, 0:1] if view3 is None else Lb.t[:, :, 0:1], 0.0), w=[Lb])
                if view3 is None:
                    P.op(q, lambda e: e.dma_start(out=Lb.t[0:rows, 1:513], in_=rwT.t[row0:row0 + rows, 0:512]),
                         r=[rwT], w=[Lb], acc=True, dma=True)
                else:
                    for k in range(2):
                        P.op(q, lambda e, k=k: e.dma_start(out=Lb.t[:, k, 1:513], in_=rwT.t[row0 + k * 128:row0 + (k + 1) * 128, 0:512]),
                             r=[rwT], w=[Lb], acc=True, dma=True)
            else:
                if view3 is None:
                    P.op(q, lambda e: e.dma_start(out=Lb.t[0:rows, 0:513], in_=rwT.t[row0:row0 + rows, t0 - 1:t0 + 512]),
                         r=[rwT], w=[Lb], dma=True)
                else:
                    for k in range(2):
                        P.op(q, lambda e, k=k: e.dma_start(out=Lb.t[:, k, 0:513],
                                                            in_=rwT.t[row0 + k * 128:row0 + (k + 1) * 128, t0 - 1:t0 + 512]),
                             r=[rwT], w=[Lb], acc=(k > 0), dma=True)

        for tb in range(NB):
            t0 = tb * 512
            load_shift("sp", Lw, 96, 6144 - 3072, t0)
            load_shift("sp", La, 96, 6240 - 3072, t0)
            load_shift("sp", Lg, 128, 6336 - 3072, t0, view3=True)
            xs = shift_mix((Lw.t[0:96, 0:512], Lw.t[0:96, 1:513], Lw), 512, mul.t[0:96, 0:1], 96, None)
            P.op("act", lambda e, xs=xs: e.activation(out=tanw.t[0:96, :], in_=xs.t[0:96, :], func=AF.Tanh), r=[xs], w=[tanw])
            xs = shift_mix((La.t[0:96, 0:512], La.t[0:96, 1:513], La), 512, mul.t[0:96, 1:2], 96, None)
            P.op("act", lambda e, xs=xs: e.activation(out=xsal.t[0:96, :], in_=xs.t[0:96, :], func=AF.Copy), r=[xs], w=[xsal])
            for k in range(2):
                xs = shift_mix((Lg.t[:, k, 0:512], Lg.t[:, k, 1:513], Lg), 512, mul.t[:, 2 + k:3 + k], 128, None)
                P.op("act", lambda e, xs=xs, k=k: e.activation(out=sigg.t[:, k, :], in_=xs.t[:, :], func=AF.Sigmoid),
                     r=[xs], w=[sigg], acc=(k > 0))
            for cp in range(8):
                Ls = []
                for i, Rg in enumerate((Lr_r, Lk_r, Lv_r)):
                    Lb = Rg.get()
                    load_shift("sp", Lb, 128, i * 1024 + cp * 128, t0)
                    Ls.append(Lb)
                xs3 = []
                for i, nm in enumerate(("xr", "xk", "xv")):
                    Lb = Ls[i]
                    d = T32("d")
                    P.op("dve", lambda e, d=d, Lb=Lb: e.tensor_tensor(out=d.t[:, :], in0=Lb.t[:, 0:512], in1=Lb.t[:, 1:513],
                                                                      op=ALU.subtract), r=[Lb], w=[d])
                    x_ = T32(nm)
                    P.op("dve", lambda e, d=d, Lb=Lb, x_=x_, i=i, cp=cp: e.scalar_tensor_tensor(
                        out=x_.t[:, :], in0=d.t[:, :], scalar=rwp.t[:, cp, i:i + 1], in1=Lb.t[:, 1:513], op0=ALU.mult, op1=ALU.add),
                        r=[d, Lb, rwp], w=[x_])
                    xs3.append(x_)
                xr, xk, xv = xs3
                pb = psr7.get()
                P.op("pe", lambda e, pb=pb, cp=cp: e.matmul(pb.t[:, 0:512], lhsT=wupb.t[0:96, cp * 128:(cp + 1) * 128],
                                                             rhs=tanw.t[0:96, :], start=True, stop=True), r=[wupb, tanw], w=[pb])
                sigw = T32("sigw")
                P.op("act", lambda e, pb=pb, sigw=sigw, cp=cp: e.activation(out=sigw.t[:, :], in_=pb.t[:, :], func=AF.Sigmoid,
                                                                            bias=rwp.t[:, cp, 3:4]), r=[pb, rwp], w=[sigw])
                cum = T32("cum")
                P.op("dve", lambda e, cum=cum, sigw=sigw: e.tensor_tensor_scan(out=cum.t[:, :], data0=scanm.t[:, :], data1=sigw.t[:, :],
                                                                               initial=0.0, op0=ALU.mult, op1=ALU.add),
                     r=[scanm, sigw], w=[cum])
                cpm = T32("cpm")
                P.op("pool", lambda e, cpm=cpm, cum=cum, sigw=sigw: e.tensor_tensor(out=cpm.t[:, :], in0=cum.t[:, :], in1=sigw.t[:, :],
                                                                                   op=ALU.subtract), r=[cum, sigw], w=[cpm])
                epos = T32("epos")
                eneg = T32("eneg")
                eprev = T32("eprev")
                P.op("act", lambda e, epos=epos, cum=cum: e.activation(out=epos.t[:, :], in_=cum.t[:, :], func=AF.Exp, scale=-C0),
                     r=[cum], w=[epos])
                P.op("act", lambda e, eneg=eneg, cum=cum: e.activation(out=eneg.t[:, :], in_=cum.t[:, :], func=AF.Exp, scale=C0),
                     r=[cum], w=[eneg])
                P.op("act", lambda e, eprev=eprev, cpm=cpm: e.activation(out=eprev.t[:, :], in_=cpm.t[:, :], func=AF.Exp, scale=-C0),
                     r=[cpm], w=[eprev])
                pb = psr7.get()
                P.op("pe", lambda e, pb=pb, cp=cp: e.matmul(pb.t[:, 0:512], lhsT=aupb.t[0:96, cp * 128:(cp + 1) * 128],
                                                             rhs=xsal.t[0:96, :], start=True, stop=True), r=[aupb, xsal], w=[pb])
                alr = T32("alr")
                P.op("act", lambda e, pb=pb, alr=alr, cp=cp: e.activation(out=alr.t[:, :], in_=pb.t[:, :], func=AF.Sigmoid,
                                                                          bias=rwp.t[:, cp, 4:5]), r=[pb, rwp], w=[alr])
                pb = psr7.get()
                for k in range(2):
                    P.op("pe", lambda e, pb=pb, cp=cp, k=k: e.matmul(pb.t[:, 0:512], lhsT=gupb.t[:, k, cp * 128:(cp + 1) * 128],
                                                                      rhs=sigg.t[:, k, :], start=(k == 0), stop=(k == 1)),
                         r=[gupb, sigg], w=[pb], acc=(k > 0))
                P.op("act", lambda e, pb=pb, cp=cp: e.activation(out=GF.t[:, cp, :], in_=pb.t[:, :], func=AF.Copy),
                     r=[pb], w=[GF], acc=True)
                sq = sq_r.get()
                P.op("act", lambda e, sq=sq, xk=xk, cp=cp: e.activation(out=sq.t[:, :], in_=xk.t[:, :], func=AF.Square,
                                                                        scale=rwp.t[:, cp, 5:6]), r=[xk, rwp], w=[sq])
                pb = psr7.get()
                P.op("pe", lambda e, pb=pb, sq=sq: e.matmul(pb.t[:, 0:512], lhsT=bones.t[:, :], rhs=sq.t[:, :], start=True, stop=True),
                     r=[bones, sq], w=[pb])
                rn = T32("d")
                P.op("act", lambda e, pb=pb, rn=rn: e.activation(out=rn.t[:, :], in_=pb.t[:, :], func=AF.Sqrt), r=[pb], w=[rn])
                P.op("dve", lambda e, rn=rn: e.tensor_scalar(out=rn.t[:, :], in0=rn.t[:, :], scalar1=1e-12, scalar2=None, op0=ALU.max),
                     r=[rn], w=[rn])
                P.op("dve", lambda e, rn=rn: e.reciprocal(out=rn.t[:, :], in_=rn.t[:, :]), r=[rn], w=[rn])
                kk = T32("cpm")
                P.op("dve", lambda e, kk=kk, xk=xk, rn=rn, cp=cp: e.scalar_tensor_tensor(
                    out=kk.t[:, :], in0=xk.t[:, :], scalar=rwp.t[:, cp, 5:6], in1=rn.t[:, :], op0=ALU.mult, op1=ALU.mult),
                    r=[xk, rn, rwp], w=[kk])
                tq = T32("d")
                P.op("dve", lambda e, tq=tq, alr=alr, cp=cp: e.tensor_scalar(out=tq.t[:, :], in0=alr.t[:, :], scalar1=-1.0,
                                                                             scalar2=rwp.t[:, cp, 6:7], op0=ALU.add, op1=ALU.mult),
                     r=[alr, rwp], w=[tq])
                kp = T32("sigw")
                P.op("dve", lambda e, kp=kp, tq=tq, xk=xk: e.scalar_tensor_tensor(out=kp.t[:, :], in0=tq.t[:, :], scalar=1.0,
                                                                                 in1=xk.t[:, :], op0=ALU.add, op1=ALU.mult),
                     r=[tq, xk], w=[kp])
                bb = T32("xsl")
                P.op("pool", lambda e, bb=bb, kk=kk, alr=alr: e.tensor_tensor(out=bb.t[:, :], in0=kk.t[:, :], in1=alr.t[:, :], op=ALU.mult),
                     r=[kk, alr], w=[bb])
                P.op("dve", lambda e, kk=kk, eprev=eprev, cp=cp: e.scalar_tensor_tensor(
                    out=AR.t[:, cp, :, 0:64], in0=v3(kk.t[:, :]), scalar=-1.0, in1=v3(eprev.t[:, :]), op0=ALU.mult, op1=ALU.mult),
                    r=[kk, eprev], w=[AR], acc=True)
                P.op("dve", lambda e, xr=xr, epos=epos, cp=cp: e.tensor_tensor(
                    out=AR.t[:, cp, :, 64:128], in0=v3(xr.t[:, :]), in1=v3(epos.t[:, :]), op=ALU.mult), r=[xr, epos], w=[AR], acc=True)
                P.op("pool", lambda e, kp=kp, eneg=eneg, cp=cp: e.tensor_tensor(out=Kt.t[:, cp, :], in0=kp.t[:, :], in1=eneg.t[:, :],
                                                                              op=ALU.mult), r=[kp, eneg], w=[Kt], acc=True)
                P.op("pool", lambda e, bb=bb, eneg=eneg, cp=cp: e.tensor_tensor(out=Bt.t[:, cp, :], in0=bb.t[:, :], in1=eneg.t[:, :],
                                                                              op=ALU.mult), r=[bb, eneg], w=[Bt], acc=True)
                P.op("dve", lambda e, epos=epos, cp=cp: e.tensor_tensor(
                    out=v3(KH.t[:, cp, :]), in0=v3(Kt.t[:, cp, :]), in1=v3(epos.t[:, :])[:, :, 63:64].to_broadcast([128, 8, 64]),
                    op=ALU.mult), r=[Kt, epos], w=[KH], acc=True)
                P.op("dve", lambda e, epos=epos, cp=cp: e.tensor_tensor(
                    out=v3(BH.t[:, cp, :]), in0=v3(Bt.t[:, cp, :]), in1=v3(epos.t[:, :])[:, :, 63:64].to_broadcast([128, 8, 64]),
                    op=ALU.mult), r=[Bt, epos], w=[BH], acc=True)
                P.op("act", lambda e, xv=xv, cp=cp: e.activation(out=VF.t[:, cp, :], in_=xv.t[:, :], func=AF.Copy),
                     r=[xv], w=[VF], acc=True)
                P.op("dve", lambda e, epos=epos, cp=cp: e.tensor_copy(out=pCs.t[:, cp, :], in_=v3(epos.t[:, :])[:, :, 63]),
                     r=[epos], w=[pCs], acc=True)
                sq2 = sq_r.get()
                P.op("dve", lambda e, sq2=sq2, xr=xr, kp=kp, cp=cp: e.scalar_tensor_tensor(
                    out=sq2.t[:, :], in0=xr.t[:, :], scalar=rwp.t[:, cp, 7:8], in1=kp.t[:, :], op0=ALU.mult, op1=ALU.mult),
                    r=[xr, kp, rwp], w=[sq2])
                pb = psr7.get()
                P.op("pe", lambda e, pb=pb, sq2=sq2: e.matmul(pb.t[:, 0:512], lhsT=bones.t[:, :], rhs=sq2.t[:, :], start=True, stop=True),
                     r=[bones, sq2], w=[pb])
                P.op("dve", lambda e, pb=pb, xv=xv, cp=cp: e.tensor_tensor(out=BON.t[:, cp, :], in0=pb.t[:, :], in1=xv.t[:, :], op=ALU.mult),
                     r=[pb, xv], w=[BON], acc=True)

            for ch in range(8 if DEBUG.get('rw_stop', 'full') != 'B' else 0):
                cs = slice(ch * 64, (ch + 1) * 64)
                tm = {}
                for key, src in (("v", VF), ("k", KH), ("b", BH)):
                    pb = psr7.get()
                    for cp in range(8):
                        P.op("pe", lambda e, pb=pb, src=src, cp=cp, cs=cs: e.transpose(
                            out=bf(pb)[0:64, cp * 128:(cp + 1) * 128], in_=src.t[:, cp, cs], identity=identb.t[:, :]),
                            r=[src, identb], w=[pb], acc=(cp > 0))
                    tmb = tm_r[key].get()
                    P.op("act", lambda e, pb=pb, tmb=tmb: e.activation(out=tmb.t[:, :], in_=bf(pb)[0:64, :], func=AF.Copy),
                         r=[pb], w=[tmb])
                    tm[key] = tmb
                Ms = {}
                for hd in range(2):
                    hp = slice(hd * 64, (hd + 1) * 64)
                    specs = (("ak", Kt, 0, mSU), ("ab", Bt, 0, mSU), ("rk", Kt, 64, mIU), ("rb", Bt, 64, mIU))
                    for key, L, off, msk in specs:
                        pb = psr7.get()
                        for cp in range(8):
                            P.op("pe", lambda e, pb=pb, L=L, cp=cp, off=off, hp=hp, ch=ch, cs=cs: e.matmul(
                                pb.t[0:64, cp * 64:(cp + 1) * 64], lhsT=L.t[hp, cp, cs], rhs=AR.t[hp, cp, ch, off:off + 64],
                                start=(cp == 0), stop=True, skip_group_check=True), r=[L, AR], w=[pb], acc=(cp > 0))
                        mb = m_r[key].get()
                        P.op("dve", lambda e, pb=pb, mb=mb, msk=msk: e.tensor_tensor(out=mb.t[:, :], in0=pb.t[0:64, :], in1=msk.t[:, :],
                                                                                   op=ALU.mult), r=[pb, msk], w=[mb])
                        Ms[(key, hd)] = mb
                    pb = psr7.get()
                    for cp in range(8):
                        P.op("pe", lambda e, pb=pb, cp=cp, hp=hp, ch=ch, cs=cs: e.matmul(
                            pb.t[0:64, cp * 64:(cp + 1) * 64], lhsT=AR.t[hp, cp, ch, 0:64], rhs=Bt.t[hp, cp, cs],
                            start=(cp == 0), stop=True, skip_group_check=True), r=[Bt, AR], w=[pb], acc=(cp > 0))
                    mb = m_r["mt"].get()
                    P.op("dve", lambda e, pb=pb, mb=mb: e.tensor_tensor(out=mb.t[:, :], in0=pb.t[0:64, :], in1=mSL.t[:, :], op=ALU.mult),
                         r=[pb, mSL], w=[mb])
                    Ms[("mt", hd)] = mb
                if DEBUG.get('rw_stop') == 'C':
                    continue
                Tm = {}
                for hd in range(2):
                    X = Ms[("ab", hd)]
                    Y = Ms[("mt", hd)]
                    Tb = t_r[hd].get()
                    TTb = tt_r[hd].get()
                    P.op("pool", lambda e, Tb=Tb, X=X: e.tensor_tensor(out=Tb.t[:, :], in0=X.t[:, :], in1=I8.t[:, :], op=ALU.add),
                         r=[X, I8], w=[Tb])
                    P.op("pool", lambda e, TTb=TTb, Y=Y: e.tensor_tensor(out=TTb.t[:, :], in0=Y.t[:, :], in1=I8.t[:, :], op=ALU.add),
                         r=[Y, I8], w=[TTb])
                    for lvl in range(5):
                        last = (lvl == 4)
                        pX = psr7.get()
                        for cp in range(8):
                            c_ = slice(cp * 64, (cp + 1) * 64)
                            P.op("pe", lambda e, pX=pX, X=X, Y=Y, c_=c_, cp=cp: e.matmul(
                                pX.t[0:64, c_], lhsT=Y.t[:, c_], rhs=X.t[:, c_], start=(cp == 0), stop=True, skip_group_check=True),
                                r=[X, Y], w=[pX], acc=(cp > 0))
                        X2 = x_r[hd].get()
                        P.op("act", lambda e, pX=pX, X2=X2: e.activation(out=X2.t[:, :], in_=pX.t[0:64, :], func=AF.Copy), r=[pX], w=[X2])
                        if not last:
                            pY = psr7.get()
                            for cp in range(8):
                                c_ = slice(cp * 64, (cp + 1) * 64)
                                P.op("pe", lambda e, pY=pY, X=X, Y=Y, c_=c_, cp=cp: e.matmul(
                                    pY.t[0:64, c_], lhsT=X.t[:, c_], rhs=Y.t[:, c_], start=(cp == 0), stop=True, skip_group_check=True),
                                    r=[X, Y], w=[pY], acc=(cp > 0))
                            Y2 = y_r[hd].get()
                            P.op("act", lambda e, pY=pY, Y2=Y2: e.activation(out=Y2.t[:, :], in_=pY.t[0:64, :], func=AF.Copy),
                                 r=[pY], w=[Y2])
                        pT = psr7.get()
                        for cp in range(8):
                            c_ = slice(cp * 64, (cp + 1) * 64)
                            P.op("pe", lambda e, pT=pT, TTb=TTb, X2=X2, c_=c_, cp=cp: e.matmul(
                                pT.t[0:64, c_], lhsT=TTb.t[:, c_], rhs=X2.t[:, c_], start=(cp == 0), stop=True, skip_group_check=True),
                                r=[TTb, X2], w=[pT], acc=(cp > 0))
                        Tn = t_r[hd].get()
                        P.op("dve", lambda e, pT=pT, Tn=Tn, Tb=Tb: e.tensor_tensor(out=Tn.t[:, :], in0=pT.t[0:64, :], in1=Tb.t[:, :], op=ALU.add),
                             r=[pT, Tb], w=[Tn])
                        if not last:
                            pTT = psr7.get()
                            for cp in range(8):
                                c_ = slice(cp * 64, (cp + 1) * 64)
                                P.op("pe", lambda e, pTT=pTT, TTb=TTb, X2=X2, c_=c_, cp=cp: e.matmul(
                                    pTT.t[0:64, c_], lhsT=X2.t[:, c_], rhs=TTb.t[:, c_], start=(cp == 0), stop=True, skip_group_check=True),
                                    r=[TTb, X2], w=[pTT], acc=(cp > 0))
                            TTn = tt_r[hd].get()
                            P.op("dve", lambda e, pTT=pTT, TTn=TTn, TTb=TTb: e.tensor_tensor(out=TTn.t[:, :], in0=pTT.t[0:64, :],
                                                                                           in1=TTb.t[:, :], op=ALU.add),
                                 r=[pTT, TTb], w=[TTn])
                            TTb = TTn
                            Y = Y2
                        Tb = Tn
                        X = X2
                    Tm[hd] = Tb
                if DEBUG.get('rw_stop') == 'D2':
                    continue
                ytm = {}
                for hd in range(2):
                    hp = slice(hd * 64, (hd + 1) * 64)
                    hcol = lambda cp, hd=hd: slice((cp * 2 + hd) * 64, (cp * 2 + hd + 1) * 64)
                    p1 = psr7.get()
                    for cp in range(8):
                        P.op("pe", lambda e, p1=p1, cp=cp, hp=hp, ch=ch: e.matmul(
                            p1.t[0:64, cp * 64:(cp + 1) * 64], lhsT=AR.t[hp, cp, ch, 0:64], rhs=STb.t[hp, cp, :],
                            start=(cp == 0), stop=True, skip_group_check=True), r=[AR, STb], w=[p1], acc=(cp > 0))
                    w1s = w1s_r.get()
                    P.op("act", lambda e, p1=p1, w1s=w1s: e.activation(out=w1s.t[:, :], in_=p1.t[0:64, :], func=AF.Copy), r=[p1], w=[w1s])
                    p2 = psr7.get()
                    mak = Ms[("ak", hd)]
                    for cp in range(8):
                        P.op("pe", lambda e, p2=p2, cp=cp, mak=mak, hcol=hcol, tmv=tm["v"]: e.matmul(
                            p2.t[0:64, cp * 64:(cp + 1) * 64], lhsT=mak.t[:, cp * 64:(cp + 1) * 64], rhs=tmv.t[:, hcol(cp)],
                            start=(cp == 0), stop=True, skip_group_check=True), r=[mak, tm["v"]], w=[p2], acc=(cp > 0))
                    wtm = wtm_r.get()
                    P.op("dve", lambda e, p2=p2, w1s=w1s, wtm=wtm: e.tensor_tensor(out=wtm.t[:, :], in0=p2.t[0:64, :], in1=w1s.t[:, :], op=ALU.add),
                         r=[p2, w1s], w=[wtm])
                    if DEBUG.get('rw_stop') == 'D3a':
                        continue
                    p3 = psr7.get()
                    Tb = Tm[hd]
                    for cp in range(8):
                        c_ = slice(cp * 64, (cp + 1) * 64)
                        P.op("pe", lambda e, p3=p3, Tb=Tb, wtm=wtm, c_=c_, cp=cp: e.matmul(
                            p3.t[0:64, c_], lhsT=Tb.t[:, c_], rhs=wtm.t[:, c_], start=(cp == 0), stop=True, skip_group_check=True),
                            r=[Tb, wtm], w=[p3], acc=(cp > 0))
                    utm = utm_r.get()
                    P.op("act", lambda e, p3=p3, utm=utm: e.activation(out=utm.t[:, :], in_=p3.t[0:64, :], func=AF.Copy), r=[p3], w=[utm])
                    if DEBUG.get('rw_stop') == 'D3b':
                        continue
                    p4 = psr7.get()
                    for cp in range(8):
                        P.op("pe", lambda e, p4=p4, cp=cp, hp=hp, ch=ch: e.matmul(
                            p4.t[0:64, cp * 64:(cp + 1) * 64], lhsT=AR.t[hp, cp, ch, 64:128], rhs=STb.t[hp, cp, :],
                            start=(cp == 0), stop=True, skip_group_check=True), r=[AR, STb], w=[p4], acc=(cp > 0))
                    y1s = y1s_r.get()
                    P.op("act", lambda e, p4=p4, y1s=y1s: e.activation(out=y1s.t[:, :], in_=p4.t[0:64, :], func=AF.Copy), r=[p4], w=[y1s])
                    p5 = psr7.get()
                    mrb = Ms[("rb", hd)]
                    mrk = Ms[("rk", hd)]
                    for cp in range(8):
                        c_ = slice(cp * 64, (cp + 1) * 64)
                        P.op("pe", lambda e, p5=p5, mrb=mrb, utm=utm, c_=c_, cp=cp: e.matmul(
                            p5.t[0:64, c_], lhsT=mrb.t[:, c_], rhs=utm.t[:, c_], start=(cp == 0), stop=False, skip_group_check=True),
                            r=[mrb, utm], w=[p5], acc=(cp > 0))
                        P.op("pe", lambda e, p5=p5, mrk=mrk, c_=c_, cp=cp, hcol=hcol, tmv=tm["v"]: e.matmul(
                            p5.t[0:64, c_], lhsT=mrk.t[:, c_], rhs=tmv.t[:, hcol(cp)], start=False, stop=True, skip_group_check=True),
                            r=[mrk, tm["v"]], w=[p5], acc=True)
                    yt = ytm_r.get()
                    P.op("dve", lambda e, p5=p5, y1s=y1s, yt=yt: e.tensor_tensor(out=yt.t[:, :], in0=p5.t[0:64, :], in1=y1s.t[:, :], op=ALU.add),
                         r=[p5, y1s], w=[yt])
                    ytm[hd] = yt
                    if DEBUG.get('rw_stop') == 'D3c':
                        continue
                    for cp in range(8):
                        c_ = slice(cp * 64, (cp + 1) * 64)
                        P.op("pe", lambda e, cp=cp, c_=c_, hp=hp, hcol=hcol, utm=utm, hd=hd, tmb_=tm["b"]: e.matmul(
                            pS.t[hp, c_], lhsT=tmb_.t[:, hcol(cp)], rhs=utm.t[:, c_], start=(cp == 0), stop=False, skip_group_check=True),
                            r=[tm["b"], utm], w=[pS], acc=not (hd == 0 and cp == 0))
                        P.op("pe", lambda e, cp=cp, c_=c_, hp=hp, hcol=hcol, tmk=tm["k"], tmv=tm["v"]: e.matmul(
                            pS.t[hp, c_], lhsT=tmk.t[:, hcol(cp)], rhs=tmv.t[:, hcol(cp)], start=False, stop=True, skip_group_check=True),
                            r=[tm["k"], tm["v"]], w=[pS], acc=True)
                if DEBUG.get('rw_stop') in ('D3a', 'D3b', 'D3c', 'D3d'):
                    continue
                P.op("dve", lambda e, ch=ch: e.tensor_tensor(out=S32.t[:, :, :], in0=S32.t[:, :, :],
                                                             in1=pCs.t[:, :, ch:ch + 1].to_broadcast([128, 8, 64]), op=ALU.mult),
                     r=[S32, pCs], w=[S32])
                P.op("dve", lambda e: e.tensor_tensor(out=S32.t[:, :, :], in0=S32.t[:, :, :],
                                                      in1=pS.t[:, :].rearrange("p (c i) -> p c i", c=8), op=ALU.add),
                     r=[S32, pS], w=[S32])
                P.op("act", lambda e: e.activation(out=STb.t[:, :, :], in_=S32.t[:, :, :], func=AF.Copy), r=[S32], w=[STb])
                if DEBUG.get('rw_stop') == 'D3':
                    continue
                pO = psr7.get()
                YH = yh_r.get()
                for hd in range(2):
                    yt = ytm[hd]
                    st = st_r.get()
                    P.op("dve", lambda e, yt=yt, st=st: e.tensor_reduce(out=st.t[:, 0:8], in_=y3(yt.t[:, :]), axis=AX.X, op=ALU.add),
                         r=[yt], w=[st])
                    ysq = ysq_r.get()
                    P.op("act", lambda e, yt=yt, ysq=ysq: e.activation(out=ysq.t[:, :], in_=yt.t[:, :], func=AF.Square), r=[yt], w=[ysq])
                    P.op("dve", lambda e, ysq=ysq, st=st: e.tensor_reduce(out=st.t[:, 8:16], in_=y3(ysq.t[:, :]), axis=AX.X, op=ALU.add),
                         r=[ysq, st], w=[st])
                    P.op("dve", lambda e, st=st: e.tensor_scalar(out=st.t[:, 16:24], in0=st.t[:, 0:8], scalar1=1.0 / 64, scalar2=None,
                                                                 op0=ALU.mult), r=[st], w=[st])
                    P.op("dve", lambda e, st=st: e.tensor_tensor(out=st.t[:, 24:32], in0=st.t[:, 16:24], in1=st.t[:, 16:24], op=ALU.mult),
                         r=[st], w=[st])
                    P.op("dve", lambda e, st=st: e.scalar_tensor_tensor(out=st.t[:, 32:40], in0=st.t[:, 8:16], scalar=1.0 / 64,
                                                                        in1=st.t[:, 24:32], op0=ALU.mult, op1=ALU.subtract),
                         r=[st], w=[st])
                    P.op("act", lambda e, st=st: e.activation(out=st.t[:, 40:48], in_=st.t[:, 32:40], func=AF.Sqrt, bias=gn_eps.t[:, 0:1]),
                         r=[st, gn_eps], w=[st])
                    P.op("dve", lambda e, st=st: e.reciprocal(out=st.t[:, 32:40], in_=st.t[:, 40:48]), r=[st], w=[st])
                    yc = ysq_r.get()
                    P.op("dve", lambda e, yt=yt, st=st, yc=yc: e.tensor_tensor(
                        out=y3(yc.t[:, :]), in0=y3(yt.t[:, :]), in1=st.t[:, 16:24].unsqueeze(2).to_broadcast([64, 8, 64]), op=ALU.subtract),
                        r=[yt, st], w=[yc])
                    P.op("dve", lambda e, yc=yc, st=st, hd=hd, YH=YH: e.tensor_tensor(
                        out=YH.t[:, :].rearrange("p (c h i) -> p c h i", c=8, h=2)[:, :, hd, :], in0=y3(yc.t[:, :]),
                        in1=st.t[:, 32:40].unsqueeze(2).to_broadcast([64, 8, 64]), op=ALU.mult),
                        r=[yc, st], w=[YH], acc=(hd > 0))
                if DEBUG.get('rw_stop') != 'E1':
                    for cp in range(8):
                        P.op("pe", lambda e, YH=YH, cp=cp, pO=pO: e.transpose(
                            out=bf(pO)[:, cp * 64:(cp + 1) * 64], in_=YH.t[:, cp * 128:(cp + 1) * 128], identity=identb.t[0:64, 0:64]),
                            r=[YH, identb], w=[pO], acc=(cp > 0))
                if DEBUG.get('rw_stop') in ('E1', 'E2a'):
                    continue
                P.op("act", lambda e, cs=cs, pO=pO: e.activation(func=AF.Copy, out=YT.t[:, :, cs], in_=bf(pO)[:, 0:512].rearrange("p (c t) -> p c t", c=8)),
                     r=[pO], w=[YT], acc=True)
            for cp in range(8 if DEBUG.get('rw_stop', 'full') == 'full' else 0):
                ta = T32("d")
                P.op("dve", lambda e, ta=ta, cp=cp: e.tensor_scalar(out=ta.t[:, :], in0=YT.t[:, cp, :], scalar1=rwp.t[:, cp, 8:9],
                                                                    scalar2=rwp.t[:, cp, 9:10], op0=ALU.mult, op1=ALU.add),
                     r=[YT, rwp], w=[ta])
                tb_ = T32("xsl")
                P.op("dve", lambda e, ta=ta, tb_=tb_, cp=cp: e.tensor_tensor(out=tb_.t[:, :], in0=ta.t[:, :], in1=BON.t[:, cp, :], op=ALU.add),
                     r=[ta, BON], w=[tb_])
                ost = ost_r.get()
                P.op("dve", lambda e, tb_=tb_, ost=ost, cp=cp: e.tensor_tensor(out=ost.t[:, :], in0=tb_.t[:, :], in1=GF.t[:, cp, :], op=ALU.mult),
                     r=[tb_, GF], w=[ost])
                P.op("sp", lambda e, ost=ost, cp=cp, tb=tb: e.dma_start(
                    out=oT.t[tb, 1024 + cp * 128:1024 + (cp + 1) * 128, :], in_=ost.t[:, :]), r=[ost], w=[oT], acc=True, dma=True)

    P.push_scope()
    phase_rwkv()
    P.barrier()
    P.pop_scope()

    if stop_after <= 3:
        P.emit()
        return nc, P, dbg

    NB4_ = SO_ // 512

    def own_off(pid, tb):
        v = pid * (NB4_ * 4096)
        if tb:
            v = v + tb * 4096
        return v

    og = P.dram("og", [NBLK * 4096, 512], BF16)
    RG = [[2 * i, 2 * i + 1] for i in range(ncore // 2)]
    if not DEBUG.get("no_cc"):
        for k in range(NBLK):
            P.op("pool", lambda e, k=k: e.collective_compute("AllGather", ALU.bypass, replica_groups=RG, ins=[oT.t[k]], outs=[og.t[k * 4096:(k + 1) * 4096, :]]),
                 r=[oT], w=[og], acc=True, cc=True)
    ogs = P.dram("ogs", [NB4_ * 4096, 512], BF16)
    if not DEBUG.get("no_cc"):
        for tb in range(NB4_):
            P.op("pool", lambda e, tb=tb: e.dma_start(
                out=ogs.t[tb * 4096:(tb + 1) * 4096, :], in_=og.t[bass.ds(own_off(P.get_pid(e), tb), 4096), :]),
                r=[og], w=[ogs], acc=True, dma=True)
    P.barrier()

    wgb = P.dram("wgb", [D, 4096], BF16)
    wab = P.dram("wab", [D, D], BF16)
    wbb = P.dram("wbb", [D, D], BF16)
    wob = P.dram("wob", [D, D], BF16)
    wf1b = P.dram("wf1b", [D, 11264], BF16)
    wf2b = P.dram("wf2b", [5632, D], BF16)
    if not DEBUG.get("no_conv"):
        convert(wgb, wg_in, D)
    convert(wab, wa_in, D)
    convert(wbb, wb_in, D)
    convert(wob, wo_in, D)
    convert(wf1b, wf1_in, D)
    convert(wf2b, wf2_in, 5632)

    def phase4():
        NB4 = SO_ // 512 if DEBUG.get("p4_stop") != "xchg" else 0
        psr.i = 0
        pv = P.sb("pv_sb", [128, 80], F32)
        onesb = P.sb("onesb", [128, 128], BF16)
        P.op("sp", lambda e: e.dma_start(out=pv.t[:, :], in_=pv_in.t[:, :]), w=[pv], dma=True)
        P.op("pool", lambda e: e.memset(onesb.t[:, :], 1.0), w=[onesb])
        xT = P.sb("xT", [128, 16, 512], F32)
        wr = Ring(P, "w4", 2, [128, 8192], BF16)
        xt_r = Ring(P, "xt4", 2, [128, D], F32)
        sqb = P.sb("sqb", [128, 16, 512], BF16)
        N1 = P.sb("N1", [128, 16, 512], BF16)
        rs_r = Ring(P, "rs4", 2, [128, 512], F32)
        tmp_r = Ring(P, "tmp4", 2, [128, 512], F32)
        ss_r = Ring(P, "ss4", 4, [128, 2], F32)

        def wtile(src, c0, n, kcn=16):
            wt = wr.get()
            P.op("sp", lambda e, wt=wt: e.dma_start(
                out=wt.t[:, 0:kcn * n].rearrange("p (k c) -> p k c", k=kcn),
                in_=src.t[:, c0:c0 + n].rearrange("(k p) c -> p k c", p=128)), r=[src], w=[wt], dma=True)
            return wt

        def wv(wt, n, kcn=16):
            return wt.t[:, 0:kcn * n].rearrange("p (k c) -> p k c", k=kcn)

        def colsum_rstd(srcsq, eps_col):
            pb = psr.get()
            for kc in range(16):
                P.op("pe", lambda e, pb=pb, kc=kc: e.matmul(pb.t[:, 0:512], lhsT=onesb.t[:, :], rhs=srcsq.t[:, kc, :],
                                                            start=(kc == 0), stop=(kc == 15)), r=[onesb, srcsq], w=[pb], acc=(kc > 0))
            rs = rs_r.get()
            P.op("act", lambda e, pb=pb, rs=rs: e.activation(out=rs.t[:, :], in_=pb.t[:, :], func=AF.Sqrt, bias=epsb.t[:, 0:1],
                                                             scale=1.0 / D), r=[pb, epsb], w=[rs])
            P.op("dve", lambda e, rs=rs: e.reciprocal(out=rs.t[:, :], in_=rs.t[:, :]), r=[rs], w=[rs])
            return rs

        for tb in range(NB4):
            P.push_scope()
            hT = P.sb("hT4_%d" % tb, [128, 16, 512], BF16)
            G4_r = Ring(P, "G4_%d" % tb, 2, [128, 4, 512], BF16)
            oab = P.sb("oab_%d" % tb, [128, 16, 512], BF16)
            mixA = P.sb("mixA_%d" % tb, [128, 16, 512], BF16)
            mixb = P.sb("mixb_%d" % tb, [128, 16, 512], BF16)
            xn_r = Ring(P, "xn4_%d" % tb, 1, [128, D], BF16)
            for tt in range(4):
                t0 = tb * 512 + tt * 128
                xt = xt_r.get()
                P.op("sp", lambda e, xt=xt, t0=t0: e.dma_start(out=xt.t[:, :], in_=xo_in.t[t0:t0 + 128, :]), w=[xt], dma=True)
                ss = ss_r.get()
                P.op("pool", lambda e, ss=ss: e.memset(ss.t[:, :], 0.0), w=[ss])
                P.op("act", lambda e, xt=xt, ss=ss: e.activation(out=sqb.t[:, 0:4, :].rearrange("p a b -> p (a b)"), in_=xt.t[:, :],
                                                                 func=AF.Square, accum_out=ss.t[:, 0:1]), r=[xt], w=[sqb, ss])
                P.op("act", lambda e, ss=ss: e.activation(out=ss.t[:, 1:2], in_=ss.t[:, 0:1], func=AF.Sqrt,
                                                          bias=epsb.t[:, 0:1], scale=1.0 / D), r=[ss, epsb], w=[ss])
                P.op("dve", lambda e, ss=ss: e.reciprocal(out=ss.t[:, 0:1], in_=ss.t[:, 1:2]), r=[ss], w=[ss])
                xn = xn_r.get()
                P.op("dve", lambda e, xt=xt, ss=ss, xn=xn: e.scalar_tensor_tensor(
                    out=xn.t[:, :], in0=xt.t[:, :], scalar=ss.t[:, 0:1], in1=gpre_b.t[:, :], op0=ALU.mult, op1=ALU.mult),
                    r=[xt, ss, gpre_b], w=[xn])
                for half in range(2):
                    pb = psr.get()
                    for j in range(8):
                        kc = half * 8 + j
                        P.op("pe", lambda e, pb=pb, xn=xn, kc=kc, j=j: e.transpose(
                            out=bf(pb)[:, j * 128:(j + 1) * 128], in_=xn.t[:, kc * 128:(kc + 1) * 128], identity=identb.t[:, :]),
                            r=[xn, identb], w=[pb], acc=(j > 0))
                    P.op("act", lambda e, pb=pb, hT=hT, half=half, tt=tt: e.activation(
                        out=hT.t[:, half * 8:half * 8 + 8, tt * 128:(tt + 1) * 128],
                        in_=bf(pb).rearrange("p (j t) -> p j t", j=8), func=AF.Copy), r=[pb], w=[hT], acc=True)
                xh = xn_r.get()
                P.op("act", lambda e, xt=xt, xh=xh: e.activation(out=xh.t[:, :], in_=xt.t[:, :], func=AF.Copy), r=[xt], w=[xh])
                for half in range(2):
                    pb = psr.get()
                    for j in range(8):
                        kc = half * 8 + j
                        P.op("pe", lambda e, pb=pb, xh=xh, kc=kc, j=j: e.transpose(
                            out=bf(pb)[:, j * 128:(j + 1) * 128], in_=xh.t[:, kc * 128:(kc + 1) * 128], identity=identb.t[:, :]),
                            r=[xh, identb], w=[pb], acc=(j > 0))
                    P.op("act", lambda e, pb=pb, half=half, tt=tt: e.activation(
                        out=xT.t[:, half * 8:half * 8 + 8, tt * 128:(tt + 1) * 128],
                        in_=bf(pb).rearrange("p (j t) -> p j t", j=8), func=AF.Copy), r=[pb], w=[xT], acc=True)
            if DEBUG.get('p4_stop') == 'A1':
                P.barrier()
                P.pop_scope()
                continue
            for br, (wsrc, goff) in enumerate(((wab, 0), (wbb, 16))):
                for r_ in range(2):
                    row0 = tb * 4096 + r_ * 2048 + br * 1024
                    P.op("sp", lambda e, r_=r_, row0=row0, oab=oab: e.dma_start(
                        out=oab.t[:, r_ * 8:(r_ + 1) * 8, :],
                        in_=ogs.t[row0:row0 + 1024, :].rearrange("(k p) t -> p k t", p=128)),
                        r=[ogs], w=[oab], acc=(r_ > 0), dma=True)
                for q4 in range(4):
                    wgt = wtile(wgb, goff * 128 + q4 * 512, 512)
                    G4 = G4_r.get()
                    for c4 in range(4):
                        pb = psr.get()
                        for kc in range(16):
                            P.op("pe", lambda e, pb=pb, wgt=wgt, c4=c4, kc=kc, hT=hT: e.matmul(
                                pb.t[:, 0:512], lhsT=wv(wgt, 512)[:, kc, c4 * 128:(c4 + 1) * 128], rhs=hT.t[:, kc, :],
                                start=(kc == 0), stop=(kc == 15)), r=[wgt, hT], w=[pb], acc=(kc > 0))
                        gch = goff + q4 * 4 + c4
                        P.op("act", lambda e, pb=pb, G4=G4, c4=c4, gch=gch: e.activation(
                            out=G4.t[:, c4, :], in_=pb.t[:, :], func=AF.Sigmoid, bias=pv.t[:, gch:gch + 1]),
                            r=[pb, pv], w=[G4], acc=(c4 > 0))
                    wbt = wtile(wsrc, q4 * 512, 512)
                    for c4 in range(4):
                        cc = q4 * 4 + c4
                        pb = psr.get()
                        for kc in range(16):
                            P.op("pe", lambda e, pb=pb, wbt=wbt, c4=c4, kc=kc, oab=oab: e.matmul(
                                pb.t[:, 0:512], lhsT=wv(wbt, 512)[:, kc, c4 * 128:(c4 + 1) * 128], rhs=oab.t[:, kc, :],
                                start=(kc == 0), stop=(kc == 15)), r=[wbt, oab], w=[pb], acc=(kc > 0))
                        if br == 0:
                            P.op("dve", lambda e, pb=pb, G4=G4, c4=c4, cc=cc, mixA=mixA: e.tensor_tensor(
                                out=mixA.t[:, cc, :], in0=pb.t[:, :], in1=G4.t[:, c4, :], op=ALU.mult),
                                r=[pb, G4], w=[mixA], acc=True)
                        else:
                            tmp = tmp_r.get()
                            P.op("dve", lambda e, pb=pb, G4=G4, c4=c4, tmp=tmp: e.tensor_tensor(
                                out=tmp.t[:, :], in0=pb.t[:, :], in1=G4.t[:, c4, :], op=ALU.mult), r=[pb, G4], w=[tmp])
                            P.op("pool", lambda e, tmp=tmp, cc=cc, mixA=mixA, mixb=mixb: e.tensor_tensor(
                                out=mixb.t[:, cc, :], in0=tmp.t[:, :], in1=mixA.t[:, cc, :], op=ALU.add),
                                r=[tmp, mixA], w=[mixb], acc=True)
            if DEBUG.get('p4_stop') == 'A2':
                P.barrier()
                P.pop_scope()
                continue
            for q4 in range(4):
                wot = wtile(wob, q4 * 512, 512)
                for c4 in range(4):
                    cc = q4 * 4 + c4
                    pb = psr.get()
                    for kc in range(16):
                        P.op("pe", lambda e, pb=pb, wot=wot, c4=c4, kc=kc, mixb=mixb: e.matmul(
                            pb.t[:, 0:512], lhsT=wv(wot, 512)[:, kc, c4 * 128:(c4 + 1) * 128], rhs=mixb.t[:, kc, :],
                            start=(kc == 0), stop=(kc == 15)), r=[wot, mixb], w=[pb], acc=(kc > 0))
                    P.op("dve", lambda e, pb=pb, cc=cc, mixA=mixA: e.tensor_copy(out=mixA.t[:, cc, :], in_=pb.t[:, :]),
                         r=[pb], w=[mixA], acc=True)
                    P.op("act", lambda e, cc=cc, mixA=mixA: e.activation(out=sqb.t[:, cc, :], in_=mixA.t[:, cc, :], func=AF.Square),
                         r=[mixA], w=[sqb], acc=True)
            rs = colsum_rstd(sqb, 0)
            for cc in range(16):
                tmp = tmp_r.get()
                P.op("dve", lambda e, tmp=tmp, cc=cc, rs=rs, mixA=mixA: e.scalar_tensor_tensor(
                    out=tmp.t[:, :], in0=mixA.t[:, cc, :], scalar=pv.t[:, 32 + cc:33 + cc], in1=rs.t[:, :], op0=ALU.mult, op1=ALU.mult),
                    r=[mixA, pv, rs], w=[tmp])
                P.op("act", lambda e, tmp=tmp, cc=cc: e.activation(out=N1.t[:, cc, :], in_=tmp.t[:, :], func=AF.Copy), r=[tmp], w=[N1], acc=True)
                P.op("pool", lambda e, tmp=tmp, cc=cc: e.tensor_tensor(out=xT.t[:, cc, :], in0=tmp.t[:, :], in1=xT.t[:, cc, :], op=ALU.add),
                     r=[tmp, xT], w=[xT], acc=True)
            P.barrier()
            P.pop_scope()
            if DEBUG.get("p4_stop") == "A":
                continue
            P.push_scope()
            h2T = P.sb("h2T_%d" % tb, [128, 16, 512], BF16)
            FT = P.sb("FT_%d" % tb, [128, 44, 512], BF16)
            yo = P.sb("yo_%d" % tb, [128, 16, 512], BF16)
            for cc in range(16):
                P.op("act", lambda e, cc=cc: e.activation(out=sqb.t[:, cc, :], in_=xT.t[:, cc, :], func=AF.Square), r=[xT], w=[sqb], acc=True)
            rs = colsum_rstd(sqb, 0)
            for cc in range(16):
                P.op("dve", lambda e, cc=cc, rs=rs, h2T=h2T: e.scalar_tensor_tensor(
                    out=h2T.t[:, cc, :], in0=xT.t[:, cc, :], scalar=pv.t[:, 48 + cc:49 + cc], in1=rs.t[:, :], op0=ALU.mult, op1=ALU.mult),
                    r=[xT, pv, rs], w=[h2T], acc=True)
            for i in range(11):
                wgt = wtile(wf1b, i * 512, 512)
                wut = wtile(wf1b, 5632 + i * 512, 512)
                for c4 in range(4):
                    pg = psr.get()
                    for kc in range(16):
                        P.op("pe", lambda e, pg=pg, wgt=wgt, c4=c4, kc=kc, h2T=h2T: e.matmul(
                            pg.t[:, 0:512], lhsT=wv(wgt, 512)[:, kc, c4 * 128:(c4 + 1) * 128], rhs=h2T.t[:, kc, :],
                            start=(kc == 0), stop=(kc == 15)), r=[wgt, h2T], w=[pg], acc=(kc > 0))
                    pu = psr.get()
                    for kc in range(16):
                        P.op("pe", lambda e, pu=pu, wut=wut, c4=c4, kc=kc, h2T=h2T: e.matmul(
                            pu.t[:, 0:512], lhsT=wv(wut, 512)[:, kc, c4 * 128:(c4 + 1) * 128], rhs=h2T.t[:, kc, :],
                            start=(kc == 0), stop=(kc == 15)), r=[wut, h2T], w=[pu], acc=(kc > 0))
                    tmp = tmp_r.get()
                    P.op("act", lambda e, pg=pg, tmp=tmp: e.activation(out=tmp.t[:, :], in_=pg.t[:, :], func=AF.Silu), r=[pg], w=[tmp])
                    P.op("dve", lambda e, pu=pu, tmp=tmp, i=i, c4=c4, FT=FT: e.tensor_tensor(
                        out=FT.t[:, i * 4 + c4, :], in0=pu.t[:, :], in1=tmp.t[:, :], op=ALU.mult), r=[pu, tmp], w=[FT], acc=True)
            for cc in range(16):
                w2t = wtile(wf2b, cc * 128, 128, kcn=44)
                pb = psr.get()
                for kc in range(44):
                    P.op("pe", lambda e, pb=pb, w2t=w2t, kc=kc, FT=FT: e.matmul(
                        pb.t[:, 0:512], lhsT=wv(w2t, 128, 44)[:, kc, :], rhs=FT.t[:, kc, :],
                        start=(kc == 0), stop=(kc == 43)), r=[w2t, FT], w=[pb], acc=(kc > 0))
                P.op("dve", lambda e, pb=pb, cc=cc, yo=yo: e.tensor_copy(out=yo.t[:, cc, :], in_=pb.t[:, :]), r=[pb], w=[yo], acc=True)
                P.op("act", lambda e, cc=cc, yo=yo: e.activation(out=sqb.t[:, cc, :], in_=yo.t[:, cc, :], func=AF.Square),
                     r=[yo], w=[sqb], acc=True)
            rs = colsum_rstd(sqb, 0)
            for cc in range(16):
                tmp = tmp_r.get()
                P.op("dve", lambda e, tmp=tmp, cc=cc, rs=rs, yo=yo: e.scalar_tensor_tensor(
                    out=tmp.t[:, :], in0=yo.t[:, cc, :], scalar=pv.t[:, 64 + cc:65 + cc], in1=rs.t[:, :], op0=ALU.mult, op1=ALU.mult),
                    r=[yo, pv, rs], w=[tmp])
                P.op("pool", lambda e, tmp=tmp, cc=cc, yo=yo: e.tensor_tensor(out=yo.t[:, cc, :], in0=tmp.t[:, :], in1=N1.t[:, cc, :], op=ALU.add),
                     r=[tmp, N1], w=[yo], acc=True)
            for tt in range(4):
                r0 = tb * 512 + tt * 128
                xt = xt_r.get()
                P.op("sp", lambda e, xt=xt, r0=r0: e.dma_start(out=xt.t[:, :], in_=xo_in.t[r0:r0 + 128, :]), w=[xt], dma=True)
                for half in range(2):
                    pb = psr.get()
                    for j in range(8):
                        kc = half * 8 + j
                        P.op("pe", lambda e, pb=pb, kc=kc, j=j, tt=tt, yo=yo: e.transpose(
                            out=bf(pb)[:, j * 128:(j + 1) * 128], in_=yo.t[:, kc, tt * 128:(tt + 1) * 128], identity=identb.t[:, :]),
                            r=[yo, identb], w=[pb], acc=(j > 0))
                    P.op("dve", lambda e, pb=pb, xt=xt, half=half: e.tensor_tensor(
                        out=xt.t[:, half * 1024:(half + 1) * 1024], in0=bf(pb)[:, :], in1=xt.t[:, half * 1024:(half + 1) * 1024], op=ALU.add),
                        r=[pb, xt], w=[xt], acc=True)
                P.op("sp", lambda e, xt=xt, r0=r0: e.dma_start(out=out_t.t[r0:r0 + 128, :], in_=xt.t[:, :]), r=[xt], w=[out_t], acc=True, dma=True)
            P.barrier()
            P.pop_scope()

    P.push_scope()
    phase4()
    P.barrier()
    P.pop_scope()

    if stop_after <= 4:
        P.emit()
        return nc, P, dbg

    P.emit()
    return nc, P, dbg


def host_inputs(inputs, nsb=8, ncore=8):
    x = np.asarray(inputs["x"], np.float32)
    w_in = np.asarray(inputs["w_in"], np.float32)[0]
    half = 64
    inv = (10000.0 ** (-np.arange(half, dtype=np.float32) / half)).astype(np.float32)
    ang = np.arange(S, dtype=np.float32)[None, :] * inv[:, None]
    cosT = np.concatenate([np.cos(ang), np.cos(ang)], 0).astype(np.float32)
    sinT = np.concatenate([np.sin(ang), np.sin(ang)], 0).astype(np.float32)
    rmat = np.zeros((128, 128), np.float32)
    for dp in range(64):
        rmat[dp + 64, dp] = -1.0
        rmat[dp, dp + 64] = 1.0
    ident = np.eye(128, dtype=np.float32)
    tri = np.arange(64)
    mSU = np.tile((tri[:, None] < tri[None, :]).astype(np.float32), (1, 8))
    mIU = np.tile((tri[:, None] <= tri[None, :]).astype(np.float32), (1, 8))
    mSL = np.tile((tri[:, None] > tri[None, :]).astype(np.float32), (1, 8))
    I8c = np.tile(np.eye(64, dtype=np.float32), (1, 8))
    scanm = np.ones((128, 512), np.float32)
    scanm[:, 0::64] = 0.0
    bones = np.zeros((128, 128), np.float32)
    bones[0:64, 0:64] = 1.0
    bones[64:128, 64:128] = 1.0
    maps = []
    S_att = nsb * 1024
    SO_ = S_att // 2
    g1 = lambda k: np.asarray(inputs[k], np.float32)[0]
    pvec = np.concatenate([g1("b_gate").reshape(32, 128).T, g1("g_mix_post").reshape(16, 128).T,
                           g1("g_ffn_pre").reshape(16, 128).T, g1("g_ffn_post").reshape(16, 128).T], 1)
    pvec = np.ascontiguousarray(pvec, np.float32)
    wg = np.ascontiguousarray(w_in[:, 12736:])
    for c in range(ncore):
        b, hh = c // 2, c % 2
        qs = slice(hh * 1024, (hh + 1) * 1024)
        cols = np.concatenate([
            np.arange(0, 2048)[qs], np.arange(2048, 4096)[qs], np.arange(4096, 6144)[qs],
            6144 + np.arange(0, 2048)[qs], 6144 + 2048 + np.arange(0, 2048)[qs], 6144 + 4096 + np.arange(0, 2048)[qs],
            6144 + 6144 + np.arange(0, 448)])
        m = {
            "x": np.ascontiguousarray(x[b, 0:S_att]),
            "xo": np.ascontiguousarray(x[b, hh * SO_:(hh + 1) * SO_]),
            "wg": wg, "wa": g1("w_branch_a"), "wb": g1("w_branch_b"), "wo": g1("w_out"),
            "wf1": g1("w_ffn_in"), "wf2": g1("w_ffn_out"), "pvec": pvec,
            "w1": np.ascontiguousarray(w_in[:, cols]),
            "gpre": np.ascontiguousarray(np.asarray(inputs["g_mix_pre"], np.float32)[0][None, :]),
            "cosT": cosT, "sinT": sinT, "rmat": rmat, "ident": ident,
            "lqk": np.ascontiguousarray(np.concatenate([np.asarray(inputs["da_lambda_q"], np.float32)[0].reshape(-1),
                                                        np.asarray(inputs["da_lambda_k"], np.float32)[0].reshape(-1)])[None, :]),
            "gsub": np.ascontiguousarray(np.asarray(inputs["da_subln_g"], np.float32)[0][None, :]),
            "rwp": rw_params(inputs, hh), "mul": rw_mul(inputs),
            "wup": np.ascontiguousarray(np.asarray(inputs["rw_w_up"], np.float32)[0][:, qs]),
            "aup": np.ascontiguousarray(np.asarray(inputs["rw_a_up"], np.float32)[0][:, qs]),
            "gup": np.ascontiguousarray(np.asarray(inputs["rw_g_up"], np.float32)[0][:, qs]),
            "mSU": mSU, "mIU": mIU, "mSL": mSL, "I8": I8c, "scanm": scanm, "bones": bones,
        }
        maps.append(m)
    return maps


def rw_params(inputs, hh):
    g = lambda k: np.asarray(inputs[k], np.float32)[0].reshape(-1)
    mu = g("rw_mu")
    cols = hh * 1024 + np.arange(1024)
    vecs = [mu[cols], mu[2048 + cols], mu[4096 + cols], g("rw_w0")[cols], g("rw_a0")[cols], g("rw_k_k")[cols],
            g("rw_k_a")[cols], g("rw_r_k")[cols], g("rw_ln_w")[cols], g("rw_ln_b")[cols]]
    a = np.stack(vecs, -1).reshape(8, 128, 10).transpose(1, 0, 2)
    return np.ascontiguousarray(a)


def rw_mul(inputs):
    mu = np.asarray(inputs["rw_mu"], np.float32)[0]
    a = np.zeros((128, 4), np.float32)
    a[0:96, 0] = mu[6144:6240]
    a[0:96, 1] = mu[6240:6336]
    a[:, 2] = mu[6336:6464]
    a[:, 3] = mu[6464:6592]
    return a


def kernel(**inputs):
    nc, P, dbg = build(nsb=8, ncore=8)
    maps = host_inputs(inputs, nsb=8, ncore=8)
    res = run_bass_kernel_spmd(nc, maps, core_ids=list(range(8)))
    out = np.zeros((4, S, D), np.float32)
    for c in range(8):
        b, hh = c // 2, c % 2
        out[b, hh * SO:(hh + 1) * SO] = res.results[c]["out"]
    return out
```

```python
import math
from contextlib import ExitStack
import numpy as np
import ml_dtypes
import concourse.bass as bass
import concourse.mybir as mybir
from concourse.bass_utils import run_bass_kernel_spmd

F32 = mybir.dt.float32
BF16 = mybir.dt.bfloat16
AF = mybir.ActivationFunctionType
ALU = mybir.AluOpType
AX = mybir.AxisListType

S = 8192
D = 2048
SO = 4096
NS_DMA = 92
NS_SW = 76
ENGS = ("pe", "act", "dve", "pool", "sp")
SAME_ENG_SYNC = True
DEBUG = {}


class Buf:
    __slots__ = ("t", "w", "r")

    def __init__(self, t):
        self.t = t
        self.w = {}
        self.r = {}


class Prog:
    def __init__(self, nc):
        self.nc = nc
        self.ops = {e: [] for e in ENGS}
        self.known = {e: {} for e in ENGS}
        self.ndma = 0
        self.ndma_sw = 0
        self.dma_seq = [0] * NS_DMA
        self.stack = ExitStack()
        self.ncc = 0
        self.scopes = []

    def sb(self, name, shape, dt):
        st = self.scopes[-1] if self.scopes else self.stack
        return Buf(st.enter_context(self.nc.sbuf_tensor(name, list(shape), dt)))

    def push_scope(self):
        self.scopes.append(ExitStack())

    def pop_scope(self):
        self.scopes.pop().close()

    def ps(self, name):
        return Buf(self.stack.enter_context(self.nc.psum_tensor(name, [128, 512], F32)))

    def dram(self, name, shape, dt):
        return Buf(self.nc.dram_tensor(name, list(shape), dt))

    def get_pid(self, e):
        if getattr(self, "_pid", None) is None:
            self._pid = e.partition_id() % 2
        return self._pid

    def op(self, eng, fn, r=(), w=(), acc=False, dma=False, cc=False):
        deps = {}

        def add(d):
            for k, i in d.items():
                if deps.get(k, 0) < i:
                    deps[k] = i

        for b in r:
            add(b.w)
        for b in w:
            add(b.w)
            add(b.r)
        ops = self.ops[eng]
        my_idx = len(ops) + 1
        if dma:
            if eng == "pool":
                slot = self.ndma_sw % NS_SW
                self.ndma_sw += 1
            else:
                slot = NS_SW + self.ndma % (NS_DMA - NS_SW)
                self.ndma += 1
            seq = self.dma_seq[slot] + 1
            self.dma_seq[slot] = seq
            if seq > 1:
                add({("d", slot): seq - 1})
            ev = (("d", slot), seq)
        elif cc:
            self.ncc += 1
            ev = (("c", 0), self.ncc)
        else:
            ev = (eng, my_idx)
        kn = self.known[eng]
        waits = []
        for k, i in deps.items():
            if k == eng and (eng == "pe" or not SAME_ENG_SYNC):
                continue
            if kn.get(k, 0) >= i:
                continue
            kn[k] = i
            waits.append((k, i))
            if not isinstance(k, tuple):
                self.ops[k][i - 1][2] = True
        ops.append([waits, fn, False, ev if (dma or cc) else None, self._snap(fn)])
        for b in r:
            if b.r.get(ev[0], 0) < ev[1]:
                b.r[ev[0]] = ev[1]
        for b in w:
            if acc:
                b.w[ev[0]] = ev[1]
            else:
                b.w = {ev[0]: ev[1]}
            b.r = {}
        return ev

    @staticmethod
    def _snap(fn):
        out = []
        for c in (fn.__closure__ or ()):
            try:
                out.append(id(c.cell_contents))
            except ValueError:
                out.append(None)
        return out

    def barrier(self):
        latest = {}
        for e in ENGS:
            n = len(self.ops[e])
            while n > 0 and (self.ops[e][n - 1][1] is None or self.ops[e][n - 1][3] is not None):
                n -= 1
            if n > 0:
                latest[e] = n
        for s in range(NS_DMA):
            if self.dma_seq[s] > 0:
                latest[("d", s)] = self.dma_seq[s]
        if self.ncc:
            latest[("c", 0)] = self.ncc
        for e in ENGS:
            kn = self.known[e]
            waits = []
            for k, i in latest.items():
                if k == e:
                    continue
                if kn.get(k, 0) >= i:
                    continue
                kn[k] = i
                waits.append((k, i))
                if not isinstance(k, tuple):
                    self.ops[k][i - 1][2] = True
            self.ops[e].append([waits, None, False, None, None])

    def emit(self):
        nc = self.nc
        st = self.stack
        self.csem = st.enter_context(nc.semaphore("ccs"))
        self.sem = {e: st.enter_context(nc.semaphore("s_" + e)) for e in ENGS}
        self.dsem = [st.enter_context(nc.semaphore("d_%d" % i)) for i in range(NS_DMA)]
        self.val = {}
        for e in ENGS:
            c = 0
            v = []
            for o in self.ops[e]:
                if o[2]:
                    c += 1
                v.append(c)
            self.val[e] = v
        block = st.enter_context(nc.Block())

        @block.tensor
        def _(e):
            self._emit("pe", e)

        @block.scalar
        def _(e):
            self._emit("act", e)

        @block.vector
        def _(e):
            self._emit("dve", e)

        @block.gpsimd
        def _(e):
            self._emit("pool", e)

        @block.sync
        def _(e):
            self._emit("sp", e)

    def _emit(self, name, e):
        for waits, fn, flagged, ev, snap in self.ops[name]:
            if fn is not None and snap != self._snap(fn):
                names = fn.__code__.co_freevars
                bad = [n for n, a, b in zip(names, snap, self._snap(fn)) if a != b]
                raise RuntimeError("late-bound closure variable(s) %s in op at line %d" % (bad, fn.__code__.co_firstlineno))
            for k, i in waits:
                if isinstance(k, tuple):
                    if k[0] == "d":
                        e.wait_ge(self.dsem[k[1]], 16 * i)
                    else:
                        e.wait_ge(self.csem, i)
                else:
                    e.wait_ge(self.sem[k], self.val[k][i - 1])
            if fn is None:
                continue
            ins = fn(e)
            if ev is not None:
                if ev[0][0] == "d":
                    ins.then_inc(self.dsem[ev[0][1]], 16)
                else:
                    ins.then_inc(self.csem)
            elif flagged:
                ins.then_inc(self.sem[name], 1)


class PsumRing:
    def __init__(self, P, n=8, bufs=None):
        self.b = bufs if bufs is not None else [P.ps("psb%d" % i) for i in range(n)]
        self.i = 0

    def get(self):
        b = self.b[self.i % len(self.b)]
        self.i += 1
        return b


class Ring:
    def __init__(self, P, name, n, shape, dt):
        self.b = [P.sb("%s%d" % (name, i), shape, dt) for i in range(n)]
        self.i = 0

    def get(self):
        b = self.b[self.i % len(self.b)]
        self.i += 1
        return b


def bf(psbuf):
    return psbuf.t[:, :].bitcast(BF16)


def build(stop_after=99, nsb=8, ncore=8):
    nc = bass.Bass("TRN2", target_bir_lowering=False)
    P = Prog(nc)

    def din(name, shape, dt=F32):
        return Buf(nc.dram_tensor(name, list(shape), dt, kind="ExternalInput"))

    S_att = nsb * 1024
    SO_ = S_att // 2
    x_in = din("x", [S_att, D])
    xo_in = din("xo", [SO_, D])
    wg_in = din("wg", [D, 4096])
    wa_in = din("wa", [D, D])
    wb_in = din("wb", [D, D])
    wo_in = din("wo", [D, D])
    wf1_in = din("wf1", [D, 11264])
    wf2_in = din("wf2", [5632, D])
    pv_in = din("pvec", [128, 80])
    w1_in = din("w1", [D, 6592])
    gpre_in = din("gpre", [1, D])
    cos_in = din("cosT", [128, S])
    sin_in = din("sinT", [128, S])
    rmat_in = din("rmat", [128, 128])
    ident_in = din("ident", [128, 128])
    lqk_in = din("lqk", [1, 512])
    rwp_in = din("rwp", [128, 8, 10])
    mul_in = din("mul", [128, 4])
    wup_in = din("wup", [96, 1024])
    aup_in = din("aup", [96, 1024])
    gup_in = din("gup", [256, 1024])
    mSU_in = din("mSU", [64, 512])
    mIU_in = din("mIU", [64, 512])
    mSL_in = din("mSL", [64, 512])
    I8_in = din("I8", [64, 512])
    scanm_in = din("scanm", [128, 512])
    bones_in = din("bones", [128, 128])
    gsub_in = din("gsub", [1, 256])
    out_t = Buf(nc.dram_tensor("out", [SO_, D], F32, kind="ExternalOutput"))
    dbg = {}

    def dout(name, shape, dt):
        dbg[name] = Buf(nc.dram_tensor(name, list(shape), dt, kind="ExternalOutput"))
        return dbg[name]

    w1b = P.dram("w1b", [D, 6592], BF16)
    if DEBUG.get("p1"):
        qkT = dout("qkT", [16, 128, nsb * 1024], BF16)
        rwT = dout("rwT", [3520, nsb * 1024], BF16)
        vtm = dout("vtm", [nsb * 1024, 1024], BF16)
        hTd = dout("hTd", [128, 16, 512], BF16)
    else:
        qkT = P.dram("qkT", [16, 128, S], BF16)
        rwT = P.dram("rwT", [3520, S], BF16)
        vtm = P.dram("vtm", [S, 1024], BF16)

    identb = P.sb("identb", [128, 128], BF16)
    identf = P.sb("identf", [128, 128], F32)
    rmatb = P.sb("rmatb", [128, 128], BF16)
    gpre_b = P.sb("gpre_b", [128, D], F32)
    P.op("pool", lambda e: e.dma_start(out=identb.t[:, :], in_=ident_in.t[:, :]), w=[identb], dma=True)
    P.op("sp", lambda e: e.dma_start(out=identf.t[:, :], in_=ident_in.t[:, :]), w=[identf], dma=True)
    P.op("pool", lambda e: e.dma_start(out=rmatb.t[:, :], in_=rmat_in.t[:, :]), w=[rmatb], dma=True)
    P.op("sp", lambda e: e.dma_start(out=gpre_b.t[:, :], in_=gpre_in.t[0:1, :].partition_broadcast(128)),
         w=[gpre_b], dma=True)

    def convert(dst, src, rows, rstep=256):
        for r0 in range(0, rows, rstep):
            P.op("pool", lambda e, r0=r0: e.dma_start(out=dst.t[r0:r0 + rstep, :], in_=src.t[r0:r0 + rstep, :]),
                 r=[src], w=[dst], acc=True, dma=True)

    convert(w1b, w1_in, D)

    psr = PsumRing(P, 8)
    epsb = P.sb("epsb", [128, 4], F32)
    P.op("pool", lambda e: e.memset(epsb.t[:, 0:1], 1e-6), w=[epsb])
    P.op("pool", lambda e: e.memset(epsb.t[:, 1:2], 1e-5), w=[epsb], acc=True)

    def phase1():
        st2 = ExitStack()
        hg = Ring(P, "hg", 4, [128, 16, 512], BF16)
        xt_r = Ring(P, "xt", 3, [128, D], F32)
        xn_r = Ring(P, "xn", 2, [128, D], BF16)
        junk = P.sb("junk", [128, D], BF16)
        ss_r = Ring(P, "ss", 4, [128, 2], F32)
        wt_r = Ring(P, "wt", 3, [128, 16, 512], BF16)
        qb_r = Ring(P, "qb", 3, [128, 512], BF16)
        t1_r = Ring(P, "t1", 3, [128, 512], F32)
        t2_r = Ring(P, "t2", 3, [128, 512], F32)
        ob_r = Ring(P, "ob", 4, [128, 512], BF16)
        cs_r = Ring(P, "cs", 2, [128, 512], F32)
        sn_r = Ring(P, "sn", 2, [128, 512], F32)

        def build_group(G, hbuf):
            for tt in range(4):
                t0 = G * 512 + tt * 128
                xt = xt_r.get()
                P.op("sp", lambda e, xt=xt, t0=t0: e.dma_start(out=xt.t[:, :], in_=x_in.t[t0:t0 + 128, :]),
                     w=[xt], dma=True)
                ss = ss_r.get()
                P.op("pool", lambda e, ss=ss: e.memset(ss.t[:, :], 0.0), w=[ss])
                P.op("act", lambda e, xt=xt, ss=ss: e.activation(out=junk.t[:, :], in_=xt.t[:, :], func=AF.Square,
                                                                 accum_out=ss.t[:, 0:1]),
                     r=[xt], w=[junk, ss])
                P.op("act", lambda e, ss=ss: e.activation(out=ss.t[:, 1:2], in_=ss.t[:, 0:1], func=AF.Sqrt,
                                                          bias=epsb.t[:, 0:1], scale=1.0 / D), r=[ss, epsb], w=[ss])
                P.op("dve", lambda e, ss=ss: e.reciprocal(out=ss.t[:, 0:1], in_=ss.t[:, 1:2]), r=[ss], w=[ss])
                xn = xn_r.get()
                P.op("dve", lambda e, xt=xt, ss=ss, xn=xn: e.scalar_tensor_tensor(
                    out=xn.t[:, :], in0=xt.t[:, :], scalar=ss.t[:, 0:1], in1=gpre_b.t[:, :],
                    op0=ALU.mult, op1=ALU.mult), r=[xt, ss, gpre_b], w=[xn])
                for half in range(2):
                    pb = psr.get()
                    for j in range(8):
                        kc = half * 8 + j
                        P.op("pe", lambda e, pb=pb, xn=xn, kc=kc, j=j: e.transpose(
                            out=bf(pb)[:, j * 128:(j + 1) * 128], in_=xn.t[:, kc * 128:(kc + 1) * 128],
                            identity=identb.t[:, :]), r=[xn, identb], w=[pb], acc=(j > 0))
                    P.op("act", lambda e, pb=pb, hbuf=hbuf, half=half, tt=tt: e.activation(
                        out=hbuf.t[:, half * 8:half * 8 + 8, tt * 128:(tt + 1) * 128],
                        in_=bf(pb).rearrange("p (j t) -> p j t", j=8), func=AF.Copy),
                        r=[pb], w=[hbuf], acc=True)

        blocks = []
        for c0 in range(0, 6592, 512):
            n = min(512, 6592 - c0)
            blocks.append((c0, n))

        for sbk in range(nsb):
            hb = [hg.get(), hg.get()]
            for g in range(2):
                build_group(sbk * 2 + g, hb[g])
            if DEBUG.get("p1") and sbk == 0:
                P.op("sp", lambda e, hb0=hb[0]: e.dma_start(out=dbg["hTd"].t[:, :, :], in_=hb0.t[:, :, :]), r=[hb[0]], w=[dbg["hTd"]], dma=True)
            for (c0, n) in blocks:
                wt = wt_r.get()
                P.op("sp", lambda e, wt=wt, c0=c0, n=n: e.dma_start(
                    out=wt.t[:, :, 0:n], in_=w1b.t[:, c0:c0 + n].rearrange("(kc p) c -> p kc c", p=128)),
                    r=[w1b], w=[wt], dma=True)
                if 2048 <= c0 < 3072:
                    for g in range(2):
                        for tt in range(4):
                            t0 = (sbk * 2 + g) * 512 + tt * 128
                            pb = psr.get()
                            for kc in range(16):
                                P.op("pe", lambda e, pb=pb, hbg=hb[g], tt=tt, kc=kc, wt=wt: e.matmul(
                                    pb.t[:, 0:512], lhsT=hbg.t[:, kc, tt * 128:(tt + 1) * 128], rhs=wt.t[:, kc, 0:512],
                                    start=(kc == 0), stop=(kc == 15)), r=[hb[g], wt], w=[pb], acc=(kc > 0))
                            ob = ob_r.get()
                            P.op("act", lambda e, pb=pb, ob=ob: e.activation(out=ob.t[:, :], in_=pb.t[:, :], func=AF.Copy),
                                 r=[pb], w=[ob])
                            P.op("pool", lambda e, ob=ob, t0=t0, c0=c0: e.dma_start(
                                out=vtm.t[t0:t0 + 128, c0 - 2048:c0 - 2048 + 512], in_=ob.t[:, :]),
                                r=[ob], w=[vtm], acc=True, dma=True)
                    continue
                for g in range(2):
                    tk0 = (sbk * 2 + g) * 512
                    is_qk = c0 < 2048
                    if is_qk:
                        cs = cs_r.get()
                        sn = sn_r.get()
                        P.op("sp", lambda e, cs=cs, tk0=tk0: e.dma_start(out=cs.t[:, :], in_=cos_in.t[:, tk0:tk0 + 512]),
                             w=[cs], dma=True)
                        P.op("sp", lambda e, sn=sn, tk0=tk0: e.dma_start(out=sn.t[:, :], in_=sin_in.t[:, tk0:tk0 + 512]),
                             w=[sn], dma=True)
                    for cc in range((n + 127) // 128):
                        m = min(128, n - cc * 128)
                        col = c0 + cc * 128
                        pb = psr.get()
                        for kc in range(16):
                            P.op("pe", lambda e, pb=pb, hbg=hb[g], cc=cc, kc=kc, wt=wt, m=m: e.matmul(
                                pb.t[0:m, 0:512], lhsT=wt.t[:, kc, cc * 128:cc * 128 + m], rhs=hbg.t[:, kc, :],
                                start=(kc == 0), stop=(kc == 15)), r=[hb[g], wt], w=[pb], acc=(kc > 0))
                        if is_qk:
                            qb = qb_r.get()
                            P.op("act", lambda e, pb=pb, qb=qb: e.activation(out=qb.t[:, :], in_=pb.t[:, :], func=AF.Copy),
                                 r=[pb], w=[qb])
                            pr = psr.get()
                            P.op("pe", lambda e, pr=pr, qb=qb: e.matmul(pr.t[:, 0:512], lhsT=rmatb.t[:, :], rhs=qb.t[:, :],
                                                                         start=True, stop=True), r=[qb, rmatb], w=[pr])
                            t1 = t1_r.get()
                            t2 = t2_r.get()
                            P.op("pool", lambda e, t1=t1, qb=qb, cs=cs: e.tensor_tensor(
                                out=t1.t[:, :], in0=qb.t[:, :], in1=cs.t[:, :], op=ALU.mult), r=[qb, cs], w=[t1])
                            P.op("dve", lambda e, t2=t2, pr=pr, sn=sn: e.tensor_tensor(
                                out=t2.t[:, :], in0=pr.t[:, :], in1=sn.t[:, :], op=ALU.mult), r=[pr, sn], w=[t2])
                            ob = ob_r.get()
                            P.op("dve", lambda e, t1=t1, t2=t2, ob=ob: e.tensor_tensor(
                                out=ob.t[:, :], in0=t1.t[:, :], in1=t2.t[:, :], op=ALU.add), r=[t1, t2], w=[ob])
                            ch = col // 128
                            P.op("pool", lambda e, ob=ob, ch=ch, tk0=tk0: e.dma_start(
                                out=qkT.t[ch, :, tk0:tk0 + 512], in_=ob.t[:, :]), r=[ob], w=[qkT], acc=True, dma=True)
                        else:
                            ob = ob_r.get()
                            P.op("act", lambda e, pb=pb, ob=ob, m=m: e.activation(out=ob.t[0:m, :], in_=pb.t[0:m, :], func=AF.Copy),
                                 r=[pb], w=[ob])
                            row = col - 3072
                            P.op("pool", lambda e, ob=ob, row=row, m=m, tk0=tk0: e.dma_start(
                                out=rwT.t[row:row + m, tk0:tk0 + 512], in_=ob.t[0:m, :]), r=[ob], w=[rwT], acc=True, dma=True)

    P.push_scope()
    phase1()
    P.barrier()
    P.pop_scope()
    wgb = P.dram("wgb", [D, 4096], BF16)
    wab = P.dram("wab", [D, D], BF16)
    wbb = P.dram("wbb", [D, D], BF16)
    wob = P.dram("wob", [D, D], BF16)
    wf1b = P.dram("wf1b", [D, 11264], BF16)
    wf2b = P.dram("wf2b", [5632, D], BF16)
    if not DEBUG.get("no_conv"):
        convert(wgb, wg_in, D)
    convert(wab, wa_in, D)
    convert(wbb, wb_in, D)
    convert(wob, wo_in, D)
    convert(wf1b, wf1_in, D)
    convert(wf2b, wf2_in, 5632)
    NBLK = S_att // 512
    if DEBUG.get("p2"):
        oT = dout("oT", [NBLK, 2048, 512], BF16)
    else:
        oT = P.dram("oT", [NBLK, 2048, 512], BF16)

    def phase_attn():
        LAMBDA_INIT = 0.8 - 0.6 * math.exp(-0.3 * 0)
        NG = S_att // 512
        NKB = S_att // 128
        kT = [P.sb("kT%d" % c, [128, S_att], BF16) for c in range(2)]
        v1 = P.sb("v1", [128, NKB, 257], BF16)
        qg_r = Ring(P, "qg", 3, [128, 512], BF16)
        pT_r = Ring(P, "pT", 4, [128, 512], BF16)
        o_c = [P.sb("o_c%d" % c, [128, 4, 257], F32) for c in range(2)]
        lqk = P.sb("lqk_sb", [128, 512], F32)
        lam = P.sb("lam_sb", [128, 8], F32)
        gsb = P.sb("gsb", [128, 256], F32)
        sm = Ring(P, "sm", 4, [128, 8], F32)
        ta_r = Ring(P, "ta", 2, [128, 256], F32)
        to_r = Ring(P, "to", 2, [128, 256], F32)
        jk = P.sb("jk2", [128, 256], BF16)
        obf_r = Ring(P, "obf", 2, [128, 256], BF16)
        ost_r = Ring(P, "ost", 2, [128, 2, 512], BF16)
        acc = psr.b[0:4]
        st_b = psr.b[4:7]
        misc = psr.b[7]
        sti = [0]

        P.op("sp", lambda e: e.dma_start(out=lqk.t[:, :], in_=lqk_in.t[0:1, :].partition_broadcast(128)), w=[lqk], dma=True)
        P.op("sp", lambda e: e.dma_start(out=gsb.t[:, :], in_=gsub_in.t[0:1, :].partition_broadcast(128)), w=[gsb], dma=True)
        P.op("dve", lambda e: e.tensor_tensor(out=lqk.t[:, 0:256], in0=lqk.t[:, 0:256], in1=lqk.t[:, 256:512], op=ALU.mult),
             r=[lqk], w=[lqk])
        P.op("dve", lambda e: e.tensor_reduce(out=lam.t[:, 0:2], in_=lqk.t[:, 0:256].rearrange("p (a b) -> p a b", a=2),
                                              axis=AX.X, op=ALU.add), r=[lqk], w=[lam])
        P.op("act", lambda e: e.activation(out=lam.t[:, 2:4], in_=lam.t[:, 0:2], func=AF.Exp), r=[lam], w=[lam])
        P.op("dve", lambda e: e.scalar_tensor_tensor(out=lam.t[:, 4:5], in0=lam.t[:, 3:4], scalar=-LAMBDA_INIT, in1=lam.t[:, 2:3],
                                                     op0=ALU.add, op1=ALU.subtract), r=[lam], w=[lam])
        P.op("dve", lambda e: e.tensor_scalar(out=gsb.t[:, :], in0=gsb.t[:, :], scalar1=1.0 - LAMBDA_INIT, scalar2=None,
                                              op0=ALU.mult), r=[gsb], w=[gsb])
        for hd in range(4):
            for c in range(2):
                P.op("sp", lambda e, c=c, hd=hd: e.dma_start(out=kT[c].t[:, :], in_=qkT.t[8 + hd * 2 + c, :, 0:S_att]),
                     r=[qkT], w=[kT[c]], dma=True)
            P.op("sp", lambda e, hd=hd: e.dma_start(
                out=v1.t[:, :, 0:256], in_=vtm.t[0:S_att, hd * 256:(hd + 1) * 256].rearrange("(kb p) e -> p kb e", p=128)),
                r=[vtm], w=[v1], dma=True)
            P.op("pool", lambda e: e.memset(v1.t[:, :, 256:257], 1.0), w=[v1], acc=True)
            for j in range(NG):
                for c in range(2):
                    qg = qg_r.get()
                    P.op("sp", lambda e, qg=qg, c=c, hd=hd, j=j: e.dma_start(
                        out=qg.t[:, :], in_=qkT.t[hd * 2 + c, :, j * 512:(j + 1) * 512]), r=[qkT], w=[qg], dma=True)
                    nkb = 4 * j + 4
                    issued = []

                    def emit_st(kb, qg=qg, c=c, j=j):
                        m = max(kb - 4 * j, 0)
                        c0 = m * 128
                        sps = st_b[sti[0] % 3]
                        sti[0] += 1
                        P.op("pe", lambda e, sps=sps, c=c, kb=kb, qg=qg, c0=c0: e.matmul(
                            sps.t[:, c0:512], lhsT=kT[c].t[:, kb * 128:(kb + 1) * 128], rhs=qg.t[:, c0:512],
                            start=True, stop=True), r=[kT[c], qg], w=[sps])
                        issued.append((sps, m, c0))

                    for kb in range(nkb):
                        while len(issued) < min(kb + 3, nkb):
                            emit_st(len(issued))
                        sps, m, c0 = issued[kb]
                        pT = pT_r.get()
                        P.op("act", lambda e, sps=sps, pT=pT, c0=c0: e.activation(
                            out=pT.t[:, c0:512], in_=sps.t[:, c0:512], func=AF.Exp, scale=128 ** -0.5), r=[sps], w=[pT])
                        if kb >= 4 * j:
                            P.op("pool", lambda e, pT=pT, c0=c0: e.memset(pT.t[64:128, c0:c0 + 64], 0.0), w=[pT], acc=True)
                        for qs in range(m, 4):
                            P.op("pe", lambda e, qs=qs, pT=pT, kb=kb, j=j: e.matmul(
                                acc[qs].t[:, 0:257], lhsT=pT.t[:, qs * 128:(qs + 1) * 128], rhs=v1.t[:, kb, :],
                                start=(kb == 0), stop=(kb == 4 * j + qs)), r=[pT, v1], w=[acc[qs]], acc=(kb > 0))
                    for qs in range(4):
                        P.op("act", lambda e, qs=qs, c=c: e.activation(out=o_c[c].t[:, qs, :], in_=acc[qs].t[:, 0:257], func=AF.Copy),
                             r=[acc[qs]], w=[o_c[c]], acc=True)
                ost = ost_r.get()
                for qs in range(4):
                    s_ = sm.get()
                    P.op("dve", lambda e, s_=s_, qs=qs: e.reciprocal(out=s_.t[:, 0:1], in_=o_c[0].t[:, qs, 256:257]), r=[o_c[0]], w=[s_])
                    P.op("dve", lambda e, s_=s_, qs=qs: e.reciprocal(out=s_.t[:, 1:2], in_=o_c[1].t[:, qs, 256:257]), r=[o_c[1]], w=[s_])
                    P.op("dve", lambda e, s_=s_: e.tensor_tensor(out=s_.t[:, 2:3], in0=s_.t[:, 1:2], in1=lam.t[:, 4:5], op=ALU.mult),
                         r=[s_, lam], w=[s_])
                    ta = ta_r.get()
                    P.op("dve", lambda e, s_=s_, ta=ta, qs=qs: e.tensor_scalar(out=ta.t[:, :], in0=o_c[0].t[:, qs, 0:256],
                                                                         scalar1=s_.t[:, 0:1], scalar2=None, op0=ALU.mult),
                         r=[s_, o_c[0]], w=[ta])
                    to = to_r.get()
                    P.op("dve", lambda e, s_=s_, ta=ta, to=to, qs=qs: e.scalar_tensor_tensor(
                        out=to.t[:, :], in0=o_c[1].t[:, qs, 0:256], scalar=s_.t[:, 2:3], in1=ta.t[:, :],
                        op0=ALU.mult, op1=ALU.add), r=[s_, o_c[1], ta], w=[to])
                    P.op("pool", lambda e, s_=s_: e.memset(s_.t[:, 3:4], 0.0), w=[s_], acc=True)
                    P.op("act", lambda e, s_=s_, to=to: e.activation(out=jk.t[:, :], in_=to.t[:, :], func=AF.Square,
                                                                     accum_out=s_.t[:, 3:4]), r=[to, s_], w=[jk, s_])
                    P.op("act", lambda e, s_=s_: e.activation(out=s_.t[:, 4:5], in_=s_.t[:, 3:4], func=AF.Sqrt,
                                                              bias=epsb.t[:, 1:2], scale=1.0 / 256), r=[s_, epsb], w=[s_])
                    P.op("dve", lambda e, s_=s_: e.reciprocal(out=s_.t[:, 5:6], in_=s_.t[:, 4:5]), r=[s_], w=[s_])
                    obf = obf_r.get()
                    P.op("dve", lambda e, s_=s_, to=to, obf=obf: e.scalar_tensor_tensor(
                        out=obf.t[:, :], in0=to.t[:, :], scalar=s_.t[:, 5:6], in1=gsb.t[:, :], op0=ALU.mult, op1=ALU.mult),
                        r=[to, s_, gsb], w=[obf])
                    for ec in range(2):
                        P.op("pe", lambda e, obf=obf, ec=ec: e.transpose(
                            out=bf(misc)[:, ec * 128:(ec + 1) * 128], in_=obf.t[:, ec * 128:(ec + 1) * 128], identity=identb.t[:, :]),
                            r=[obf, identb], w=[misc], acc=(ec > 0))
                    P.op("act", lambda e, ost=ost, qs=qs: e.activation(
                        out=ost.t[:, :, qs * 128:(qs + 1) * 128], in_=bf(misc)[:, 0:256].rearrange("p (a t) -> p a t", a=2),
                        func=AF.Copy), r=[misc], w=[ost], acc=True)
                for ec in range(2):
                    P.op("sp", lambda e, ost=ost, ec=ec, hd=hd, j=j: e.dma_start(
                        out=oT.t[j, hd * 256 + ec * 128:hd * 256 + (ec + 1) * 128, :], in_=ost.t[:, ec, :]),
                        r=[ost], w=[oT], acc=True, dma=True)

    P.push_scope()
    phase_attn()
    P.barrier()
    P.pop_scope()

    if stop_after <= 2:
        P.emit()
        return nc, P, dbg

    def phase_rwkv():
        v3 = lambda ap: ap.rearrange("p (c t) -> p c t", c=8)
        y3 = lambda ap: ap.rearrange("p (c i) -> p c i", c=8)
        psr_all = psr
        psr7 = PsumRing(P, bufs=psr_all.b[0:7])
        pS = psr_all.b[7]
        NB = S_att // 512
        C0 = math.exp(-0.5)
        rwp = P.sb("rwp_sb", [128, 8, 10], F32)
        mul = P.sb("mul_sb", [128, 4], F32)
        wupb = P.sb("wupb", [96, 1024], BF16)
        aupb = P.sb("aupb", [96, 1024], BF16)
        gupb = P.sb("gupb", [128, 2, 1024], BF16)
        mSU = P.sb("mSU_sb", [64, 512], F32)
        mIU = P.sb("mIU_sb", [64, 512], F32)
        mSL = P.sb("mSL_sb", [64, 512], F32)
        I8 = P.sb("I8_sb", [64, 512], BF16)
        scanm = P.sb("scanm_sb", [128, 512], F32)
        bones = P.sb("bones_sb", [128, 128], BF16)
        gn_eps = P.sb("gn_eps", [64, 1], F32)
        for (dst, src, q) in ((rwp, rwp_in, "sp"), (mul, mul_in, "sp"), (mSU, mSU_in, "sp"), (mIU, mIU_in, "sp"),
                              (mSL, mSL_in, "sp"), (scanm, scanm_in, "sp"), (wupb, wup_in, "pool"), (aupb, aup_in, "pool"),
                              (I8, I8_in, "pool"), (bones, bones_in, "pool")):
            nd = len(dst.t.shape)
            if nd == 3:
                P.op(q, lambda e, dst=dst, src=src: e.dma_start(out=dst.t[:, :, :], in_=src.t[:, :, :]), w=[dst], dma=True)
            else:
                P.op(q, lambda e, dst=dst, src=src: e.dma_start(out=dst.t[:, :], in_=src.t[:, :]), w=[dst], dma=True)
        P.op("pool", lambda e: e.dma_start(out=gupb.t[:, :, :], in_=gup_in.t[:, :].rearrange("(k p) c -> p k c", p=128)),
             w=[gupb], dma=True)
        P.op("pool", lambda e: e.memset(gn_eps.t[:, :], 64e-5), w=[gn_eps])

        S32 = P.sb("S32", [128, 8, 64], F32)
        STb = P.sb("STb", [128, 8, 64], BF16)
        P.op("pool", lambda e: e.memset(S32.t[:, :, :], 0.0), w=[S32])
        P.op("pool", lambda e: e.memset(STb.t[:, :, :], 0.0), w=[STb])

        AR = P.sb("AR", [128, 8, 8, 128], BF16)
        Kt = P.sb("Kt", [128, 8, 512], BF16)
        Bt = P.sb("Bt", [128, 8, 512], BF16)
        VF = P.sb("VF", [128, 8, 512], BF16)
        KH = P.sb("KH", [128, 8, 512], BF16)
        BH = P.sb("BH", [128, 8, 512], BF16)
        GF = P.sb("GF", [128, 8, 512], BF16)
        BON = P.sb("BON", [128, 8, 512], BF16)
        YT = P.sb("YT", [128, 8, 512], BF16)
        pCs = P.sb("pCs", [128, 8, 8], F32)
        Lw = P.sb("Lw", [128, 513], BF16)
        La = P.sb("La", [128, 513], BF16)
        Lg = P.sb("Lg", [128, 2, 513], BF16)
        tanw = P.sb("tanw", [128, 512], BF16)
        xsal = P.sb("xsal", [128, 512], BF16)
        sigg = P.sb("sigg", [128, 2, 512], BF16)
        f32r = {}

        def T32(name, n=1):
            if name not in f32r:
                f32r[name] = Ring(P, "rw_" + name, n, [128, 512], F32)
            return f32r[name].get()

        Lr_r = Ring(P, "Lr", 1, [128, 513], BF16)
        Lk_r = Ring(P, "Lk", 1, [128, 513], BF16)
        Lv_r = Ring(P, "Lv", 1, [128, 513], BF16)
        sq_r = Ring(P, "sqr", 2, [128, 512], BF16)
        ost_r = Ring(P, "rwost", 2, [128, 512], BF16)
        tm_r = {k: Ring(P, "tm" + k, 2, [64, 1024], BF16) for k in ("v", "k", "b")}
        m_r = {k: Ring(P, "m" + k, 2, [64, 512], BF16) for k in ("ak", "ab", "rk", "rb", "mt")}
        x_r = [Ring(P, "xr%d" % h, 2, [64, 512], BF16) for h in range(2)]
        y_r = [Ring(P, "yr%d" % h, 2, [64, 512], BF16) for h in range(2)]
        t_r = [Ring(P, "tr%d" % h, 2, [64, 512], BF16) for h in range(2)]
        tt_r = [Ring(P, "ttr%d" % h, 2, [64, 512], BF16) for h in range(2)]
        w1s_r = Ring(P, "w1s", 1, [64, 512], F32)
        wtm_r = Ring(P, "wtm", 2, [64, 512], BF16)
        utm_r = Ring(P, "utm", 2, [64, 512], BF16)
        y1s_r = Ring(P, "y1s", 1, [64, 512], F32)
        ytm_r = Ring(P, "ytm", 2, [64, 512], F32)
        ysq_r = Ring(P, "ysq", 2, [64, 512], F32)
        yh_r = Ring(P, "yh", 2, [64, 1024], BF16)
        st_r = Ring(P, "gnst", 4, [64, 48], F32)

        def shift_mix(L, n, mu_ap, parts, out_fn):
            d = T32("d")
            P.op("dve", lambda e: e.tensor_tensor(out=d.t[0:parts, :], in0=L[0], in1=L[1], op=ALU.subtract), r=[L[2]], w=[d])
            xs = T32("xsl")
            P.op("dve", lambda e: e.scalar_tensor_tensor(out=xs.t[0:parts, :], in0=d.t[0:parts, :], scalar=mu_ap, in1=L[1],
                                                         op0=ALU.mult, op1=ALU.add), r=[d, L[2], mul, rwp], w=[xs])
            return xs

        def load_shift(q, Lb, rows, row0, t0, view3=None):
            if t0 == 0:
                P.op("pool", lambda e: e.memset(Lb.t[:, 0:1] if view3 is None else Lb.t[:, :, 0:1], 0.0), w=[Lb])
                if view3 is None:
                    P.op(q, lambda e: e.dma_start(out=Lb.t[0:rows, 1:513], in_=rwT.t[row0:row0 + rows, 0:512]),
                         r=[rwT], w=[Lb], acc=True, dma=True)
                else:
                    for k in range(2):
                        P.op(q, lambda e, k=k: e.dma_start(out=Lb.t[:, k, 1:513], in_=rwT.t[row0 + k * 128:row0 + (k + 1) * 128, 0:512]),
                             r=[rwT], w=[Lb], acc=True, dma=True)
            else:
                if view3 is None:
                    P.op(q, lambda e: e.dma_start(out=Lb.t[0:rows, 0:513], in_=rwT.t[row0:row0 + rows, t0 - 1:t0 + 512]),
                         r=[rwT], w=[Lb], dma=True)
                else:
                    for k in range(2):
                        P.op(q, lambda e, k=k: e.dma_start(out=Lb.t[:, k, 0:513],
                                                            in_=rwT.t[row0 + k * 128:row0 + (k + 1) * 128, t0 - 1:t0 + 512]),
                             r=[rwT], w=[Lb], acc=(k > 0), dma=True)

        for tb in range(NB):
            t0 = tb * 512
            load_shift("sp", Lw, 96, 6144 - 3072, t0)
            load_shift("sp", La, 96, 6240 - 3072, t0)
            load_shift("sp", Lg, 128, 6336 - 3072, t0, view3=True)
            xs = shift_mix((Lw.t[0:96, 0:512], Lw.t[0:96, 1:513], Lw), 512, mul.t[0:96, 0:1], 96, None)
            P.op("act", lambda e, xs=xs: e.activation(out=tanw.t[0:96, :], in_=xs.t[0:96, :], func=AF.Tanh), r=[xs], w=[tanw])
            xs = shift_mix((La.t[0:96, 0:512], La.t[0:96, 1:513], La), 512, mul.t[0:96, 1:2], 96, None)
            P.op("act", lambda e, xs=xs: e.activation(out=xsal.t[0:96, :], in_=xs.t[0:96, :], func=AF.Copy), r=[xs], w=[xsal])
            for k in range(2):
                xs = shift_mix((Lg.t[:, k, 0:512], Lg.t[:, k, 1:513], Lg), 512, mul.t[:, 2 + k:3 + k], 128, None)
                P.op("act", lambda e, xs=xs, k=k: e.activation(out=sigg.t[:, k, :], in_=xs.t[:, :], func=AF.Sigmoid),
                     r=[xs], w=[sigg], acc=(k > 0))
            for cp in range(8):
                Ls = []
                for i, Rg in enumerate((Lr_r, Lk_r, Lv_r)):
                    Lb = Rg.get()
                    load_shift("sp", Lb, 128, i * 1024 + cp * 128, t0)
                    Ls.append(Lb)
                xs3 = []
                for i, nm in enumerate(("xr", "xk", "xv")):
                    Lb = Ls[i]
                    d = T32("d")
                    P.op("dve", lambda e, d=d, Lb=Lb: e.tensor_tensor(out=d.t[:, :], in0=Lb.t[:, 0:512], in1=Lb.t[:, 1:513],
                                                                      op=ALU.subtract), r=[Lb], w=[d])
                    x_ = T32(nm)
                    P.op("dve", lambda e, d=d, Lb=Lb, x_=x_, i=i, cp=cp: e.scalar_tensor_tensor(
                        out=x_.t[:, :], in0=d.t[:, :], scalar=rwp.t[:, cp, i:i + 1], in1=Lb.t[:, 1:513], op0=ALU.mult, op1=ALU.add),
                        r=[d, Lb, rwp], w=[x_])
                    xs3.append(x_)
                xr, xk, xv = xs3
                pb = psr7.get()
                P.op("pe", lambda e, pb=pb, cp=cp: e.matmul(pb.t[:, 0:512], lhsT=wupb.t[0:96, cp * 128:(cp + 1) * 128],
                                                             rhs=tanw.t[0:96, :], start=True, stop=True), r=[wupb, tanw], w=[pb])
                sigw = T32("sigw")
                P.op("act", lambda e, pb=pb, sigw=sigw, cp=cp: e.activation(out=sigw.t[:, :], in_=pb.t[:, :], func=AF.Sigmoid,
                                                                            bias=rwp.t[:, cp, 3:4]), r=[pb, rwp], w=[sigw])
                cum = T32("cum")
                P.op("dve", lambda e, cum=cum, sigw=sigw: e.tensor_tensor_scan(out=cum.t[:, :], data0=scanm.t[:, :], data1=sigw.t[:, :],
                                                                               initial=0.0, op0=ALU.mult, op1=ALU.add),
                     r=[scanm, sigw], w=[cum])
                cpm = T32("cpm")
                P.op("pool", lambda e, cpm=cpm, cum=cum, sigw=sigw: e.tensor_tensor(out=cpm.t[:, :], in0=cum.t[:, :], in1=sigw.t[:, :],
                                                                                   op=ALU.subtract), r=[cum, sigw], w=[cpm])
                epos = T32("epos")
                eneg = T32("eneg")
                eprev = T32("eprev")
                P.op("act", lambda e, epos=epos, cum=cum: e.activation(out=epos.t[:, :], in_=cum.t[:, :], func=AF.Exp, scale=-C0),
                     r=[cum], w=[epos])
                P.op("act", lambda e, eneg=eneg, cum=cum: e.activation(out=eneg.t[:, :], in_=cum.t[:, :], func=AF.Exp, scale=C0),
                     r=[cum], w=[eneg])
                P.op("act", lambda e, eprev=eprev, cpm=cpm: e.activation(out=eprev.t[:, :], in_=cpm.t[:, :], func=AF.Exp, scale=-C0),
                     r=[cpm], w=[eprev])
                pb = psr7.get()
                P.op("pe", lambda e, pb=pb, cp=cp: e.matmul(pb.t[:, 0:512], lhsT=aupb.t[0:96, cp * 128:(cp + 1) * 128],
                                                             rhs=xsal.t[0:96, :], start=True, stop=True), r=[aupb, xsal], w=[pb])
                alr = T32("alr")
                P.op("act", lambda e, pb=pb, alr=alr, cp=cp: e.activation(out=alr.t[:, :], in_=pb.t[:, :], func=AF.Sigmoid,
                                                                          bias=rwp.t[:, cp, 4:5]), r=[pb, rwp], w=[alr])
                pb = psr7.get()
                for k in range(2):
                    P.op("pe", lambda e, pb=pb, cp=cp, k=k: e.matmul(pb.t[:, 0:512], lhsT=gupb.t[:, k, cp * 128:(cp + 1) * 128],
                                                                      rhs=sigg.t[:, k, :], start=(k == 0), stop=(k == 1)),
                         r=[gupb, sigg], w=[pb], acc=(k > 0))
                P.op("act", lambda e, pb=pb, cp=cp: e.activation(out=GF.t[:, cp, :], in_=pb.t[:, :], func=AF.Copy),
                     r=[pb], w=[GF], acc=True)
                sq = sq_r.get()
                P.op("act", lambda e, sq=sq, xk=xk, cp=cp: e.activation(out=sq.t[:, :], in_=xk.t[:, :], func=AF.Square,
                                                                        scale=rwp.t[:, cp, 5:6]), r=[xk, rwp], w=[sq])
                pb = psr7.get()
                P.op("pe", lambda e, pb=pb, sq=sq: e.matmul(pb.t[:, 0:512], lhsT=bones.t[:, :], rhs=sq.t[:, :], start=True, stop=True),
                     r=[bones, sq], w=[pb])
                rn = T32("d")
                P.op("act", lambda e, pb=pb, rn=rn: e.activation(out=rn.t[:, :], in_=pb.t[:, :], func=AF.Sqrt), r=[pb], w=[rn])
                P.op("dve", lambda e, rn=rn: e.tensor_scalar(out=rn.t[:, :], in0=rn.t[:, :], scalar1=1e-12, scalar2=None, op0=ALU.max),
                     r=[rn], w=[rn])
                P.op("dve", lambda e, rn=rn: e.reciprocal(out=rn.t[:, :], in_=rn.t[:, :]), r=[rn], w=[rn])
                kk = T32("cpm")
                P.op("dve", lambda e, kk=kk, xk=xk, rn=rn, cp=cp: e.scalar_tensor_tensor(
                    out=kk.t[:, :], in0=xk.t[:, :], scalar=rwp.t[:, cp, 5:6], in1=rn.t[:, :], op0=ALU.mult, op1=ALU.mult),
                    r=[xk, rn, rwp], w=[kk])
                tq = T32("d")
                P.op("dve", lambda e, tq=tq, alr=alr, cp=cp: e.tensor_scalar(out=tq.t[:, :], in0=alr.t[:, :], scalar1=-1.0,
                                                                             scalar2=rwp.t[:, cp, 6:7], op0=ALU.add, op1=ALU.mult),
                     r=[alr, rwp], w=[tq])
                kp = T32("sigw")
                P.op("dve", lambda e, kp=kp, tq=tq, xk=xk: e.scalar_tensor_tensor(out=kp.t[:, :], in0=tq.t[:, :], scalar=1.0,
                                                                                 in1=xk.t[:, :], op0=ALU.add, op1=ALU.mult),
                     r=[tq, xk], w=[kp])
                bb = T32("xsl")
                P.op("pool", lambda e, bb=bb, kk=kk, alr=alr: e.tensor_tensor(out=bb.t[:, :], in0=kk.t[:, :], in1=alr.t[:, :], op=ALU.mult),
                     r=[kk, alr], w=[bb])
                P.op("dve", lambda e, kk=kk, eprev=eprev, cp=cp: e.scalar_tensor_tensor(
                    out=AR.t[:, cp, :, 0:64], in0=v3(kk.t[:, :]), scalar=-1.0, in1=v3(eprev.t[:, :]), op0=ALU.mult, op1=ALU.mult),
                    r=[kk, eprev], w=[AR], acc=True)
                P.op("dve", lambda e, xr=xr, epos=epos, cp=cp: e.tensor_tensor(
                    out=AR.t[:, cp, :, 64:128], in0=v3(xr.t[:, :]), in1=v3(epos.t[:, :]), op=ALU.mult), r=[xr, epos], w=[AR], acc=True)
                P.op("pool", lambda e, kp=kp, eneg=eneg, cp=cp: e.tensor_tensor(out=Kt.t[:, cp, :], in0=kp.t[:, :], in1=eneg.t[:, :],
                                                                              op=ALU.mult), r=[kp, eneg], w=[Kt], acc=True)
                P.op("pool", lambda e, bb=bb, eneg=eneg, cp=cp: e.tensor_tensor(out=Bt.t[:, cp, :], in0=bb.t[:, :], in1=eneg.t[:, :],
                                                                              op=ALU.mult), r=[bb, eneg], w=[Bt], acc=True)
                P.op("dve", lambda e, epos=epos, cp=cp: e.tensor_tensor(
                    out=v3(KH.t[:, cp, :]), in0=v3(Kt.t[:, cp, :]), in1=v3(epos.t[:, :])[:, :, 63:64].to_broadcast([128, 8, 64]),
                    op=ALU.mult), r=[Kt, epos], w=[KH], acc=True)
                P.op("dve", lambda e, epos=epos, cp=cp: e.tensor_tensor(
                    out=v3(BH.t[:, cp, :]), in0=v3(Bt.t[:, cp, :]), in1=v3(epos.t[:, :])[:, :, 63:64].to_broadcast([128, 8, 64]),
                    op=ALU.mult), r=[Bt, epos], w=[BH], acc=True)
                P.op("act", lambda e, xv=xv, cp=cp: e.activation(out=VF.t[:, cp, :], in_=xv.t[:, :], func=AF.Copy),
                     r=[xv], w=[VF], acc=True)
                P.op("dve", lambda e, epos=epos, cp=cp: e.tensor_copy(out=pCs.t[:, cp, :], in_=v3(epos.t[:, :])[:, :, 63]),
                     r=[epos], w=[pCs], acc=True)
                sq2 = sq_r.get()
                P.op("dve", lambda e, sq2=sq2, xr=xr, kp=kp, cp=cp: e.scalar_tensor_tensor(
                    out=sq2.t[:, :], in0=xr.t[:, :], scalar=rwp.t[:, cp, 7:8], in1=kp.t[:, :], op0=ALU.mult, op1=ALU.mult),
                    r=[xr, kp, rwp], w=[sq2])
                pb = psr7.get()
                P.op("pe", lambda e, pb=pb, sq2=sq2: e.matmul(pb.t[:, 0:512], lhsT=bones.t[:, :], rhs=sq2.t[:, :], start=True, stop=True),
                     r=[bones, sq2], w=[pb])
                P.op("dve", lambda e, pb=pb, xv=xv, cp=cp: e.tensor_tensor(out=BON.t[:, cp, :], in0=pb.t[:, :], in1=xv.t[:, :], op=ALU.mult),
                     r=[pb, xv], w=[BON], acc=True)

            for ch in range(8 if DEBUG.get('rw_stop', 'full') != 'B' else 0):
                cs = slice(ch * 64, (ch + 1) * 64)
                tm = {}
                for key, src in (("v", VF), ("k", KH), ("b", BH)):
                    pb = psr7.get()
                    for cp in range(8):
                        P.op("pe", lambda e, pb=pb, src=src, cp=cp, cs=cs: e.transpose(
                            out=bf(pb)[0:64, cp * 128:(cp + 1) * 128], in_=src.t[:, cp, cs], identity=identb.t[:, :]),
                            r=[src, identb], w=[pb], acc=(cp > 0))
                    tmb = tm_r[key].get()
                    P.op("act", lambda e, pb=pb, tmb=tmb: e.activation(out=tmb.t[:, :], in_=bf(pb)[0:64, :], func=AF.Copy),
                         r=[pb], w=[tmb])
                    tm[key] = tmb
                Ms = {}
                for hd in range(2):
                    hp = slice(hd * 64, (hd + 1) * 64)
                    specs = (("ak", Kt, 0, mSU), ("ab", Bt, 0, mSU), ("rk", Kt, 64, mIU), ("rb", Bt, 64, mIU))
                    for key, L, off, msk in specs:
                        pb = psr7.get()
                        for cp in range(8):
                            P.op("pe", lambda e, pb=pb, L=L, cp=cp, off=off, hp=hp, ch=ch, cs=cs: e.matmul(
                                pb.t[0:64, cp * 64:(cp + 1) * 64], lhsT=L.t[hp, cp, cs], rhs=AR.t[hp, cp, ch, off:off + 64],
                                start=(cp == 0), stop=True, skip_group_check=True), r=[L, AR], w=[pb], acc=(cp > 0))
                        mb = m_r[key].get()
                        P.op("dve", lambda e, pb=pb, mb=mb, msk=msk: e.tensor_tensor(out=mb.t[:, :], in0=pb.t[0:64, :], in1=msk.t[:, :],
                                                                                   op=ALU.mult), r=[pb, msk], w=[mb])
                        Ms[(key, hd)] = mb
                    pb = psr7.get()
                    for cp in range(8):
                        P.op("pe", lambda e, pb=pb, cp=cp, hp=hp, ch=ch, cs=cs: e.matmul(
                            pb.t[0:64, cp * 64:(cp + 1) * 64], lhsT=AR.t[hp, cp, ch, 0:64], rhs=Bt.t[hp, cp, cs],
                            start=(cp == 0), stop=True, skip_group_check=True), r=[Bt, AR], w=[pb], acc=(cp > 0))
                    mb = m_r["mt"].get()
                    P.op("dve", lambda e, pb=pb, mb=mb: e.tensor_tensor(out=mb.t[:, :], in0=pb.t[0:64, :], in1=mSL.t[:, :], op=ALU.mult),
                         r=[pb, mSL], w=[mb])
                    Ms[("mt", hd)] = mb
                if DEBUG.get('rw_stop') == 'C':
                    continue
                Tm = {}
                for hd in range(2):
                    X = Ms[("ab", hd)]
                    Y = Ms[("mt", hd)]
                    Tb = t_r[hd].get()
                    TTb = tt_r[hd].get()
                    P.op("pool", lambda e, Tb=Tb, X=X: e.tensor_tensor(out=Tb.t[:, :], in0=X.t[:, :], in1=I8.t[:, :], op=ALU.add),
                         r=[X, I8], w=[Tb])
                    P.op("pool", lambda e, TTb=TTb, Y=Y: e.tensor_tensor(out=TTb.t[:, :], in0=Y.t[:, :], in1=I8.t[:, :], op=ALU.add),
                         r=[Y, I8], w=[TTb])
                    for lvl in range(5):
                        last = (lvl == 4)
                        pX = psr7.get()
                        for cp in range(8):
                            c_ = slice(cp * 64, (cp + 1) * 64)
                            P.op("pe", lambda e, pX=pX, X=X, Y=Y, c_=c_, cp=cp: e.matmul(
                                pX.t[0:64, c_], lhsT=Y.t[:, c_], rhs=X.t[:, c_], start=(cp == 0), stop=True, skip_group_check=True),
                                r=[X, Y], w=[pX], acc=(cp > 0))
                        X2 = x_r[hd].get()
                        P.op("act", lambda e, pX=pX, X2=X2: e.activation(out=X2.t[:, :], in_=pX.t[0:64, :], func=AF.Copy), r=[pX], w=[X2])
                        if not last:
                            pY = psr7.get()
                            for cp in range(8):
                                c_ = slice(cp * 64, (cp + 1) * 64)
                                P.op("pe", lambda e, pY=pY, X=X, Y=Y, c_=c_, cp=cp: e.matmul(
                                    pY.t[0:64, c_], lhsT=X.t[:, c_], rhs=Y.t[:, c_], start=(cp == 0), stop=True, skip_group_check=True),
                                    r=[X, Y], w=[pY], acc=(cp > 0))
                            Y2 = y_r[hd].get()
                            P.op("act", lambda e, pY=pY, Y2=Y2: e.activation(out=Y2.t[:, :], in_=pY.t[0:64, :], func=AF.Copy),
                                 r=[pY], w=[Y2])
                        pT = psr7.get()
                        for cp in range(8):
                            c_ = slice(cp * 64, (cp + 1) * 64)
                            P.op("pe", lambda e, pT=pT, TTb=TTb, X2=X2, c_=c_, cp=cp: e.matmul(
                                pT.t[0:64, c_], lhsT=TTb.t[:, c_], rhs=X2.t[:, c_], start=(cp == 0), stop=True, skip_group_check=True),
                                r=[TTb, X2], w=[pT], acc=(cp > 0))
                        Tn = t_r[hd].get()
                        P.op("dve", lambda e, pT=pT, Tn=Tn, Tb=Tb: e.tensor_tensor(out=Tn.t[:, :], in0=pT.t[0:64, :], in1=Tb.t[:, :], op=ALU.add),
                             r=[pT, Tb], w=[Tn])
                        if not last:
                            pTT = psr7.get()
                            for cp in range(8):
                                c_ = slice(cp * 64, (cp + 1) * 64)
                                P.op("pe", lambda e, pTT=pTT, TTb=TTb, X2=X2, c_=c_, cp=cp: e.matmul(
                                    pTT.t[0:64, c_], lhsT=X2.t[:, c_], rhs=TTb.t[:, c_], start=(cp == 0), stop=True, skip_group_check=True),
                                    r=[TTb, X2], w=[pTT], acc=(cp > 0))
                            TTn = tt_r[hd].get()
                            P.op("dve", lambda e, pTT=pTT, TTn=TTn, TTb=TTb: e.tensor_tensor(out=TTn.t[:, :], in0=pTT.t[0:64, :],
                                                                                           in1=TTb.t[:, :], op=ALU.add),
                                 r=[pTT, TTb], w=[TTn])
                            TTb = TTn
                            Y = Y2
                        Tb = Tn
                        X = X2
                    Tm[hd] = Tb
                if DEBUG.get('rw_stop') == 'D2':
                    continue
                ytm = {}
                for hd in range(2):
                    hp = slice(hd * 64, (hd + 1) * 64)
                    hcol = lambda cp, hd=hd: slice((cp * 2 + hd) * 64, (cp * 2 + hd + 1) * 64)
                    p1 = psr7.get()
                    for cp in range(8):
                        P.op("pe", lambda e, p1=p1, cp=cp, hp=hp, ch=ch: e.matmul(
                            p1.t[0:64, cp * 64:(cp + 1) * 64], lhsT=AR.t[hp, cp, ch, 0:64], rhs=STb.t[hp, cp, :],
                            start=(cp == 0), stop=True, skip_group_check=True), r=[AR, STb], w=[p1], acc=(cp > 0))
                    w1s = w1s_r.get()
                    P.op("act", lambda e, p1=p1, w1s=w1s: e.activation(out=w1s.t[:, :], in_=p1.t[0:64, :], func=AF.Copy), r=[p1], w=[w1s])
                    p2 = psr7.get()
                    mak = Ms[("ak", hd)]
                    for cp in range(8):
                        P.op("pe", lambda e, p2=p2, cp=cp, mak=mak, hcol=hcol, tmv=tm["v"]: e.matmul(
                            p2.t[0:64, cp * 64:(cp + 1) * 64], lhsT=mak.t[:, cp * 64:(cp + 1) * 64], rhs=tmv.t[:, hcol(cp)],
                            start=(cp == 0), stop=True, skip_group_check=True), r=[mak, tm["v"]], w=[p2], acc=(cp > 0))
                    wtm = wtm_r.get()
                    P.op("dve", lambda e, p2=p2, w1s=w1s, wtm=wtm: e.tensor_tensor(out=wtm.t[:, :], in0=p2.t[0:64, :], in1=w1s.t[:, :], op=ALU.add),
                         r=[p2, w1s], w=[wtm])
                    if DEBUG.get('rw_stop') == 'D3a':
                        continue
                    p3 = psr7.get()
                    Tb = Tm[hd]
                    for cp in range(8):
                        c_ = slice(cp * 64, (cp + 1) * 64)
                        P.op("pe", lambda e, p3=p3, Tb=Tb, wtm=wtm, c_=c_, cp=cp: e.matmul(
                            p3.t[0:64, c_], lhsT=Tb.t[:, c_], rhs=wtm.t[:, c_], start=(cp == 0), stop=True, skip_group_check=True),
                            r=[Tb, wtm], w=[p3], acc=(cp > 0))
                    utm = utm_r.get()
                    P.op("act", lambda e, p3=p3, utm=utm: e.activation(out=utm.t[:, :], in_=p3.t[0:64, :], func=AF.Copy), r=[p3], w=[utm])
                    if DEBUG.get('rw_stop') == 'D3b':
                        continue
                    p4 = psr7.get()
                    for cp in range(8):
                        P.op("pe", lambda e, p4=p4, cp=cp, hp=hp, ch=ch: e.matmul(
                            p4.t[0:64, cp * 64:(cp + 1) * 64], lhsT=AR.t[hp, cp, ch, 64:128], rhs=STb.t[hp, cp, :],
                            start=(cp == 0), stop=True, skip_group_check=True), r=[AR, STb], w=[p4], acc=(cp > 0))
                    y1s = y1s_r.get()
                    P.op("act", lambda e, p4=p4, y1s=y1s: e.activation(out=y1s.t[:, :], in_=p4.t[0:64, :], func=AF.Copy), r=[p4], w=[y1s])
                    p5 = psr7.get()
                    mrb = Ms[("rb", hd)]
                    mrk = Ms[("rk", hd)]
                    for cp in range(8):
                        c_ = slice(cp * 64, (cp + 1) * 64)
                        P.op("pe", lambda e, p5=p5, mrb=mrb, utm=utm, c_=c_, cp=cp: e.matmul(
                            p5.t[0:64, c_], lhsT=mrb.t[:, c_], rhs=utm.t[:, c_], start=(cp == 0), stop=False, skip_group_check=True),
                            r=[mrb, utm], w=[p5], acc=(cp > 0))
                        P.op("pe", lambda e, p5=p5, mrk=mrk, c_=c_, cp=cp, hcol=hcol, tmv=tm["v"]: e.matmul(
                            p5.t[0:64, c_], lhsT=mrk.t[:, c_], rhs=tmv.t[:, hcol(cp)], start=False, stop=True, skip_group_check=True),
                            r=[mrk, tm["v"]], w=[p5], acc=True)
                    yt = ytm_r.get()
                    P.op("dve", lambda e, p5=p5, y1s=y1s, yt=yt: e.tensor_tensor(out=yt.t[:, :], in0=p5.t[0:64, :], in1=y1s.t[:, :], op=ALU.add),
                         r=[p5, y1s], w=[yt])
                    ytm[hd] = yt
                    if DEBUG.get('rw_stop') == 'D3c':
                        continue
                    for cp in range(8):
                        c_ = slice(cp * 64, (cp + 1) * 64)
                        P.op("pe", lambda e, cp=cp, c_=c_, hp=hp, hcol=hcol, utm=utm, hd=hd, tmb_=tm["b"]: e.matmul(
                            pS.t[hp, c_], lhsT=tmb_.t[:, hcol(cp)], rhs=utm.t[:, c_], start=(cp == 0), stop=False, skip_group_check=True),
                            r=[tm["b"], utm], w=[pS], acc=not (hd == 0 and cp == 0))
                        P.op("pe", lambda e, cp=cp, c_=c_, hp=hp, hcol=hcol, tmk=tm["k"], tmv=tm["v"]: e.matmul(
                            pS.t[hp, c_], lhsT=tmk.t[:, hcol(cp)], rhs=tmv.t[:, hcol(cp)], start=False, stop=True, skip_group_check=True),
                            r=[tm["k"], tm["v"]], w=[pS], acc=True)
                if DEBUG.get('rw_stop') in ('D3a', 'D3b', 'D3c', 'D3d'):
                    continue
                P.op("dve", lambda e, ch=ch: e.tensor_tensor(out=S32.t[:, :, :], in0=S32.t[:, :, :],
                                                             in1=pCs.t[:, :, ch:ch + 1].to_broadcast([128, 8, 64]), op=ALU.mult),
                     r=[S32, pCs], w=[S32])
                P.op("dve", lambda e: e.tensor_tensor(out=S32.t[:, :, :], in0=S32.t[:, :, :],
                                                      in1=pS.t[:, :].rearrange("p (c i) -> p c i", c=8), op=ALU.add),
                     r=[S32, pS], w=[S32])
                P.op("act", lambda e: e.activation(out=STb.t[:, :, :], in_=S32.t[:, :, :], func=AF.Copy), r=[S32], w=[STb])
                if DEBUG.get('rw_stop') == 'D3':
                    continue
                pO = psr7.get()
                YH = yh_r.get()
                for hd in range(2):
                    yt = ytm[hd]
                    st = st_r.get()
                    P.op("dve", lambda e, yt=yt, st=st: e.tensor_reduce(out=st.t[:, 0:8], in_=y3(yt.t[:, :]), axis=AX.X, op=ALU.add),
                         r=[yt], w=[st])
                    ysq = ysq_r.get()
                    P.op("act", lambda e, yt=yt, ysq=ysq: e.activation(out=ysq.t[:, :], in_=yt.t[:, :], func=AF.Square), r=[yt], w=[ysq])
                    P.op("dve", lambda e, ysq=ysq, st=st: e.tensor_reduce(out=st.t[:, 8:16], in_=y3(ysq.t[:, :]), axis=AX.X, op=ALU.add),
                         r=[ysq, st], w=[st])
                    P.op("dve", lambda e, st=st: e.tensor_scalar(out=st.t[:, 16:24], in0=st.t[:, 0:8], scalar1=1.0 / 64, scalar2=None,
                                                                 op0=ALU.mult), r=[st], w=[st])
                    P.op("dve", lambda e, st=st: e.tensor_tensor(out=st.t[:, 24:32], in0=st.t[:, 16:24], in1=st.t[:, 16:24], op=ALU.mult),
                         r=[st], w=[st])
                    P.op("dve", lambda e, st=st: e.scalar_tensor_tensor(out=st.t[:, 32:40], in0=st.t[:, 8:16], scalar=1.0 / 64,
                                                                        in1=st.t[:, 24:32], op0=ALU.mult, op1=ALU.subtract),
                         r=[st], w=[st])
                    P.op("act", lambda e, st=st: e.activation(out=st.t[:, 40:48], in_=st.t[:, 32:40], func=AF.Sqrt, bias=gn_eps.t[:, 0:1]),
                         r=[st, gn_eps], w=[st])
                    P.op("dve", lambda e, st=st: e.reciprocal(out=st.t[:, 32:40], in_=st.t[:, 40:48]), r=[st], w=[st])
                    yc = ysq_r.get()
                    P.op("dve", lambda e, yt=yt, st=st, yc=yc: e.tensor_tensor(
                        out=y3(yc.t[:, :]), in0=y3(yt.t[:, :]), in1=st.t[:, 16:24].unsqueeze(2).to_broadcast([64, 8, 64]), op=ALU.subtract),
                        r=[yt, st], w=[yc])
                    P.op("dve", lambda e, yc=yc, st=st, hd=hd, YH=YH: e.tensor_tensor(
                        out=YH.t[:, :].rearrange("p (c h i) -> p c h i", c=8, h=2)[:, :, hd, :], in0=y3(yc.t[:, :]),
                        in1=st.t[:, 32:40].unsqueeze(2).to_broadcast([64, 8, 64]), op=ALU.mult),
                        r=[yc, st], w=[YH], acc=(hd > 0))
                if DEBUG.get('rw_stop') != 'E1':
                    for cp in range(8):
                        P.op("pe", lambda e, YH=YH, cp=cp, pO=pO: e.transpose(
                            out=bf(pO)[:, cp * 64:(cp + 1) * 64], in_=YH.t[:, cp * 128:(cp + 1) * 128], identity=identb.t[0:64, 0:64]),
                            r=[YH, identb], w=[pO], acc=(cp > 0))
                if DEBUG.get('rw_stop') in ('E1', 'E2a'):
                    continue
                P.op("act", lambda e, cs=cs, pO=pO: e.activation(func=AF.Copy, out=YT.t[:, :, cs], in_=bf(pO)[:, 0:512].rearrange("p (c t) -> p c t", c=8)),
                     r=[pO], w=[YT], acc=True)
            for cp in range(8 if DEBUG.get('rw_stop', 'full') == 'full' else 0):
                ta = T32("d")
                P.op("dve", lambda e, ta=ta, cp=cp: e.tensor_scalar(out=ta.t[:, :], in0=YT.t[:, cp, :], scalar1=rwp.t[:, cp, 8:9],
                                                                    scalar2=rwp.t[:, cp, 9:10], op0=ALU.mult, op1=ALU.add),
                     r=[YT, rwp], w=[ta])
                tb_ = T32("xsl")
                P.op("dve", lambda e, ta=ta, tb_=tb_, cp=cp: e.tensor_tensor(out=tb_.t[:, :], in0=ta.t[:, :], in1=BON.t[:, cp, :], op=ALU.add),
                     r=[ta, BON], w=[tb_])
                ost = ost_r.get()
                P.op("dve", lambda e, tb_=tb_, ost=ost, cp=cp: e.tensor_tensor(out=ost.t[:, :], in0=tb_.t[:, :], in1=GF.t[:, cp, :], op=ALU.mult),
                     r=[tb_, GF], w=[ost])
                P.op("sp", lambda e, ost=ost, cp=cp, tb=tb: e.dma_start(
                    out=oT.t[tb, 1024 + cp * 128:1024 + (cp + 1) * 128, :], in_=ost.t[:, :]), r=[ost], w=[oT], acc=True, dma=True)

    P.push_scope()
    phase_rwkv()
    P.barrier()
    P.pop_scope()

    if stop_after <= 3:
        P.emit()
        return nc, P, dbg

    NB4_ = SO_ // 512

    def own_off(pid, tb):
        v = pid * (NB4_ * 4096)
        if tb:
            v = v + tb * 4096
        return v

    og = P.dram("og", [NBLK * 4096, 512], BF16)
    RG = [[2 * i, 2 * i + 1] for i in range(ncore // 2)]
    if not DEBUG.get("no_cc"):
        for k in range(NBLK):
            P.op("pool", lambda e, k=k: e.collective_compute("AllGather", ALU.bypass, replica_groups=RG, ins=[oT.t[k]], outs=[og.t[k * 4096:(k + 1) * 4096, :]]),
                 r=[oT], w=[og], acc=True, cc=True)
    ogs = P.dram("ogs", [NB4_ * 4096, 512], BF16)
    if not DEBUG.get("no_cc"):
        for tb in range(NB4_):
            P.op("pool", lambda e, tb=tb: e.dma_start(
                out=ogs.t[tb * 4096:(tb + 1) * 4096, :], in_=og.t[bass.ds(own_off(P.get_pid(e), tb), 4096), :]),
                r=[og], w=[ogs], acc=True, dma=True)
    P.barrier()


    def phase4():
        NB4 = SO_ // 512 if DEBUG.get("p4_stop") != "xchg" else 0
        psr.i = 0
        pv = P.sb("pv_sb", [128, 80], F32)
        onesb = P.sb("onesb", [128, 128], BF16)
        P.op("sp", lambda e: e.dma_start(out=pv.t[:, :], in_=pv_in.t[:, :]), w=[pv], dma=True)
        P.op("pool", lambda e: e.memset(onesb.t[:, :], 1.0), w=[onesb])
        xT = P.sb("xT", [128, 16, 512], F32)
        wr = Ring(P, "w4", 2, [128, 8192], BF16)
        xt_r = Ring(P, "xt4", 2, [128, D], F32)
        sqb = P.sb("sqb", [128, 16, 512], BF16)
        N1 = P.sb("N1", [128, 16, 512], BF16)
        rs_r = Ring(P, "rs4", 2, [128, 512], F32)
        tmp_r = Ring(P, "tmp4", 2, [128, 512], F32)
        ss_r = Ring(P, "ss4", 4, [128, 2], F32)

        def wtile(src, c0, n, kcn=16):
            wt = wr.get()
            P.op("sp", lambda e, wt=wt: e.dma_start(
                out=wt.t[:, 0:kcn * n].rearrange("p (k c) -> p k c", k=kcn),
                in_=src.t[:, c0:c0 + n].rearrange("(k p) c -> p k c", p=128)), r=[src], w=[wt], dma=True)
            return wt

        def wv(wt, n, kcn=16):
            return wt.t[:, 0:kcn * n].rearrange("p (k c) -> p k c", k=kcn)

        def colsum_rstd(srcsq, eps_col):
            pb = psr.get()
            for kc in range(16):
                P.op("pe", lambda e, pb=pb, kc=kc: e.matmul(pb.t[:, 0:512], lhsT=onesb.t[:, :], rhs=srcsq.t[:, kc, :],
                                                            start=(kc == 0), stop=(kc == 15)), r=[onesb, srcsq], w=[pb], acc=(kc > 0))
            rs = rs_r.get()
            P.op("act", lambda e, pb=pb, rs=rs: e.activation(out=rs.t[:, :], in_=pb.t[:, :], func=AF.Sqrt, bias=epsb.t[:, 0:1],
                                                             scale=1.0 / D), r=[pb, epsb], w=[rs])
            P.op("dve", lambda e, rs=rs: e.reciprocal(out=rs.t[:, :], in_=rs.t[:, :]), r=[rs], w=[rs])
            return rs

        for tb in range(NB4):
            P.push_scope()
            hT = P.sb("hT4_%d" % tb, [128, 16, 512], BF16)
            G4_r = Ring(P, "G4_%d" % tb, 2, [128, 4, 512], BF16)
            oab = P.sb("oab_%d" % tb, [128, 16, 512], BF16)
            mixA = P.sb("mixA_%d" % tb, [128, 16, 512], BF16)
            mixb = P.sb("mixb_%d" % tb, [128, 16, 512], BF16)
            xn_r = Ring(P, "xn4_%d" % tb, 1, [128, D], BF16)
            for tt in range(4):
                t0 = tb * 512 + tt * 128
                xt = xt_r.get()
                P.op("sp", lambda e, xt=xt, t0=t0: e.dma_start(out=xt.t[:, :], in_=xo_in.t[t0:t0 + 128, :]), w=[xt], dma=True)
                ss = ss_r.get()
                P.op("pool", lambda e, ss=ss: e.memset(ss.t[:, :], 0.0), w=[ss])
                P.op("act", lambda e, xt=xt, ss=ss: e.activation(out=sqb.t[:, 0:4, :].rearrange("p a b -> p (a b)"), in_=xt.t[:, :],
                                                                 func=AF.Square, accum_out=ss.t[:, 0:1]), r=[xt], w=[sqb, ss])
                P.op("act", lambda e, ss=ss: e.activation(out=ss.t[:, 1:2], in_=ss.t[:, 0:1], func=AF.Sqrt,
                                                          bias=epsb.t[:, 0:1], scale=1.0 / D), r=[ss, epsb], w=[ss])
                P.op("dve", lambda e, ss=ss: e.reciprocal(out=ss.t[:, 0:1], in_=ss.t[:, 1:2]), r=[ss], w=[ss])
                xn = xn_r.get()
                P.op("dve", lambda e, xt=xt, ss=ss, xn=xn: e.scalar_tensor_tensor(
                    out=xn.t[:, :], in0=xt.t[:, :], scalar=ss.t[:, 0:1], in1=gpre_b.t[:, :], op0=ALU.mult, op1=ALU.mult),
                    r=[xt, ss, gpre_b], w=[xn])
                for half in range(2):
                    pb = psr.get()
                    for j in range(8):
                        kc = half * 8 + j
                        P.op("pe", lambda e, pb=pb, xn=xn, kc=kc, j=j: e.transpose(
                            out=bf(pb)[:, j * 128:(j + 1) * 128], in_=xn.t[:, kc * 128:(kc + 1) * 128], identity=identb.t[:, :]),
                            r=[xn, identb], w=[pb], acc=(j > 0))
                    P.op("act", lambda e, pb=pb, hT=hT, half=half, tt=tt: e.activation(
                        out=hT.t[:, half * 8:half * 8 + 8, tt * 128:(tt + 1) * 128],
                        in_=bf(pb).rearrange("p (j t) -> p j t", j=8), func=AF.Copy), r=[pb], w=[hT], acc=True)
                xh = xn_r.get()
                P.op("act", lambda e, xt=xt, xh=xh: e.activation(out=xh.t[:, :], in_=xt.t[:, :], func=AF.Copy), r=[xt], w=[xh])
                for half in range(2):
                    pb = psr.get()
                    for j in range(8):
                        kc = half * 8 + j
                        P.op("pe", lambda e, pb=pb, xh=xh, kc=kc, j=j: e.transpose(
                            out=bf(pb)[:, j * 128:(j + 1) * 128], in_=xh.t[:, kc * 128:(kc + 1) * 128], identity=identb.t[:, :]),
                            r=[xh, identb], w=[pb], acc=(j > 0))
                    P.op("act", lambda e, pb=pb, half=half, tt=tt: e.activation(
                        out=xT.t[:, half * 8:half * 8 + 8, tt * 128:(tt + 1) * 128],
                        in_=bf(pb).rearrange("p (j t) -> p j t", j=8), func=AF.Copy), r=[pb], w=[xT], acc=True)
            if DEBUG.get('p4_stop') == 'A1':
                P.barrier()
                P.pop_scope()
                continue
            for br, (wsrc, goff) in enumerate(((wab, 0), (wbb, 16))):
                for r_ in range(2):
                    row0 = tb * 4096 + r_ * 2048 + br * 1024
                    P.op("sp", lambda e, r_=r_, row0=row0, oab=oab: e.dma_start(
                        out=oab.t[:, r_ * 8:(r_ + 1) * 8, :],
                        in_=ogs.t[row0:row0 + 1024, :].rearrange("(k p) t -> p k t", p=128)),
                        r=[ogs], w=[oab], acc=(r_ > 0), dma=True)
                for q4 in range(4):
                    wgt = wtile(wgb, goff * 128 + q4 * 512, 512)
                    G4 = G4_r.get()
                    for c4 in range(4):
                        pb = psr.get()
                        for kc in range(16):
                            P.op("pe", lambda e, pb=pb, wgt=wgt, c4=c4, kc=kc, hT=hT: e.matmul(
                                pb.t[:, 0:512], lhsT=wv(wgt, 512)[:, kc, c4 * 128:(c4 + 1) * 128], rhs=hT.t[:, kc, :],
                                start=(kc == 0), stop=(kc == 15)), r=[wgt, hT], w=[pb], acc=(kc > 0))
                        gch = goff + q4 * 4 + c4
                        P.op("act", lambda e, pb=pb, G4=G4, c4=c4, gch=gch: e.activation(
                            out=G4.t[:, c4, :], in_=pb.t[:, :], func=AF.Sigmoid, bias=pv.t[:, gch:gch + 1]),
                            r=[pb, pv], w=[G4], acc=(c4 > 0))
                    wbt = wtile(wsrc, q4 * 512, 512)
                    for c4 in range(4):
                        cc = q4 * 4 + c4
                        pb = psr.get()
                        for kc in range(16):
                            P.op("pe", lambda e, pb=pb, wbt=wbt, c4=c4, kc=kc, oab=oab: e.matmul(
                                pb.t[:, 0:512], lhsT=wv(wbt, 512)[:, kc, c4 * 128:(c4 + 1) * 128], rhs=oab.t[:, kc, :],
                                start=(kc == 0), stop=(kc == 15)), r=[wbt, oab], w=[pb], acc=(kc > 0))
                        if br == 0:
                            P.op("dve", lambda e, pb=pb, G4=G4, c4=c4, cc=cc, mixA=mixA: e.tensor_tensor(
                                out=mixA.t[:, cc, :], in0=pb.t[:, :], in1=G4.t[:, c4, :], op=ALU.mult),
                                r=[pb, G4], w=[mixA], acc=True)
                        else:
                            tmp = tmp_r.get()
                            P.op("dve", lambda e, pb=pb, G4=G4, c4=c4, tmp=tmp: e.tensor_tensor(
                                out=tmp.t[:, :], in0=pb.t[:, :], in1=G4.t[:, c4, :], op=ALU.mult), r=[pb, G4], w=[tmp])
                            P.op("pool", lambda e, tmp=tmp, cc=cc, mixA=mixA, mixb=mixb: e.tensor_tensor(
                                out=mixb.t[:, cc, :], in0=tmp.t[:, :], in1=mixA.t[:, cc, :], op=ALU.add),
                                r=[tmp, mixA], w=[mixb], acc=True)
            if DEBUG.get('p4_stop') == 'A2':
                P.barrier()
                P.pop_scope()
                continue
            for q4 in range(4):
                wot = wtile(wob, q4 * 512, 512)
                for c4 in range(4):
                    cc = q4 * 4 + c4
                    pb = psr.get()
                    for kc in range(16):
                        P.op("pe", lambda e, pb=pb, wot=wot, c4=c4, kc=kc, mixb=mixb: e.matmul(
                            pb.t[:, 0:512], lhsT=wv(wot, 512)[:, kc, c4 * 128:(c4 + 1) * 128], rhs=mixb.t[:, kc, :],
                            start=(kc == 0), stop=(kc == 15)), r=[wot, mixb], w=[pb], acc=(kc > 0))
                    P.op("dve", lambda e, pb=pb, cc=cc, mixA=mixA: e.tensor_copy(out=mixA.t[:, cc, :], in_=pb.t[:, :]),
                         r=[pb], w=[mixA], acc=True)
                    P.op("act", lambda e, cc=cc, mixA=mixA: e.activation(out=sqb.t[:, cc, :], in_=mixA.t[:, cc, :], func=AF.Square),
                         r=[mixA], w=[sqb], acc=True)
            rs = colsum_rstd(sqb, 0)
            for cc in range(16):
                tmp = tmp_r.get()
                P.op("dve", lambda e, tmp=tmp, cc=cc, rs=rs, mixA=mixA: e.scalar_tensor_tensor(
                    out=tmp.t[:, :], in0=mixA.t[:, cc, :], scalar=pv.t[:, 32 + cc:33 + cc], in1=rs.t[:, :], op0=ALU.mult, op1=ALU.mult),
                    r=[mixA, pv, rs], w=[tmp])
                P.op("act", lambda e, tmp=tmp, cc=cc: e.activation(out=N1.t[:, cc, :], in_=tmp.t[:, :], func=AF.Copy), r=[tmp], w=[N1], acc=True)
                P.op("pool", lambda e, tmp=tmp, cc=cc: e.tensor_tensor(out=xT.t[:, cc, :], in0=tmp.t[:, :], in1=xT.t[:, cc, :], op=ALU.add),
                     r=[tmp, xT], w=[xT], acc=True)
            P.barrier()
            P.pop_scope()
            if DEBUG.get("p4_stop") == "A":
                continue
            P.push_scope()
            h2T = P.sb("h2T_%d" % tb, [128, 16, 512], BF16)
            FT = P.sb("FT_%d" % tb, [128, 44, 512], BF16)
            yo = P.sb("yo_%d" % tb, [128, 16, 512], BF16)
            for cc in range(16):
                P.op("act", lambda e, cc=cc: e.activation(out=sqb.t[:, cc, :], in_=xT.t[:, cc, :], func=AF.Square), r=[xT], w=[sqb], acc=True)
            rs = colsum_rstd(sqb, 0)
            for cc in range(16):
                P.op("dve", lambda e, cc=cc, rs=rs, h2T=h2T: e.scalar_tensor_tensor(
                    out=h2T.t[:, cc, :], in0=xT.t[:, cc, :], scalar=pv.t[:, 48 + cc:49 + cc], in1=rs.t[:, :], op0=ALU.mult, op1=ALU.mult),
                    r=[xT, pv, rs], w=[h2T], acc=True)
            for i in range(11):
                wgt = wtile(wf1b, i * 512, 512)
                wut = wtile(wf1b, 5632 + i * 512, 512)
                for c4 in range(4):
                    pg = psr.get()
                    for kc in range(16):
                        P.op("pe", lambda e, pg=pg, wgt=wgt, c4=c4, kc=kc, h2T=h2T: e.matmul(
                            pg.t[:, 0:512], lhsT=wv(wgt, 512)[:, kc, c4 * 128:(c4 + 1) * 128], rhs=h2T.t[:, kc, :],
                            start=(kc == 0), stop=(kc == 15)), r=[wgt, h2T], w=[pg], acc=(kc > 0))
                    pu = psr.get()
                    for kc in range(16):
                        P.op("pe", lambda e, pu=pu, wut=wut, c4=c4, kc=kc, h2T=h2T: e.matmul(
                            pu.t[:, 0:512], lhsT=wv(wut, 512)[:, kc, c4 * 128:(c4 + 1) * 128], rhs=h2T.t[:, kc, :],
                            start=(kc == 0), stop=(kc == 15)), r=[wut, h2T], w=[pu], acc=(kc > 0))
                    tmp = tmp_r.get()
                    P.op("act", lambda e, pg=pg, tmp=tmp: e.activation(out=tmp.t[:, :], in_=pg.t[:, :], func=AF.Silu), r=[pg], w=[tmp])
                    P.op("dve", lambda e, pu=pu, tmp=tmp, i=i, c4=c4, FT=FT: e.tensor_tensor(
                        out=FT.t[:, i * 4 + c4, :], in0=pu.t[:, :], in1=tmp.t[:, :], op=ALU.mult), r=[pu, tmp], w=[FT], acc=True)
            for cc in range(16):
                w2t = wtile(wf2b, cc * 128, 128, kcn=44)
                pb = psr.get()
                for kc in range(44):
                    P.op("pe", lambda e, pb=pb, w2t=w2t, kc=kc, FT=FT: e.matmul(
                        pb.t[:, 0:512], lhsT=wv(w2t, 128, 44)[:, kc, :], rhs=FT.t[:, kc, :],
                        start=(kc == 0), stop=(kc == 43)), r=[w2t, FT], w=[pb], acc=(kc > 0))
                P.op("dve", lambda e, pb=pb, cc=cc, yo=yo: e.tensor_copy(out=yo.t[:, cc, :], in_=pb.t[:, :]), r=[pb], w=[yo], acc=True)
                P.op("act", lambda e, cc=cc, yo=yo: e.activation(out=sqb.t[:, cc, :], in_=yo.t[:, cc, :], func=AF.Square),
                     r=[yo], w=[sqb], acc=True)
            rs = colsum_rstd(sqb, 0)
            for cc in range(16):
                tmp = tmp_r.get()
                P.op("dve", lambda e, tmp=tmp, cc=cc, rs=rs, yo=yo: e.scalar_tensor_tensor(
                    out=tmp.t[:, :], in0=yo.t[:, cc, :], scalar=pv.t[:, 64 + cc:65 + cc], in1=rs.t[:, :], op0=ALU.mult, op1=ALU.mult),
                    r=[yo, pv, rs], w=[tmp])
                P.op("pool", lambda e, tmp=tmp, cc=cc, yo=yo: e.tensor_tensor(out=yo.t[:, cc, :], in0=tmp.t[:, :], in1=N1.t[:, cc, :], op=ALU.add),
                     r=[tmp, N1], w=[yo], acc=True)
            for tt in range(4):
                r0 = tb * 512 + tt * 128
                xt = xt_r.get()
                P.op("sp", lambda e, xt=xt, r0=r0: e.dma_start(out=xt.t[:, :], in_=xo_in.t[r0:r0 + 128, :]), w=[xt], dma=True)
                for half in range(2):
                    pb = psr.get()
                    for j in range(8):
                        kc = half * 8 + j
                        P.op("pe", lambda e, pb=pb, kc=kc, j=j, tt=tt, yo=yo: e.transpose(
                            out=bf(pb)[:, j * 128:(j + 1) * 128], in_=yo.t[:, kc, tt * 128:(tt + 1) * 128], identity=identb.t[:, :]),
                            r=[yo, identb], w=[pb], acc=(j > 0))
                    P.op("dve", lambda e, pb=pb, xt=xt, half=half: e.tensor_tensor(
                        out=xt.t[:, half * 1024:(half + 1) * 1024], in0=bf(pb)[:, :], in1=xt.t[:, half * 1024:(half + 1) * 1024], op=ALU.add),
                        r=[pb, xt], w=[xt], acc=True)
                P.op("sp", lambda e, xt=xt, r0=r0: e.dma_start(out=out_t.t[r0:r0 + 128, :], in_=xt.t[:, :]), r=[xt], w=[out_t], acc=True, dma=True)
            P.barrier()
            P.pop_scope()

    P.push_scope()
    phase4()
    P.barrier()
    P.pop_scope()

    if stop_after <= 4:
        P.emit()
        return nc, P, dbg

    P.emit()
    return nc, P, dbg


def host_inputs(inputs, nsb=8, ncore=8):
    x = np.asarray(inputs["x"], np.float32)
    w_in = np.asarray(inputs["w_in"], np.float32)[0]
    half = 64
    inv = (10000.0 ** (-np.arange(half, dtype=np.float32) / half)).astype(np.float32)
    ang = np.arange(S, dtype=np.float32)[None, :] * inv[:, None]
    cosT = np.concatenate([np.cos(ang), np.cos(ang)], 0).astype(np.float32)
    sinT = np.concatenate([np.sin(ang), np.sin(ang)], 0).astype(np.float32)
    rmat = np.zeros((128, 128), np.float32)
    for dp in range(64):
        rmat[dp + 64, dp] = -1.0
        rmat[dp, dp + 64] = 1.0
    ident = np.eye(128, dtype=np.float32)
    tri = np.arange(64)
    mSU = np.tile((tri[:, None] < tri[None, :]).astype(np.float32), (1, 8))
    mIU = np.tile((tri[:, None] <= tri[None, :]).astype(np.float32), (1, 8))
    mSL = np.tile((tri[:, None] > tri[None, :]).astype(np.float32), (1, 8))
    I8c = np.tile(np.eye(64, dtype=np.float32), (1, 8))
    scanm = np.ones((128, 512), np.float32)
    scanm[:, 0::64] = 0.0
    bones = np.zeros((128, 128), np.float32)
    bones[0:64, 0:64] = 1.0
    bones[64:128, 64:128] = 1.0
    maps = []
    S_att = nsb * 1024
    SO_ = S_att // 2
    g1 = lambda k: np.asarray(inputs[k], np.float32)[0]
    pvec = np.concatenate([g1("b_gate").reshape(32, 128).T, g1("g_mix_post").reshape(16, 128).T,
                           g1("g_ffn_pre").reshape(16, 128).T, g1("g_ffn_post").reshape(16, 128).T], 1)
    pvec = np.ascontiguousarray(pvec, np.float32)
    wg = np.ascontiguousarray(w_in[:, 12736:])
    for c in range(ncore):
        b, hh = c // 2, c % 2
        qs = slice(hh * 1024, (hh + 1) * 1024)
        cols = np.concatenate([
            np.arange(0, 2048)[qs], np.arange(2048, 4096)[qs], np.arange(4096, 6144)[qs],
            6144 + np.arange(0, 2048)[qs], 6144 + 2048 + np.arange(0, 2048)[qs], 6144 + 4096 + np.arange(0, 2048)[qs],
            6144 + 6144 + np.arange(0, 448)])
        m = {
            "x": np.ascontiguousarray(x[b, 0:S_att]),
            "xo": np.ascontiguousarray(x[b, hh * SO_:(hh + 1) * SO_]),
            "wg": wg, "wa": g1("w_branch_a"), "wb": g1("w_branch_b"), "wo": g1("w_out"),
            "wf1": g1("w_ffn_in"), "wf2": g1("w_ffn_out"), "pvec": pvec,
            "w1": np.ascontiguousarray(w_in[:, cols]),
            "gpre": np.ascontiguousarray(np.asarray(inputs["g_mix_pre"], np.float32)[0][None, :]),
            "cosT": cosT, "sinT": sinT, "rmat": rmat, "ident": ident,
            "lqk": np.ascontiguousarray(np.concatenate([np.asarray(inputs["da_lambda_q"], np.float32)[0].reshape(-1),
                                                        np.asarray(inputs["da_lambda_k"], np.float32)[0].reshape(-1)])[None, :]),
            "gsub": np.ascontiguousarray(np.asarray(inputs["da_subln_g"], np.float32)[0][None, :]),
            "rwp": rw_params(inputs, hh), "mul": rw_mul(inputs),
            "wup": np.ascontiguousarray(np.asarray(inputs["rw_w_up"], np.float32)[0][:, qs]),
            "aup": np.ascontiguousarray(np.asarray(inputs["rw_a_up"], np.float32)[0][:, qs]),
            "gup": np.ascontiguousarray(np.asarray(inputs["rw_g_up"], np.float32)[0][:, qs]),
            "mSU": mSU, "mIU": mIU, "mSL": mSL, "I8": I8c, "scanm": scanm, "bones": bones,
        }
        maps.append(m)
    return maps


def rw_params(inputs, hh):
    g = lambda k: np.asarray(inputs[k], np.float32)[0].reshape(-1)
    mu = g("rw_mu")
    cols = hh * 1024 + np.arange(1024)
    vecs = [mu[cols], mu[2048 + cols], mu[4096 + cols], g("rw_w0")[cols], g("rw_a0")[cols], g("rw_k_k")[cols],
            g("rw_k_a")[cols], g("rw_r_k")[cols], g("rw_ln_w")[cols], g("rw_ln_b")[cols]]
    a = np.stack(vecs, -1).reshape(8, 128, 10).transpose(1, 0, 2)
    return np.ascontiguousarray(a)


def rw_mul(inputs):
    mu = np.asarray(inputs["rw_mu"], np.float32)[0]
    a = np.zeros((128, 4), np.float32)
    a[0:96, 0] = mu[6144:6240]
    a[0:96, 1] = mu[6240:6336]
    a[:, 2] = mu[6336:6464]
    a[:, 3] = mu[6464:6592]
    return a


def kernel(**inputs):
    nc, P, dbg = build(nsb=8, ncore=8)
    maps = host_inputs(inputs, nsb=8, ncore=8)
    res = run_bass_kernel_spmd(nc, maps, core_ids=list(range(8)))
    out = np.zeros((4, S, D), np.float32)
    for c in range(8):
        b, hh = c // 2, c % 2
        out[b, hh * SO:(hh + 1) * SO] = res.results[c]["out"]
    return out
```

```python
import math
from contextlib import ExitStack
import numpy as np
import ml_dtypes
import concourse.bass as bass
import concourse.mybir as mybir
from concourse.bass_utils import run_bass_kernel_spmd

F32 = mybir.dt.float32
BF16 = mybir.dt.bfloat16
AF = mybir.ActivationFunctionType
ALU = mybir.AluOpType
AX = mybir.AxisListType

S = 8192
D = 2048
SO = 4096
NS_DMA = 92
NS_SW = 76
ENGS = ("pe", "act", "dve", "pool", "sp")
SAME_ENG_SYNC = True
DEBUG = {}


class Buf:
    __slots__ = ("t", "w", "r")

    def __init__(self, t):
        self.t = t
        self.w = {}
        self.r = {}


class Prog:
    def __init__(self, nc):
        self.nc = nc
        self.ops = {e: [] for e in ENGS}
        self.known = {e: {} for e in ENGS}
        self.ndma = 0
        self.ndma_sw = 0
        self.dma_seq = [0] * NS_DMA
        self.stack = ExitStack()
        self.ncc = 0
        self.scopes = []

    def sb(self, name, shape, dt):
        st = self.scopes[-1] if self.scopes else self.stack
        return Buf(st.enter_context(self.nc.sbuf_tensor(name, list(shape), dt)))

    def push_scope(self):
        self.scopes.append(ExitStack())

    def pop_scope(self):
        self.scopes.pop().close()

    def ps(self, name):
        return Buf(self.stack.enter_context(self.nc.psum_tensor(name, [128, 512], F32)))

    def dram(self, name, shape, dt):
        return Buf(self.nc.dram_tensor(name, list(shape), dt))

    def get_pid(self, e):
        if getattr(self, "_pid", None) is None:
            self._pid = e.partition_id() % 2
        return self._pid

    def op(self, eng, fn, r=(), w=(), acc=False, dma=False, cc=False):
        deps = {}

        def add(d):
            for k, i in d.items():
                if deps.get(k, 0) < i:
                    deps[k] = i

        for b in r:
            add(b.w)
        for b in w:
            add(b.w)
            add(b.r)
        ops = self.ops[eng]
        my_idx = len(ops) + 1
        if dma:
            if eng == "pool":
                slot = self.ndma_sw % NS_SW
                self.ndma_sw += 1
            else:
                slot = NS_SW + self.ndma % (NS_DMA - NS_SW)
                self.ndma += 1
            seq = self.dma_seq[slot] + 1
            self.dma_seq[slot] = seq
            if seq > 1:
                add({("d", slot): seq - 1})
            ev = (("d", slot), seq)
        elif cc:
            self.ncc += 1
            ev = (("c", 0), self.ncc)
        else:
            ev = (eng, my_idx)
        kn = self.known[eng]
        waits = []
        for k, i in deps.items():
            if k == eng and (eng == "pe" or not SAME_ENG_SYNC):
                continue
            if kn.get(k, 0) >= i:
                continue
            kn[k] = i
            waits.append((k, i))
            if not isinstance(k, tuple):
                self.ops[k][i - 1][2] = True
        ops.append([waits, fn, False, ev if (dma or cc) else None, self._snap(fn)])
        for b in r:
            if b.r.get(ev[0], 0) < ev[1]:
                b.r[ev[0]] = ev[1]
        for b in w:
            if acc:
                b.w[ev[0]] = ev[1]
            else:
                b.w = {ev[0]: ev[1]}
            b.r = {}
        return ev

    @staticmethod
    def _snap(fn):
        out = []
        for c in (fn.__closure__ or ()):
            try:
                out.append(id(c.cell_contents))
            except ValueError:
                out.append(None)
        return out

    def barrier(self):
        latest = {}
        for e in ENGS:
            n = len(self.ops[e])
            while n > 0 and (self.ops[e][n - 1][1] is None or self.ops[e][n - 1][3] is not None):
                n -= 1
            if n > 0:
                latest[e] = n
        for s in range(NS_DMA):
            if self.dma_seq[s] > 0:
                latest[("d", s)] = self.dma_seq[s]
        if self.ncc:
            latest[("c", 0)] = self.ncc
        for e in ENGS:
            kn = self.known[e]
            waits = []
            for k, i in latest.items():
                if k == e:
                    continue
                if kn.get(k, 0) >= i:
                    continue
                kn[k] = i
                waits.append((k, i))
                if not isinstance(k, tuple):
                    self.ops[k][i - 1][2] = True
            self.ops[e].append([waits, None, False, None, None])

    def emit(self):
        nc = self.nc
        st = self.stack
        self.csem = st.enter_context(nc.semaphore("ccs"))
        self.sem = {e: st.enter_context(nc.semaphore("s_" + e)) for e in ENGS}
        self.dsem = [st.enter_context(nc.semaphore("d_%d" % i)) for i in range(NS_DMA)]
        self.val = {}
        for e in ENGS:
            c = 0
            v = []
            for o in self.ops[e]:
                if o[2]:
                    c += 1
                v.append(c)
            self.val[e] = v
        block = st.enter_context(nc.Block())

        @block.tensor
        def _(e):
            self._emit("pe", e)

        @block.scalar
        def _(e):
            self._emit("act", e)

        @block.vector
        def _(e):
            self._emit("dve", e)

        @block.gpsimd
        def _(e):
            self._emit("pool", e)

        @block.sync
        def _(e):
            self._emit("sp", e)

    def _emit(self, name, e):
        for waits, fn, flagged, ev, snap in self.ops[name]:
            if fn is not None and snap != self._snap(fn):
                names = fn.__code__.co_freevars
                bad = [n for n, a, b in zip(names, snap, self._snap(fn)) if a != b]
                raise RuntimeError("late-bound closure variable(s) %s in op at line %d" % (bad, fn.__code__.co_firstlineno))
            for k, i in waits:
                if isinstance(k, tuple):
                    if k[0] == "d":
                        e.wait_ge(self.dsem[k[1]], 16 * i)
                    else:
                        e.wait_ge(self.csem, i)
                else:
                    e.wait_ge(self.sem[k], self.val[k][i - 1])
            if fn is None:
                continue
            ins = fn(e)
            if ev is not None:
                if ev[0][0] == "d":
                    ins.then_inc(self.dsem[ev[0][1]], 16)
                else:
                    ins.then_inc(self.csem)
            elif flagged:
                ins.then_inc(self.sem[name], 1)


class PsumRing:
    def __init__(self, P, n=8, bufs=None):
        self.b = bufs if bufs is not None else [P.ps("psb%d" % i) for i in range(n)]
        self.i = 0

    def get(self):
        b = self.b[self.i % len(self.b)]
        self.i += 1
        return b


class Ring:
    def __init__(self, P, name, n, shape, dt):
        self.b = [P.sb("%s%d" % (name, i), shape, dt) for i in range(n)]
        self.i = 0

    def get(self):
        b = self.b[self.i % len(self.b)]
        self.i += 1
        return b


def bf(psbuf):
    return psbuf.t[:, :].bitcast(BF16)


def build(stop_after=99, nsb=8, ncore=8):
    nc = bass.Bass("TRN2", target_bir_lowering=False)
    P = Prog(nc)

    def din(name, shape, dt=F32):
        return Buf(nc.dram_tensor(name, list(shape), dt, kind="ExternalInput"))

    S_att = nsb * 1024
    SO_ = S_att // 2
    x_in = din("x", [S_att, D])
    xo_in = din("xo", [SO_, D])
    wg_in = din("wg", [D, 4096])
    wa_in = din("wa", [D, D])
    wb_in = din("wb", [D, D])
    wo_in = din("wo", [D, D])
    wf1_in = din("wf1", [D, 11264])
    wf2_in = din("wf2", [5632, D])
    pv_in = din("pvec", [128, 80])
    w1_in = din("w1", [D, 6592])
    gpre_in = din("gpre", [1, D])
    cos_in = din("cosT", [128, S])
    sin_in = din("sinT", [128, S])
    rmat_in = din("rmat", [128, 128])
    ident_in = din("ident", [128, 128])
    lqk_in = din("lqk", [1, 512])
    rwp_in = din("rwp", [128, 8, 10])
    mul_in = din("mul", [128, 4])
    wup_in = din("wup", [96, 1024])
    aup_in = din("aup", [96, 1024])
    gup_in = din("gup", [256, 1024])
    mSU_in = din("mSU", [64, 512])
    mIU_in = din("mIU", [64, 512])
    mSL_in = din("mSL", [64, 512])
    I8_in = din("I8", [64, 512])
    scanm_in = din("scanm", [128, 512])
    bones_in = din("bones", [128, 128])
    gsub_in = din("gsub", [1, 256])
    out_t = Buf(nc.dram_tensor("out", [SO_, D], F32, kind="ExternalOutput"))
    dbg = {}

    def dout(name, shape, dt):
        dbg[name] = Buf(nc.dram_tensor(name, list(shape), dt, kind="ExternalOutput"))
        return dbg[name]

    w1b = P.dram("w1b", [D, 6592], BF16)
    if DEBUG.get("p1"):
        qkT = dout("qkT", [16, 128, nsb * 1024], BF16)
        rwT = dout("rwT", [3520, nsb * 1024], BF16)
        vtm = dout("vtm", [nsb * 1024, 1024], BF16)
        hTd = dout("hTd", [128, 16, 512], BF16)
    else:
        qkT = P.dram("qkT", [16, 128, S], BF16)
        rwT = P.dram("rwT", [3520, S], BF16)
        vtm = P.dram("vtm", [S, 1024], BF16)

    identb = P.sb("identb", [128, 128], BF16)
    identf = P.sb("identf", [128, 128], F32)
    rmatb = P.sb("rmatb", [128, 128], BF16)
    gpre_b = P.sb("gpre_b", [128, D], F32)
    P.op("pool", lambda e: e.dma_start(out=identb.t[:, :], in_=ident_in.t[:, :]), w=[identb], dma=True)
    P.op("sp", lambda e: e.dma_start(out=identf.t[:, :], in_=ident_in.t[:, :]), w=[identf], dma=True)
    P.op("pool", lambda e: e.dma_start(out=rmatb.t[:, :], in_=rmat_in.t[:, :]), w=[rmatb], dma=True)
    P.op("sp", lambda e: e.dma_start(out=gpre_b.t[:, :], in_=gpre_in.t[0:1, :].partition_broadcast(128)),
         w=[gpre_b], dma=True)

    def convert(dst, src, rows, rstep=256):
        for r0 in range(0, rows, rstep):
            P.op("pool", lambda e, r0=r0: e.dma_start(out=dst.t[r0:r0 + rstep, :], in_=src.t[r0:r0 + rstep, :]),
                 r=[src], w=[dst], acc=True, dma=True)

    convert(w1b, w1_in, D)

    psr = PsumRing(P, 8)
    epsb = P.sb("epsb", [128, 4], F32)
    P.op("pool", lambda e: e.memset(epsb.t[:, 0:1], 1e-6), w=[epsb])
    P.op("pool", lambda e: e.memset(epsb.t[:, 1:2], 1e-5), w=[epsb], acc=True)

    def phase1():
        st2 = ExitStack()
        hg = Ring(P, "hg", 4, [128, 16, 512], BF16)
        xt_r = Ring(P, "xt", 3, [128, D], F32)
        xn_r = Ring(P, "xn", 2, [128, D], BF16)
        junk = P.sb("junk", [128, D], BF16)
        ss_r = Ring(P, "ss", 4, [128, 2], F32)
        wt_r = Ring(P, "wt", 3, [128, 16, 512], BF16)
        qb_r = Ring(P, "qb", 3, [128, 512], BF16)
        t1_r = Ring(P, "t1", 3, [128, 512], F32)
        t2_r = Ring(P, "t2", 3, [128, 512], F32)
        ob_r = Ring(P, "ob", 4, [128, 512], BF16)
        cs_r = Ring(P, "cs", 2, [128, 512], F32)
        sn_r = Ring(P, "sn", 2, [128, 512], F32)

        def build_group(G, hbuf):
            for tt in range(4):
                t0 = G * 512 + tt * 128
                xt = xt_r.get()
                P.op("sp", lambda e, xt=xt, t0=t0: e.dma_start(out=xt.t[:, :], in_=x_in.t[t0:t0 + 128, :]),
                     w=[xt], dma=True)
                ss = ss_r.get()
                P.op("pool", lambda e, ss=ss: e.memset(ss.t[:, :], 0.0), w=[ss])
                P.op("act", lambda e, xt=xt, ss=ss: e.activation(out=junk.t[:, :], in_=xt.t[:, :], func=AF.Square,
                                                                 accum_out=ss.t[:, 0:1]),
                     r=[xt], w=[junk, ss])
                P.op("act", lambda e, ss=ss: e.activation(out=ss.t[:, 1:2], in_=ss.t[:, 0:1], func=AF.Sqrt,
                                                          bias=epsb.t[:, 0:1], scale=1.0 / D), r=[ss, epsb], w=[ss])
                P.op("dve", lambda e, ss=ss: e.reciprocal(out=ss.t[:, 0:1], in_=ss.t[:, 1:2]), r=[ss], w=[ss])
                xn = xn_r.get()
                P.op("dve", lambda e, xt=xt, ss=ss, xn=xn: e.scalar_tensor_tensor(
                    out=xn.t[:, :], in0=xt.t[:, :], scalar=ss.t[:, 0:1], in1=gpre_b.t[:, :],
                    op0=ALU.mult, op1=ALU.mult), r=[xt, ss, gpre_b], w=[xn])
                for half in range(2):
                    pb = psr.get()
                    for j in range(8):
                        kc = half * 8 + j
                        P.op("pe", lambda e, pb=pb, xn=xn, kc=kc, j=j: e.transpose(
                            out=bf(pb)[:, j * 128:(j + 1) * 128], in_=xn.t[:, kc * 128:(kc + 1) * 128],
                            identity=identb.t[:, :]), r=[xn, identb], w=[pb], acc=(j > 0))
                    P.op("act", lambda e, pb=pb, hbuf=hbuf, half=half, tt=tt: e.activation(
                        out=hbuf.t[:, half * 8:half * 8 + 8, tt * 128:(tt + 1) * 128],
                        in_=bf(pb).rearrange("p (j t) -> p j t", j=8), func=AF.Copy),
                        r=[pb], w=[hbuf], acc=True)

        blocks = []
        for c0 in range(0, 6592, 512):
            n = min(512, 6592 - c0)
            blocks.append((c0, n))

        for sbk in range(nsb):
            hb = [hg.get(), hg.get()]
            for g in range(2):
                build_group(sbk * 2 + g, hb[g])
            if DEBUG.get("p1") and sbk == 0:
                P.op("sp", lambda e, hb0=hb[0]: e.dma_start(out=dbg["hTd"].t[:, :, :], in_=hb0.t[:, :, :]), r=[hb[0]], w=[dbg["hTd"]], dma=True)
            for (c0, n) in blocks:
                wt = wt_r.get()
                P.op("sp", lambda e, wt=wt, c0=c0, n=n: e.dma_start(
                    out=wt.t[:, :, 0:n], in_=w1b.t[:, c0:c0 + n].rearrange("(kc p) c -> p kc c", p=128)),
                    r=[w1b], w=[wt], dma=True)
                if 2048 <= c0 < 3072:
                    for g in range(2):
                        for tt in range(4):
                            t0 = (sbk * 2 + g) * 512 + tt * 128
                            pb = psr.get()
                            for kc in range(16):
                                P.op("pe", lambda e, pb=pb, hbg=hb[g], tt=tt, kc=kc, wt=wt: e.matmul(
                                    pb.t[:, 0:512], lhsT=hbg.t[:, kc, tt * 128:(tt + 1) * 128], rhs=wt.t[:, kc, 0:512],
                                    start=(kc == 0), stop=(kc == 15)), r=[hb[g], wt], w=[pb], acc=(kc > 0))
                            ob = ob_r.get()
                            P.op("act", lambda e, pb=pb, ob=ob: e.activation(out=ob.t[:, :], in_=pb.t[:, :], func=AF.Copy),
                                 r=[pb], w=[ob])
                            P.op("pool", lambda e, ob=ob, t0=t0, c0=c0: e.dma_start(
                                out=vtm.t[t0:t0 + 128, c0 - 2048:c0 - 2048 + 512], in_=ob.t[:, :]),
                                r=[ob], w=[vtm], acc=True, dma=True)
                    continue
                for g in range(2):
                    tk0 = (sbk * 2 + g) * 512
                    is_qk = c0 < 2048
                    if is_qk:
                        cs = cs_r.get()
                        sn = sn_r.get()
                        P.op("sp", lambda e, cs=cs, tk0=tk0: e.dma_start(out=cs.t[:, :], in_=cos_in.t[:, tk0:tk0 + 512]),
                             w=[cs], dma=True)
                        P.op("sp", lambda e, sn=sn, tk0=tk0: e.dma_start(out=sn.t[:, :], in_=sin_in.t[:, tk0:tk0 + 512]),
                             w=[sn], dma=True)
                    for cc in range((n + 127) // 128):
                        m = min(128, n - cc * 128)
                        col = c0 + cc * 128
                        pb = psr.get()
                        for kc in range(16):
                            P.op("pe", lambda e, pb=pb, hbg=hb[g], cc=cc, kc=kc, wt=wt, m=m: e.matmul(
                                pb.t[0:m, 0:512], lhsT=wt.t[:, kc, cc * 128:cc * 128 + m], rhs=hbg.t[:, kc, :],
                                start=(kc == 0), stop=(kc == 15)), r=[hb[g], wt], w=[pb], acc=(kc > 0))
                        if is_qk:
                            qb = qb_r.get()
                            P.op("act", lambda e, pb=pb, qb=qb: e.activation(out=qb.t[:, :], in_=pb.t[:, :], func=AF.Copy),
                                 r=[pb], w=[qb])
                            pr = psr.get()
                            P.op("pe", lambda e, pr=pr, qb=qb: e.matmul(pr.t[:, 0:512], lhsT=rmatb.t[:, :], rhs=qb.t[:, :],
                                                                         start=True, stop=True), r=[qb, rmatb], w=[pr])
                            t1 = t1_r.get()
                            t2 = t2_r.get()
                            P.op("pool", lambda e, t1=t1, qb=qb, cs=cs: e.tensor_tensor(
                                out=t1.t[:, :], in0=qb.t[:, :], in1=cs.t[:, :], op=ALU.mult), r=[qb, cs], w=[t1])
                            P.op("dve", lambda e, t2=t2, pr=pr, sn=sn: e.tensor_tensor(
                                out=t2.t[:, :], in0=pr.t[:, :], in1=sn.t[:, :], op=ALU.mult), r=[pr, sn], w=[t2])
                            ob = ob_r.get()
                            P.op("dve", lambda e, t1=t1, t2=t2, ob=ob: e.tensor_tensor(
                                out=ob.t[:, :], in0=t1.t[:, :], in1=t2.t[:, :], op=ALU.add), r=[t1, t2], w=[ob])
                            ch = col // 128
                            P.op("pool", lambda e, ob=ob, ch=ch, tk0=tk0: e.dma_start(
                                out=qkT.t[ch, :, tk0:tk0 + 512], in_=ob.t[:, :]), r=[ob], w=[qkT], acc=True, dma=True)
                        else:
                            ob = ob_r.get()
                            P.op("act", lambda e, pb=pb, ob=ob, m=m: e.activation(out=ob.t[0:m, :], in_=pb.t[0:m, :], func=AF.Copy),
                                 r=[pb], w=[ob])
                            row = col - 3072
                            P.op("pool", lambda e, ob=ob, row=row, m=m, tk0=tk0: e.dma_start(
                                out=rwT.t[row:row + m, tk0:tk0 + 512], in_=ob.t[0:m, :]), r=[ob], w=[rwT], acc=True, dma=True)

    P.push_scope()
    phase1()
    P.barrier()
    P.pop_scope()
    wgb = P.dram("wgb", [D, 4096], BF16)
    wab = P.dram("wab", [D, D], BF16)
    wbb = P.dram("wbb", [D, D], BF16)
    wob = P.dram("wob", [D, D], BF16)
    wf1b = P.dram("wf1b", [D, 11264], BF16)
    wf2b = P.dram("wf2b", [5632, D], BF16)
    if not DEBUG.get("no_conv"):
        convert(wgb, wg_in, D)
    convert(wab, wa_in, D)
    convert(wbb, wb_in, D)
    convert(wob, wo_in, D)
    convert(wf1b, wf1_in, D)
    convert(wf2b, wf2_in, 5632)
    NBLK = S_att // 512
    if DEBUG.get("p2"):
        oT = dout("oT", [NBLK, 2048, 512], BF16)
    else:
        oT = P.dram("oT", [NBLK, 2048, 512], BF16)

    def phase_attn():
        LAMBDA_INIT = 0.8 - 0.6 * math.exp(-0.3 * 0)
        NG = S_att // 512
        NKB = S_att // 128
        kT = [P.sb("kT%d" % c, [128, S_att], BF16) for c in range(2)]
        v1 = P.sb("v1", [128, NKB, 257], BF16)
        qg_r = Ring(P, "qg", 3, [128, 512], BF16)
        pT_r = Ring(P, "pT", 4, [128, 512], BF16)
        o_c = [P.sb("o_c%d" % c, [128, 4, 257], F32) for c in range(2)]
        lqk = P.sb("lqk_sb", [128, 512], F32)
        lam = P.sb("lam_sb", [128, 8], F32)
        gsb = P.sb("gsb", [128, 256], F32)
        sm = Ring(P, "sm", 4, [128, 8], F32)
        ta_r = Ring(P, "ta", 2, [128, 256], F32)
        to_r = Ring(P, "to", 2, [128, 256], F32)
        jk = P.sb("jk2", [128, 256], BF16)
        obf_r = Ring(P, "obf", 2, [128, 256], BF16)
        ost_r = Ring(P, "ost", 2, [128, 2, 512], BF16)
        acc = psr.b[0:4]
        st_b = psr.b[4:7]
        misc = psr.b[7]
        sti = [0]

        P.op("sp", lambda e: e.dma_start(out=lqk.t[:, :], in_=lqk_in.t[0:1, :].partition_broadcast(128)), w=[lqk], dma=True)
        P.op("sp", lambda e: e.dma_start(out=gsb.t[:, :], in_=gsub_in.t[0:1, :].partition_broadcast(128)), w=[gsb], dma=True)
        P.op("dve", lambda e: e.tensor_tensor(out=lqk.t[:, 0:256], in0=lqk.t[:, 0:256], in1=lqk.t[:, 256:512], op=ALU.mult),
             r=[lqk], w=[lqk])
        P.op("dve", lambda e: e.tensor_reduce(out=lam.t[:, 0:2], in_=lqk.t[:, 0:256].rearrange("p (a b) -> p a b", a=2),
                                              axis=AX.X, op=ALU.add), r=[lqk], w=[lam])
        P.op("act", lambda e: e.activation(out=lam.t[:, 2:4], in_=lam.t[:, 0:2], func=AF.Exp), r=[lam], w=[lam])
        P.op("dve", lambda e: e.scalar_tensor_tensor(out=lam.t[:, 4:5], in0=lam.t[:, 3:4], scalar=-LAMBDA_INIT, in1=lam.t[:, 2:3],
                                                     op0=ALU.add, op1=ALU.subtract), r=[lam], w=[lam])
        P.op("dve", lambda e: e.tensor_scalar(out=gsb.t[:, :], in0=gsb.t[:, :], scalar1=1.0 - LAMBDA_INIT, scalar2=None,
                                              op0=ALU.mult), r=[gsb], w=[gsb])
        for hd in range(4):
            for c in range(2):
                P.op("sp", lambda e, c=c, hd=hd: e.dma_start(out=kT[c].t[:, :], in_=qkT.t[8 + hd * 2 + c, :, 0:S_att]),
                     r=[qkT], w=[kT[c]], dma=True)
            P.op("sp", lambda e, hd=hd: e.dma_start(
                out=v1.t[:, :, 0:256], in_=vtm.t[0:S_att, hd * 256:(hd + 1) * 256].rearrange("(kb p) e -> p kb e", p=128)),
                r=[vtm], w=[v1], dma=True)
            P.op("dve", lambda e: e.memset(v1.t[:, :, 256:257], 1.0), w=[v1], acc=True)
            for j in range(NG):
                for c in range(2):
                    qg = qg_r.get()
                    P.op("sp", lambda e, qg=qg, c=c, hd=hd, j=j: e.dma_start(
                        out=qg.t[:, :], in_=qkT.t[hd * 2 + c, :, j * 512:(j + 1) * 512]), r=[qkT], w=[qg], dma=True)
                    nkb = 4 * j + 4
                    issued = []

                    def emit_st(kb, qg=qg, c=c, j=j):
                        m = max(kb - 4 * j, 0)
                        c0 = m * 128
                        sps = st_b[sti[0] % 3]
                        sti[0] += 1
                        P.op("pe", lambda e, sps=sps, c=c, kb=kb, qg=qg, c0=c0: e.matmul(
                            sps.t[:, c0:512], lhsT=kT[c].t[:, kb * 128:(kb + 1) * 128], rhs=qg.t[:, c0:512],
                            start=True, stop=True), r=[kT[c], qg], w=[sps])
                        issued.append((sps, m, c0))

                    for kb in range(nkb):
                        while len(issued) < min(kb + 3, nkb):
                            emit_st(len(issued))
                        sps, m, c0 = issued[kb]
                        pT = pT_r.get()
                        P.op("act", lambda e, sps=sps, pT=pT, c0=c0: e.activation(
                            out=pT.t[:, c0:512], in_=sps.t[:, c0:512], func=AF.Exp, scale=128 ** -0.5), r=[sps], w=[pT])
                        if kb >= 4 * j:
                            P.op("dve", lambda e, pT=pT, c0=c0: e.memset(pT.t[64:128, c0:c0 + 64], 0.0), w=[pT], acc=True)
                        for qs in range(m, 4):
                            P.op("pe", lambda e, qs=qs, pT=pT, kb=kb, j=j: e.matmul(
                                acc[qs].t[:, 0:257], lhsT=pT.t[:, qs * 128:(qs + 1) * 128], rhs=v1.t[:, kb, :],
                                start=(kb == 0), stop=(kb == 4 * j + qs)), r=[pT, v1], w=[acc[qs]], acc=(kb > 0))
                    for qs in range(4):
                        P.op("act", lambda e, qs=qs, c=c: e.activation(out=o_c[c].t[:, qs, :], in_=acc[qs].t[:, 0:257], func=AF.Copy),
                             r=[acc[qs]], w=[o_c[c]], acc=True)
                ost = ost_r.get()
                for qs in range(4):
                    s_ = sm.get()
                    P.op("dve", lambda e, s_=s_, qs=qs: e.reciprocal(out=s_.t[:, 0:1], in_=o_c[0].t[:, qs, 256:257]), r=[o_c[0]], w=[s_])
                    P.op("dve", lambda e, s_=s_, qs=qs: e.reciprocal(out=s_.t[:, 1:2], in_=o_c[1].t[:, qs, 256:257]), r=[o_c[1]], w=[s_])
                    P.op("dve", lambda e, s_=s_: e.tensor_tensor(out=s_.t[:, 2:3], in0=s_.t[:, 1:2], in1=lam.t[:, 4:5], op=ALU.mult),
                         r=[s_, lam], w=[s_])
                    ta = ta_r.get()
                    P.op("dve", lambda e, s_=s_, ta=ta, qs=qs: e.tensor_scalar(out=ta.t[:, :], in0=o_c[0].t[:, qs, 0:256],
                                                                         scalar1=s_.t[:, 0:1], scalar2=None, op0=ALU.mult),
                         r=[s_, o_c[0]], w=[ta])
                    to = to_r.get()
                    P.op("dve", lambda e, s_=s_, ta=ta, to=to, qs=qs: e.scalar_tensor_tensor(
                        out=to.t[:, :], in0=o_c[1].t[:, qs, 0:256], scalar=s_.t[:, 2:3], in1=ta.t[:, :],
                        op0=ALU.mult, op1=ALU.add), r=[s_, o_c[1], ta], w=[to])
                    P.op("dve", lambda e, s_=s_: e.memset(s_.t[:, 3:4], 0.0), w=[s_], acc=True)
                    P.op("act", lambda e, s_=s_, to=to: e.activation(out=jk.t[:, :], in_=to.t[:, :], func=AF.Square,
                                                                     accum_out=s_.t[:, 3:4]), r=[to, s_], w=[jk, s_])
                    P.op("act", lambda e, s_=s_: e.activation(out=s_.t[:, 4:5], in_=s_.t[:, 3:4], func=AF.Sqrt,
                                                              bias=epsb.t[:, 1:2], scale=1.0 / 256), r=[s_, epsb], w=[s_])
                    P.op("dve", lambda e, s_=s_: e.reciprocal(out=s_.t[:, 5:6], in_=s_.t[:, 4:5]), r=[s_], w=[s_])
                    obf = obf_r.get()
                    P.op("dve", lambda e, s_=s_, to=to, obf=obf: e.scalar_tensor_tensor(
                        out=obf.t[:, :], in0=to.t[:, :], scalar=s_.t[:, 5:6], in1=gsb.t[:, :], op0=ALU.mult, op1=ALU.mult),
                        r=[to, s_, gsb], w=[obf])
                    for ec in range(2):
                        P.op("pe", lambda e, obf=obf, ec=ec: e.transpose(
                            out=bf(misc)[:, ec * 128:(ec + 1) * 128], in_=obf.t[:, ec * 128:(ec + 1) * 128], identity=identb.t[:, :]),
                            r=[obf, identb], w=[misc], acc=(ec > 0))
                    P.op("act", lambda e, ost=ost, qs=qs: e.activation(
                        out=ost.t[:, :, qs * 128:(qs + 1) * 128], in_=bf(misc)[:, 0:256].rearrange("p (a t) -> p a t", a=2),
                        func=AF.Copy), r=[misc], w=[ost], acc=True)
                for ec in range(2):
                    P.op("sp", lambda e, ost=ost, ec=ec, hd=hd, j=j: e.dma_start(
                        out=oT.t[j, hd * 256 + ec * 128:hd * 256 + (ec + 1) * 128, :], in_=ost.t[:, ec, :]),
                        r=[ost], w=[oT], acc=True, dma=True)

    P.push_scope()
    phase_attn()
    P.barrier()
    P.pop_scope()

    if stop_after <= 2:
        P.emit()
        return nc, P, dbg

    def phase_rwkv():
        v3 = lambda ap: ap.rearrange("p (c t) -> p c t", c=8)
        y3 = lambda ap: ap.rearrange("p (c i) -> p c i", c=8)
        psr_all = psr
        psr7 = PsumRing(P, bufs=psr_all.b[0:7])
        pS = psr_all.b[7]
        NB = S_att // 512
        C0 = math.exp(-0.5)
        rwp = P.sb("rwp_sb", [128, 8, 10], F32)
        mul = P.sb("mul_sb", [128, 4], F32)
        wupb = P.sb("wupb", [96, 1024], BF16)
        aupb = P.sb("aupb", [96, 1024], BF16)
        gupb = P.sb("gupb", [128, 2, 1024], BF16)
        mSU = P.sb("mSU_sb", [64, 512], F32)
        mIU = P.sb("mIU_sb", [64, 512], F32)
        mSL = P.sb("mSL_sb", [64, 512], F32)
        I8 = P.sb("I8_sb", [64, 512], BF16)
        scanm = P.sb("scanm_sb", [128, 512], F32)
        bones = P.sb("bones_sb", [128, 128], BF16)
        gn_eps = P.sb("gn_eps", [64, 1], F32)
        for (dst, src, q) in ((rwp, rwp_in, "sp"), (mul, mul_in, "sp"), (mSU, mSU_in, "sp"), (mIU, mIU_in, "sp"),
                              (mSL, mSL_in, "sp"), (scanm, scanm_in, "sp"), (wupb, wup_in, "pool"), (aupb, aup_in, "pool"),
                              (I8, I8_in, "pool"), (bones, bones_in, "pool")):
            nd = len(dst.t.shape)
            if nd == 3:
                P.op(q, lambda e, dst=dst, src=src: e.dma_start(out=dst.t[:, :, :], in_=src.t[:, :, :]), w=[dst], dma=True)
            else:
                P.op(q, lambda e, dst=dst, src=src: e.dma_start(out=dst.t[:, :], in_=src.t[:, :]), w=[dst], dma=True)
        P.op("pool", lambda e: e.dma_start(out=gupb.t[:, :, :], in_=gup_in.t[:, :].rearrange("(k p) c -> p k c", p=128)),
             w=[gupb], dma=True)
        P.op("pool", lambda e: e.memset(gn_eps.t[:, :], 64e-5), w=[gn_eps])

        S32 = P.sb("S32", [128, 8, 64], F32)
        STb = P.sb("STb", [128, 8, 64], BF16)
        P.op("pool", lambda e: e.memset(S32.t[:, :, :], 0.0), w=[S32])
        P.op("pool", lambda e: e.memset(STb.t[:, :, :], 0.0), w=[STb])

        AR = P.sb("AR", [128, 8, 8, 128], BF16)
        Kt = P.sb("Kt", [128, 8, 512], BF16)
        Bt = P.sb("Bt", [128, 8, 512], BF16)
        VF = P.sb("VF", [128, 8, 512], BF16)
        KH = P.sb("KH", [128, 8, 512], BF16)
        BH = P.sb("BH", [128, 8, 512], BF16)
        GF = P.sb("GF", [128, 8, 512], BF16)
        BON = P.sb("BON", [128, 8, 512], BF16)
        YT = P.sb("YT", [128, 8, 512], BF16)
        pCs = P.sb("pCs", [128, 8, 8], F32)
        Lw = P.sb("Lw", [128, 513], BF16)
        La = P.sb("La", [128, 513], BF16)
        Lg = P.sb("Lg", [128, 2, 513], BF16)
        tanw = P.sb("tanw", [128, 512], BF16)
        xsal = P.sb("xsal", [128, 512], BF16)
        sigg = P.sb("sigg", [128, 2, 512], BF16)
        f32r = {}

        def T32(name, n=1):
            if name not in f32r:
                f32r[name] = Ring(P, "rw_" + name, n, [128, 512], F32)
            return f32r[name].get()

        Lr_r = Ring(P, "Lr", 1, [128, 513], BF16)
        Lk_r = Ring(P, "Lk", 1, [128, 513], BF16)
        Lv_r = Ring(P, "Lv", 1, [128, 513], BF16)
        sq_r = Ring(P, "sqr", 2, [128, 512], BF16)
        ost_r = Ring(P, "rwost", 2, [128, 512], BF16)
        tm_r = {k: Ring(P, "tm" + k, 2, [64, 1024], BF16) for k in ("v", "k", "b")}
        m_r = {k: Ring(P, "m" + k, 2, [64, 512], BF16) for k in ("ak", "ab", "rk", "rb", "mt")}
        x_r = [Ring(P, "xr%d" % h, 2, [64, 512], BF16) for h in range(2)]
        y_r = [Ring(P, "yr%d" % h, 2, [64, 512], BF16) for h in range(2)]
        t_r = [Ring(P, "tr%d" % h, 2, [64, 512], BF16) for h in range(2)]
        tt_r = [Ring(P, "ttr%d" % h, 2, [64, 512], BF16) for h in range(2)]
        w1s_r = Ring(P, "w1s", 1, [64, 512], F32)
        wtm_r = Ring(P, "wtm", 2, [64, 512], BF16)
        utm_r = Ring(P, "utm", 2, [64, 512], BF16)
        y1s_r = Ring(P, "y1s", 1, [64, 512], F32)
        ytm_r = Ring(P, "ytm", 2, [64, 512], F32)
        ysq_r = Ring(P, "ysq", 2, [64, 512], F32)
        yh_r = Ring(P, "yh", 2, [64, 1024], BF16)
        st_r = Ring(P, "gnst", 4, [64, 48], F32)

        def shift_mix(L, n, mu_ap, parts, out_fn):
            d = T32("d")
            P.op("dve", lambda e: e.tensor_tensor(out=d.t[0:parts, :], in0=L[0], in1=L[1], op=ALU.subtract), r=[L[2]], w=[d])
            xs = T32("xsl")
            P.op("dve", lambda e: e.scalar_tensor_tensor(out=xs.t[0:parts, :], in0=d.t[0:parts, :], scalar=mu_ap, in1=L[1],
                                                         op0=ALU.mult, op1=ALU.add), r=[d, L[2], mul, rwp], w=[xs])
            return xs

        def load_shift(q, Lb, rows, row0, t0, view3=None):
            if t0 == 0:
                P.op("pool", lambda e: e.memset(Lb.t[:, 0:1] if view3 is None else Lb.t[:, :, 0:1], 0.0), w=[Lb])
                if view3 is None:
                    P.op(q, lambda e: e.dma_start(out=Lb.t[0:rows, 1:513], in_=rwT.t[row0:row0 + rows, 0:512]),
                         r=[rwT], w=[Lb], acc=True, dma=True)
                else:
                    for k in range(2):
                        P.op(q, lambda e, k=k: e.dma_start(out=Lb.t[:, k, 1:513], in_=rwT.t[row0 + k * 128:row0 + (k + 1) * 128, 0:512]),
                             r=[rwT], w=[Lb], acc=True, dma=True)
            else:
                if view3 is None:
                    P.op(q, lambda e: e.dma_start(out=Lb.t[0:rows, 0:513], in_=rwT.t[row0:row0 + rows, t0 - 1:t0 + 512]),
                         r=[rwT], w=[Lb], dma=True)
                else:
                    for k in range(2):
                        P.op(q, lambda e, k=k: e.dma_start(out=Lb.t[:, k, 0:513],
                                                            in_=rwT.t[row0 + k * 128:row0 + (k + 1) * 128, t0 - 1:t0 + 512]),
                             r=[rwT], w=[Lb], acc=(k > 0), dma=True)

        for tb in range(NB):
            t0 = tb * 512
            load_shift("sp", Lw, 96, 6144 - 3072, t0)
            load_shift("sp", La, 96, 6240 - 3072, t0)
            load_shift("sp", Lg, 128, 6336 - 3072, t0, view3=True)
            xs = shift_mix((Lw.t[0:96, 0:512], Lw.t[0:96, 1:513], Lw), 512, mul.t[0:96, 0:1], 96, None)
            P.op("act", lambda e, xs=xs: e.activation(out=tanw.t[0:96, :], in_=xs.t[0:96, :], func=AF.Tanh), r=[xs], w=[tanw])
            xs = shift_mix((La.t[0:96, 0:512], La.t[0:96, 1:513], La), 512, mul.t[0:96, 1:2], 96, None)
            P.op("act", lambda e, xs=xs: e.activation(out=xsal.t[0:96, :], in_=xs.t[0:96, :], func=AF.Copy), r=[xs], w=[xsal])
            for k in range(2):
                xs = shift_mix((Lg.t[:, k, 0:512], Lg.t[:, k, 1:513], Lg), 512, mul.t[:, 2 + k:3 + k], 128, None)
                P.op("act", lambda e, xs=xs, k=k: e.activation(out=sigg.t[:, k, :], in_=xs.t[:, :], func=AF.Sigmoid),
                     r=[xs], w=[sigg], acc=(k > 0))
            for cp in range(8):
                Ls = []
                for i, Rg in enumerate((Lr_r, Lk_r, Lv_r)):
                    Lb = Rg.get()
                    load_shift("sp", Lb, 128, i * 1024 + cp * 128, t0)
                    Ls.append(Lb)
                xs3 = []
                for i, nm in enumerate(("xr", "xk", "xv")):
                    Lb = Ls[i]
                    d = T32("d")
                    P.op("dve", lambda e, d=d, Lb=Lb: e.tensor_tensor(out=d.t[:, :], in0=Lb.t[:, 0:512], in1=Lb.t[:, 1:513],
                                                                      op=ALU.subtract), r=[Lb], w=[d])
                    x_ = T32(nm)
                    P.op("dve", lambda e, d=d, Lb=Lb, x_=x_, i=i, cp=cp: e.scalar_tensor_tensor(
                        out=x_.t[:, :], in0=d.t[:, :], scalar=rwp.t[:, cp, i:i + 1], in1=Lb.t[:, 1:513], op0=ALU.mult, op1=ALU.add),
                        r=[d, Lb, rwp], w=[x_])
                    xs3.append(x_)
                xr, xk, xv = xs3
                pb = psr7.get()
                P.op("pe", lambda e, pb=pb, cp=cp: e.matmul(pb.t[:, 0:512], lhsT=wupb.t[0:96, cp * 128:(cp + 1) * 128],
                                                             rhs=tanw.t[0:96, :], start=True, stop=True), r=[wupb, tanw], w=[pb])
                sigw = T32("sigw")
                P.op("act", lambda e, pb=pb, sigw=sigw, cp=cp: e.activation(out=sigw.t[:, :], in_=pb.t[:, :], func=AF.Sigmoid,
                                                                            bias=rwp.t[:, cp, 3:4]), r=[pb, rwp], w=[sigw])
                cum = T32("cum")
                P.op("dve", lambda e, cum=cum, sigw=sigw: e.tensor_tensor_scan(out=cum.t[:, :], data0=scanm.t[:, :], data1=sigw.t[:, :],
                                                                               initial=0.0, op0=ALU.mult, op1=ALU.add),
                     r=[scanm, sigw], w=[cum])
                cpm = T32("cpm")
                P.op("pool", lambda e, cpm=cpm, cum=cum, sigw=sigw: e.tensor_tensor(out=cpm.t[:, :], in0=cum.t[:, :], in1=sigw.t[:, :],
                                                                                   op=ALU.subtract), r=[cum, sigw], w=[cpm])
                epos = T32("epos")
                eneg = T32("eneg")
                eprev = T32("eprev")
                P.op("act", lambda e, epos=epos, cum=cum: e.activation(out=epos.t[:, :], in_=cum.t[:, :], func=AF.Exp, scale=-C0),
                     r=[cum], w=[epos])
                P.op("act", lambda e, eneg=eneg, cum=cum: e.activation(out=eneg.t[:, :], in_=cum.t[:, :], func=AF.Exp, scale=C0),
                     r=[cum], w=[eneg])
                P.op("act", lambda e, eprev=eprev, cpm=cpm: e.activation(out=eprev.t[:, :], in_=cpm.t[:, :], func=AF.Exp, scale=-C0),
                     r=[cpm], w=[eprev])
                pb = psr7.get()
                P.op("pe", lambda e, pb=pb, cp=cp: e.matmul(pb.t[:, 0:512], lhsT=aupb.t[0:96, cp * 128:(cp + 1) * 128],
                                                             rhs=xsal.t[0:96, :], start=True, stop=True), r=[aupb, xsal], w=[pb])
                alr = T32("alr")
                P.op("act", lambda e, pb=pb, alr=alr, cp=cp: e.activation(out=alr.t[:, :], in_=pb.t[:, :], func=AF.Sigmoid,
                                                                          bias=rwp.t[:, cp, 4:5]), r=[pb, rwp], w=[alr])
                pb = psr7.get()
                for k in range(2):
                    P.op("pe", lambda e, pb=pb, cp=cp, k=k: e.matmul(pb.t[:, 0:512], lhsT=gupb.t[:, k, cp * 128:(cp + 1) * 128],
                                                                      rhs=sigg.t[:, k, :], start=(k == 0), stop=(k == 1)),
                         r=[gupb, sigg], w=[pb], acc=(k > 0))
                P.op("act", lambda e, pb=pb, cp=cp: e.activation(out=GF.t[:, cp, :], in_=pb.t[:, :], func=AF.Copy),
                     r=[pb], w=[GF], acc=True)
                sq = sq_r.get()
                P.op("act", lambda e, sq=sq, xk=xk, cp=cp: e.activation(out=sq.t[:, :], in_=xk.t[:, :], func=AF.Square,
                                                                        scale=rwp.t[:, cp, 5:6]), r=[xk, rwp], w=[sq])
                pb = psr7.get()
                P.op("pe", lambda e, pb=pb, sq=sq: e.matmul(pb.t[:, 0:512], lhsT=bones.t[:, :], rhs=sq.t[:, :], start=True, stop=True),
                     r=[bones, sq], w=[pb])
                rn = T32("d")
                P.op("act", lambda e, pb=pb, rn=rn: e.activation(out=rn.t[:, :], in_=pb.t[:, :], func=AF.Sqrt), r=[pb], w=[rn])
                P.op("dve", lambda e, rn=rn: e.tensor_scalar(out=rn.t[:, :], in0=rn.t[:, :], scalar1=1e-12, scalar2=None, op0=ALU.max),
                     r=[rn], w=[rn])
                P.op("dve", lambda e, rn=rn: e.reciprocal(out=rn.t[:, :], in_=rn.t[:, :]), r=[rn], w=[rn])
                kk = T32("cpm")
                P.op("dve", lambda e, kk=kk, xk=xk, rn=rn, cp=cp: e.scalar_tensor_tensor(
                    out=kk.t[:, :], in0=xk.t[:, :], scalar=rwp.t[:, cp, 5:6], in1=rn.t[:, :], op0=ALU.mult, op1=ALU.mult),
                    r=[xk, rn, rwp], w=[kk])
                tq = T32("d")
                P.op("dve", lambda e, tq=tq, alr=alr, cp=cp: e.tensor_scalar(out=tq.t[:, :], in0=alr.t[:, :], scalar1=-1.0,
                                                                             scalar2=rwp.t[:, cp, 6:7], op0=ALU.add, op1=ALU.mult),
                     r=[alr, rwp], w=[tq])
                kp = T32("sigw")
                P.op("dve", lambda e, kp=kp, tq=tq, xk=xk: e.scalar_tensor_tensor(out=kp.t[:, :], in0=tq.t[:, :], scalar=1.0,
                                                                                 in1=xk.t[:, :], op0=ALU.add, op1=ALU.mult),
                     r=[tq, xk], w=[kp])
                bb = T32("xsl")
                P.op("pool", lambda e, bb=bb, kk=kk, alr=alr: e.tensor_tensor(out=bb.t[:, :], in0=kk.t[:, :], in1=alr.t[:, :], op=ALU.mult),
                     r=[kk, alr], w=[bb])
                P.op("dve", lambda e, kk=kk, eprev=eprev, cp=cp: e.scalar_tensor_tensor(
                    out=AR.t[:, cp, :, 0:64], in0=v3(kk.t[:, :]), scalar=-1.0, in1=v3(eprev.t[:, :]), op0=ALU.mult, op1=ALU.mult),
                    r=[kk, eprev], w=[AR], acc=True)
                P.op("dve", lambda e, xr=xr, epos=epos, cp=cp: e.tensor_tensor(
                    out=AR.t[:, cp, :, 64:128], in0=v3(xr.t[:, :]), in1=v3(epos.t[:, :]), op=ALU.mult), r=[xr, epos], w=[AR], acc=True)
                P.op("pool", lambda e, kp=kp, eneg=eneg, cp=cp: e.tensor_tensor(out=Kt.t[:, cp, :], in0=kp.t[:, :], in1=eneg.t[:, :],
                                                                              op=ALU.mult), r=[kp, eneg], w=[Kt], acc=True)
                P.op("pool", lambda e, bb=bb, eneg=eneg, cp=cp: e.tensor_tensor(out=Bt.t[:, cp, :], in0=bb.t[:, :], in1=eneg.t[:, :],
                                                                              op=ALU.mult), r=[bb, eneg], w=[Bt], acc=True)
                P.op("pool", lambda e, epos=epos, cp=cp: e.tensor_tensor(
                    out=v3(KH.t[:, cp, :]), in0=v3(Kt.t[:, cp, :]), in1=v3(epos.t[:, :])[:, :, 63:64].to_broadcast([128, 8, 64]),
                    op=ALU.mult), r=[Kt, epos], w=[KH], acc=True)
                P.op("pool", lambda e, epos=epos, cp=cp: e.tensor_tensor(
                    out=v3(BH.t[:, cp, :]), in0=v3(Bt.t[:, cp, :]), in1=v3(epos.t[:, :])[:, :, 63:64].to_broadcast([128, 8, 64]),
                    op=ALU.mult), r=[Bt, epos], w=[BH], acc=True)
                P.op("act", lambda e, xv=xv, cp=cp: e.activation(out=VF.t[:, cp, :], in_=xv.t[:, :], func=AF.Copy),
                     r=[xv], w=[VF], acc=True)
                P.op("dve", lambda e, epos=epos, cp=cp: e.tensor_copy(out=pCs.t[:, cp, :], in_=v3(epos.t[:, :])[:, :, 63]),
                     r=[epos], w=[pCs], acc=True)
                sq2 = sq_r.get()
                P.op("dve", lambda e, sq2=sq2, xr=xr, kp=kp, cp=cp: e.scalar_tensor_tensor(
                    out=sq2.t[:, :], in0=xr.t[:, :], scalar=rwp.t[:, cp, 7:8], in1=kp.t[:, :], op0=ALU.mult, op1=ALU.mult),
                    r=[xr, kp, rwp], w=[sq2])
                pb = psr7.get()
                P.op("pe", lambda e, pb=pb, sq2=sq2: e.matmul(pb.t[:, 0:512], lhsT=bones.t[:, :], rhs=sq2.t[:, :], start=True, stop=True),
                     r=[bones, sq2], w=[pb])
                P.op("dve", lambda e, pb=pb, xv=xv, cp=cp: e.tensor_tensor(out=BON.t[:, cp, :], in0=pb.t[:, :], in1=xv.t[:, :], op=ALU.mult),
                     r=[pb, xv], w=[BON], acc=True)

            for ch in range(8 if DEBUG.get('rw_stop', 'full') != 'B' else 0):
                cs = slice(ch * 64, (ch + 1) * 64)
                tm = {}
                for key, src in (("v", VF), ("k", KH), ("b", BH)):
                    pb = psr7.get()
                    for cp in range(8):
                        P.op("pe", lambda e, pb=pb, src=src, cp=cp, cs=cs: e.transpose(
                            out=bf(pb)[0:64, cp * 128:(cp + 1) * 128], in_=src.t[:, cp, cs], identity=identb.t[:, :]),
                            r=[src, identb], w=[pb], acc=(cp > 0))
                    tmb = tm_r[key].get()
                    P.op("act", lambda e, pb=pb, tmb=tmb: e.activation(out=tmb.t[:, :], in_=bf(pb)[0:64, :], func=AF.Copy),
                         r=[pb], w=[tmb])
                    tm[key] = tmb
                Ms = {}
                for hd in range(2):
                    hp = slice(hd * 64, (hd + 1) * 64)
                    specs = (("ak", Kt, 0, mSU), ("ab", Bt, 0, mSU), ("rk", Kt, 64, mIU), ("rb", Bt, 64, mIU))
                    for key, L, off, msk in specs:
                        pb = psr7.get()
                        for cp in range(8):
                            P.op("pe", lambda e, pb=pb, L=L, cp=cp, off=off, hp=hp, ch=ch, cs=cs: e.matmul(
                                pb.t[0:64, cp * 64:(cp + 1) * 64], lhsT=L.t[hp, cp, cs], rhs=AR.t[hp, cp, ch, off:off + 64],
                                start=(cp == 0), stop=True, skip_group_check=True), r=[L, AR], w=[pb], acc=(cp > 0))
                        mb = m_r[key].get()
                        P.op("dve", lambda e, pb=pb, mb=mb, msk=msk: e.tensor_tensor(out=mb.t[:, :], in0=pb.t[0:64, :], in1=msk.t[:, :],
                                                                                   op=ALU.mult), r=[pb, msk], w=[mb])
                        Ms[(key, hd)] = mb
                    pb = psr7.get()
                    for cp in range(8):
                        P.op("pe", lambda e, pb=pb, cp=cp, hp=hp, ch=ch, cs=cs: e.matmul(
                            pb.t[0:64, cp * 64:(cp + 1) * 64], lhsT=AR.t[hp, cp, ch, 0:64], rhs=Bt.t[hp, cp, cs],
                            start=(cp == 0), stop=True, skip_group_check=True), r=[Bt, AR], w=[pb], acc=(cp > 0))
                    mb = m_r["mt"].get()
                    P.op("dve", lambda e, pb=pb, mb=mb: e.tensor_tensor(out=mb.t[:, :], in0=pb.t[0:64, :], in1=mSL.t[:, :], op=ALU.mult),
                         r=[pb, mSL], w=[mb])
                    Ms[("mt", hd)] = mb
                if DEBUG.get('rw_stop') == 'C':
                    continue
                Tm = {}
                for hd in range(2):
                    X = Ms[("ab", hd)]
                    Y = Ms[("mt", hd)]
                    Tb = t_r[hd].get()
                    TTb = tt_r[hd].get()
                    P.op("pool", lambda e, Tb=Tb, X=X: e.tensor_tensor(out=Tb.t[:, :], in0=X.t[:, :], in1=I8.t[:, :], op=ALU.add),
                         r=[X, I8], w=[Tb])
                    P.op("pool", lambda e, TTb=TTb, Y=Y: e.tensor_tensor(out=TTb.t[:, :], in0=Y.t[:, :], in1=I8.t[:, :], op=ALU.add),
                         r=[Y, I8], w=[TTb])
                    for lvl in range(5):
                        last = (lvl == 4)
                        pX = psr7.get()
                        for cp in range(8):
                            c_ = slice(cp * 64, (cp + 1) * 64)
                            P.op("pe", lambda e, pX=pX, X=X, Y=Y, c_=c_, cp=cp: e.matmul(
                                pX.t[0:64, c_], lhsT=Y.t[:, c_], rhs=X.t[:, c_], start=(cp == 0), stop=True, skip_group_check=True),
                                r=[X, Y], w=[pX], acc=(cp > 0))
                        X2 = x_r[hd].get()
                        P.op("act", lambda e, pX=pX, X2=X2: e.activation(out=X2.t[:, :], in_=pX.t[0:64, :], func=AF.Copy), r=[pX], w=[X2])
                        if not last:
                            pY = psr7.get()
                            for cp in range(8):
                                c_ = slice(cp * 64, (cp + 1) * 64)
                                P.op("pe", lambda e, pY=pY, X=X, Y=Y, c_=c_, cp=cp: e.matmul(
                                    pY.t[0:64, c_], lhsT=X.t[:, c_], rhs=Y.t[:, c_], start=(cp == 0), stop=True, skip_group_check=True),
                                    r=[X, Y], w=[pY], acc=(cp > 0))
                            Y2 = y_r[hd].get()
                            P.op("act", lambda e, pY=pY, Y2=Y2: e.activation(out=Y2.t[:, :], in_=pY.t[0:64, :], func=AF.Copy),
                                 r=[pY], w=[Y2])
                        pT = psr7.get()
                        for cp in range(8):
                            c_ = slice(cp * 64, (cp + 1) * 64)
                            P.op("pe", lambda e, pT=pT, TTb=TTb, X2=X2, c_=c_, cp=cp: e.matmul(
                                pT.t[0:64, c_], lhsT=TTb.t[:, c_], rhs=X2.t[:, c_], start=(cp == 0), stop=True, skip_group_check=True),
                                r=[TTb, X2], w=[pT], acc=(cp > 0))
                        Tn = t_r[hd].get()
                        P.op("dve", lambda e, pT=pT, Tn=Tn, Tb=Tb: e.tensor_tensor(out=Tn.t[:, :], in0=pT.t[0:64, :], in1=Tb.t[:, :], op=ALU.add),
                             r=[pT, Tb], w=[Tn])
                        if not last:
                            pTT = psr7.get()
                            for cp in range(8):
                                c_ = slice(cp * 64, (cp + 1) * 64)
                                P.op("pe", lambda e, pTT=pTT, TTb=TTb, X2=X2, c_=c_, cp=cp: e.matmul(
                                    pTT.t[0:64, c_], lhsT=X2.t[:, c_], rhs=TTb.t[:, c_], start=(cp == 0), stop=True, skip_group_check=True),
                                    r=[TTb, X2], w=[pTT], acc=(cp > 0))
                            TTn = tt_r[hd].get()
                            P.op("dve", lambda e, pTT=pTT, TTn=TTn, TTb=TTb: e.tensor_tensor(out=TTn.t[:, :], in0=pTT.t[0:64, :],
                                                                                           in1=TTb.t[:, :], op=ALU.add),
                                 r=[pTT, TTb], w=[TTn])
                            TTb = TTn
                            Y = Y2
                        Tb = Tn
                        X = X2
                    Tm[hd] = Tb
                if DEBUG.get('rw_stop') == 'D2':
                    continue
                ytm = {}
                for hd in range(2):
                    hp = slice(hd * 64, (hd + 1) * 64)
                    hcol = lambda cp, hd=hd: slice((cp * 2 + hd) * 64, (cp * 2 + hd + 1) * 64)
                    p1 = psr7.get()
                    for cp in range(8):
                        P.op("pe", lambda e, p1=p1, cp=cp, hp=hp, ch=ch: e.matmul(
                            p1.t[0:64, cp * 64:(cp + 1) * 64], lhsT=AR.t[hp, cp, ch, 0:64], rhs=STb.t[hp, cp, :],
                            start=(cp == 0), stop=True, skip_group_check=True), r=[AR, STb], w=[p1], acc=(cp > 0))
                    w1s = w1s_r.get()
                    P.op("act", lambda e, p1=p1, w1s=w1s: e.activation(out=w1s.t[:, :], in_=p1.t[0:64, :], func=AF.Copy), r=[p1], w=[w1s])
                    p2 = psr7.get()
                    mak = Ms[("ak", hd)]
                    for cp in range(8):
                        P.op("pe", lambda e, p2=p2, cp=cp, mak=mak, hcol=hcol, tmv=tm["v"]: e.matmul(
                            p2.t[0:64, cp * 64:(cp + 1) * 64], lhsT=mak.t[:, cp * 64:(cp + 1) * 64], rhs=tmv.t[:, hcol(cp)],
                            start=(cp == 0), stop=True, skip_group_check=True), r=[mak, tm["v"]], w=[p2], acc=(cp > 0))
                    wtm = wtm_r.get()
                    P.op("dve", lambda e, p2=p2, w1s=w1s, wtm=wtm: e.tensor_tensor(out=wtm.t[:, :], in0=p2.t[0:64, :], in1=w1s.t[:, :], op=ALU.add),
                         r=[p2, w1s], w=[wtm])
                    if DEBUG.get('rw_stop') == 'D3a':
                        continue
                    p3 = psr7.get()
                    Tb = Tm[hd]
                    for cp in range(8):
                        c_ = slice(cp * 64, (cp + 1) * 64)
                        P.op("pe", lambda e, p3=p3, Tb=Tb, wtm=wtm, c_=c_, cp=cp: e.matmul(
                            p3.t[0:64, c_], lhsT=Tb.t[:, c_], rhs=wtm.t[:, c_], start=(cp == 0), stop=True, skip_group_check=True),
                            r=[Tb, wtm], w=[p3], acc=(cp > 0))
                    utm = utm_r.get()
                    P.op("act", lambda e, p3=p3, utm=utm: e.activation(out=utm.t[:, :], in_=p3.t[0:64, :], func=AF.Copy), r=[p3], w=[utm])
                    if DEBUG.get('rw_stop') == 'D3b':
                        continue
                    p4 = psr7.get()
                    for cp in range(8):
                        P.op("pe", lambda e, p4=p4, cp=cp, hp=hp, ch=ch: e.matmul(
                            p4.t[0:64, cp * 64:(cp + 1) * 64], lhsT=AR.t[hp, cp, ch, 64:128], rhs=STb.t[hp, cp, :],
                            start=(cp == 0), stop=True, skip_group_check=True), r=[AR, STb], w=[p4], acc=(cp > 0))
                    y1s = y1s_r.get()
                    P.op("act", lambda e, p4=p4, y1s=y1s: e.activation(out=y1s.t[:, :], in_=p4.t[0:64, :], func=AF.Copy), r=[p4], w=[y1s])
                    p5 = psr7.get()
                    mrb = Ms[("rb", hd)]
                    mrk = Ms[("rk", hd)]
                    for cp in range(8):
                        c_ = slice(cp * 64, (cp + 1) * 64)
                        P.op("pe", lambda e, p5=p5, mrb=mrb, utm=utm, c_=c_, cp=cp: e.matmul(
                            p5.t[0:64, c_], lhsT=mrb.t[:, c_], rhs=utm.t[:, c_], start=(cp == 0), stop=False, skip_group_check=True),
                            r=[mrb, utm], w=[p5], acc=(cp > 0))
                        P.op("pe", lambda e, p5=p5, mrk=mrk, c_=c_, cp=cp, hcol=hcol, tmv=tm["v"]: e.matmul(
                            p5.t[0:64, c_], lhsT=mrk.t[:, c_], rhs=tmv.t[:, hcol(cp)], start=False, stop=True, skip_group_check=True),
                            r=[mrk, tm["v"]], w=[p5], acc=True)
                    yt = ytm_r.get()
                    P.op("dve", lambda e, p5=p5, y1s=y1s, yt=yt: e.tensor_tensor(out=yt.t[:, :], in0=p5.t[0:64, :], in1=y1s.t[:, :], op=ALU.add),
                         r=[p5, y1s], w=[yt])
                    ytm[hd] = yt
                    if DEBUG.get('rw_stop') == 'D3c':
                        continue
                    for cp in range(8):
                        c_ = slice(cp * 64, (cp + 1) * 64)
                        P.op("pe", lambda e, cp=cp, c_=c_, hp=hp, hcol=hcol, utm=utm, hd=hd, tmb_=tm["b"]: e.matmul(
                            pS.t[hp, c_], lhsT=tmb_.t[:, hcol(cp)], rhs=utm.t[:, c_], start=(cp == 0), stop=False, skip_group_check=True),
                            r=[tm["b"], utm], w=[pS], acc=not (hd == 0 and cp == 0))
                        P.op("pe", lambda e, cp=cp, c_=c_, hp=hp, hcol=hcol, tmk=tm["k"], tmv=tm["v"]: e.matmul(
                            pS.t[hp, c_], lhsT=tmk.t[:, hcol(cp)], rhs=tmv.t[:, hcol(cp)], start=False, stop=True, skip_group_check=True),
                            r=[tm["k"], tm["v"]], w=[pS], acc=True)
                if DEBUG.get('rw_stop') in ('D3a', 'D3b', 'D3c', 'D3d'):
                    continue
                P.op("dve", lambda e, ch=ch: e.tensor_tensor(out=S32.t[:, :, :], in0=S32.t[:, :, :],
                                                             in1=pCs.t[:, :, ch:ch + 1].to_broadcast([128, 8, 64]), op=ALU.mult),
                     r=[S32, pCs], w=[S32])
                P.op("dve", lambda e: e.tensor_tensor(out=S32.t[:, :, :], in0=S32.t[:, :, :],
                                                      in1=pS.t[:, :].rearrange("p (c i) -> p c i", c=8), op=ALU.add),
                     r=[S32, pS], w=[S32])
                P.op("act", lambda e: e.activation(out=STb.t[:, :, :], in_=S32.t[:, :, :], func=AF.Copy), r=[S32], w=[STb])
                if DEBUG.get('rw_stop') == 'D3':
                    continue
                pO = psr7.get()
                YH = yh_r.get()
                for hd in range(2):
                    yt = ytm[hd]
                    st = st_r.get()
                    P.op("dve", lambda e, yt=yt, st=st: e.tensor_reduce(out=st.t[:, 0:8], in_=y3(yt.t[:, :]), axis=AX.X, op=ALU.add),
                         r=[yt], w=[st])
                    ysq = ysq_r.get()
                    P.op("act", lambda e, yt=yt, ysq=ysq: e.activation(out=ysq.t[:, :], in_=yt.t[:, :], func=AF.Square), r=[yt], w=[ysq])
                    P.op("dve", lambda e, ysq=ysq, st=st: e.tensor_reduce(out=st.t[:, 8:16], in_=y3(ysq.t[:, :]), axis=AX.X, op=ALU.add),
                         r=[ysq, st], w=[st])
                    P.op("dve", lambda e, st=st: e.tensor_scalar(out=st.t[:, 16:24], in0=st.t[:, 0:8], scalar1=1.0 / 64, scalar2=None,
                                                                 op0=ALU.mult), r=[st], w=[st])
                    P.op("dve", lambda e, st=st: e.tensor_tensor(out=st.t[:, 24:32], in0=st.t[:, 16:24], in1=st.t[:, 16:24], op=ALU.mult),
                         r=[st], w=[st])
                    P.op("dve", lambda e, st=st: e.scalar_tensor_tensor(out=st.t[:, 32:40], in0=st.t[:, 8:16], scalar=1.0 / 64,
                                                                        in1=st.t[:, 24:32], op0=ALU.mult, op1=ALU.subtract),
                         r=[st], w=[st])
                    P.op("act", lambda e, st=st: e.activation(out=st.t[:, 40:48], in_=st.t[:, 32:40], func=AF.Sqrt, bias=gn_eps.t[:, 0:1]),
                         r=[st, gn_eps], w=[st])
                    P.op("dve", lambda e, st=st: e.reciprocal(out=st.t[:, 32:40], in_=st.t[:, 40:48]), r=[st], w=[st])
                    yc = ysq_r.get()
                    P.op("pool", lambda e, yt=yt, st=st, yc=yc: e.tensor_tensor(
                        out=y3(yc.t[:, :]), in0=y3(yt.t[:, :]), in1=st.t[:, 16:24].unsqueeze(2).to_broadcast([64, 8, 64]), op=ALU.subtract),
                        r=[yt, st], w=[yc])
                    P.op("pool", lambda e, yc=yc, st=st, hd=hd, YH=YH: e.tensor_tensor(
                        out=YH.t[:, :].rearrange("p (c h i) -> p c h i", c=8, h=2)[:, :, hd, :], in0=y3(yc.t[:, :]),
                        in1=st.t[:, 32:40].unsqueeze(2).to_broadcast([64, 8, 64]), op=ALU.mult),
                        r=[yc, st], w=[YH], acc=(hd > 0))
                if DEBUG.get('rw_stop') != 'E1':
                    for cp in range(8):
                        P.op("pe", lambda e, YH=YH, cp=cp, pO=pO: e.transpose(
                            out=bf(pO)[:, cp * 64:(cp + 1) * 64], in_=YH.t[:, cp * 128:(cp + 1) * 128], identity=identb.t[0:64, 0:64]),
                            r=[YH, identb], w=[pO], acc=(cp > 0))
                if DEBUG.get('rw_stop') in ('E1', 'E2a'):
                    continue
                P.op("act", lambda e, cs=cs, pO=pO: e.activation(func=AF.Copy, out=YT.t[:, :, cs], in_=bf(pO)[:, 0:512].rearrange("p (c t) -> p c t", c=8)),
                     r=[pO], w=[YT], acc=True)
            for cp in range(8 if DEBUG.get('rw_stop', 'full') == 'full' else 0):
                ta = T32("d")
                P.op("dve", lambda e, ta=ta, cp=cp: e.tensor_scalar(out=ta.t[:, :], in0=YT.t[:, cp, :], scalar1=rwp.t[:, cp, 8:9],
                                                                    scalar2=rwp.t[:, cp, 9:10], op0=ALU.mult, op1=ALU.add),
                     r=[YT, rwp], w=[ta])
                tb_ = T32("xsl")
                P.op("dve", lambda e, ta=ta, tb_=tb_, cp=cp: e.tensor_tensor(out=tb_.t[:, :], in0=ta.t[:, :], in1=BON.t[:, cp, :], op=ALU.add),
                     r=[ta, BON], w=[tb_])
                ost = ost_r.get()
                P.op("dve", lambda e, tb_=tb_, ost=ost, cp=cp: e.tensor_tensor(out=ost.t[:, :], in0=tb_.t[:, :], in1=GF.t[:, cp, :], op=ALU.mult),
                     r=[tb_, GF], w=[ost])
                P.op("sp", lambda e, ost=ost, cp=cp, tb=tb: e.dma_start(
                    out=oT.t[tb, 1024 + cp * 128:1024 + (cp + 1) * 128, :], in_=ost.t[:, :]), r=[ost], w=[oT], acc=True, dma=True)

    P.push_scope()
    phase_rwkv()
    P.barrier()
    P.pop_scope()

    if stop_after <= 3:
        P.emit()
        return nc, P, dbg

    NB4_ = SO_ // 512

    def own_off(pid, tb):
        v = pid * (NB4_ * 4096)
        if tb:
            v = v + tb * 4096
        return v

    og = P.dram("og", [NBLK * 4096, 512], BF16)
    RG = [[2 * i, 2 * i + 1] for i in range(ncore // 2)]
    if not DEBUG.get("no_cc"):
        for k in range(NBLK):
            P.op("pool", lambda e, k=k: e.collective_compute("AllGather", ALU.bypass, replica_groups=RG, ins=[oT.t[k]], outs=[og.t[k * 4096:(k + 1) * 4096, :]]),
                 r=[oT], w=[og], acc=True, cc=True)
    ogs = P.dram("ogs", [NB4_ * 4096, 512], BF16)
    if not DEBUG.get("no_cc"):
        for tb in range(NB4_):
            P.op("pool", lambda e, tb=tb: e.dma_start(
                out=ogs.t[tb * 4096:(tb + 1) * 4096, :], in_=og.t[bass.ds(own_off(P.get_pid(e), tb), 4096), :]),
                r=[og], w=[ogs], acc=True, dma=True)
    P.barrier()


    def phase4():
        NB4 = SO_ // 512 if DEBUG.get("p4_stop") != "xchg" else 0
        psr.i = 0
        pv = P.sb("pv_sb", [128, 80], F32)
        onesb = P.sb("onesb", [128, 128], BF16)
        P.op("sp", lambda e: e.dma_start(out=pv.t[:, :], in_=pv_in.t[:, :]), w=[pv], dma=True)
        P.op("pool", lambda e: e.memset(onesb.t[:, :], 1.0), w=[onesb])
        xT = P.sb("xT", [128, 16, 512], F32)
        wr = Ring(P, "w4", 2, [128, 8192], BF16)
        xt_r = Ring(P, "xt4", 2, [128, D], F32)
        sqb = P.sb("sqb", [128, 16, 512], BF16)
        N1 = P.sb("N1", [128, 16, 512], BF16)
        rs_r = Ring(P, "rs4", 2, [128, 512], F32)
        tmp_r = Ring(P, "tmp4", 2, [128, 512], F32)
        ss_r = Ring(P, "ss4", 4, [128, 2], F32)

        def wtile(src, c0, n, kcn=16):
            wt = wr.get()
            P.op("sp", lambda e, wt=wt: e.dma_start(
                out=wt.t[:, 0:kcn * n].rearrange("p (k c) -> p k c", k=kcn),
                in_=src.t[:, c0:c0 + n].rearrange("(k p) c -> p k c", p=128)), r=[src], w=[wt], dma=True)
            return wt

        def wv(wt, n, kcn=16):
            return wt.t[:, 0:kcn * n].rearrange("p (k c) -> p k c", k=kcn)

        def colsum_rstd(srcsq, eps_col):
            pb = psr.get()
            for kc in range(16):
                P.op("pe", lambda e, pb=pb, kc=kc: e.matmul(pb.t[:, 0:512], lhsT=onesb.t[:, :], rhs=srcsq.t[:, kc, :],
                                                            start=(kc == 0), stop=(kc == 15)), r=[onesb, srcsq], w=[pb], acc=(kc > 0))
            rs = rs_r.get()
            P.op("act", lambda e, pb=pb, rs=rs: e.activation(out=rs.t[:, :], in_=pb.t[:, :], func=AF.Sqrt, bias=epsb.t[:, 0:1],
                                                             scale=1.0 / D), r=[pb, epsb], w=[rs])
            P.op("dve", lambda e, rs=rs: e.reciprocal(out=rs.t[:, :], in_=rs.t[:, :]), r=[rs], w=[rs])
            return rs

        for tb in range(NB4):
            P.push_scope()
            hT = P.sb("hT4_%d" % tb, [128, 16, 512], BF16)
            G4_r = Ring(P, "G4_%d" % tb, 2, [128, 4, 512], BF16)
            oab = P.sb("oab_%d" % tb, [128, 16, 512], BF16)
            mixA = P.sb("mixA_%d" % tb, [128, 16, 512], BF16)
            mixb = P.sb("mixb_%d" % tb, [128, 16, 512], BF16)
            xn_r = Ring(P, "xn4_%d" % tb, 1, [128, D], BF16)
            for tt in range(4):
                t0 = tb * 512 + tt * 128
                xt = xt_r.get()
                P.op("sp", lambda e, xt=xt, t0=t0: e.dma_start(out=xt.t[:, :], in_=xo_in.t[t0:t0 + 128, :]), w=[xt], dma=True)
                ss = ss_r.get()
                P.op("pool", lambda e, ss=ss: e.memset(ss.t[:, :], 0.0), w=[ss])
                P.op("act", lambda e, xt=xt, ss=ss: e.activation(out=sqb.t[:, 0:4, :].rearrange("p a b -> p (a b)"), in_=xt.t[:, :],
                                                                 func=AF.Square, accum_out=ss.t[:, 0:1]), r=[xt], w=[sqb, ss])
                P.op("act", lambda e, ss=ss: e.activation(out=ss.t[:, 1:2], in_=ss.t[:, 0:1], func=AF.Sqrt,
                                                          bias=epsb.t[:, 0:1], scale=1.0 / D), r=[ss, epsb], w=[ss])
                P.op("dve", lambda e, ss=ss: e.reciprocal(out=ss.t[:, 0:1], in_=ss.t[:, 1:2]), r=[ss], w=[ss])
                xn = xn_r.get()
                P.op("dve", lambda e, xt=xt, ss=ss, xn=xn: e.scalar_tensor_tensor(
                    out=xn.t[:, :], in0=xt.t[:, :], scalar=ss.t[:, 0:1], in1=gpre_b.t[:, :], op0=ALU.mult, op1=ALU.mult),
                    r=[xt, ss, gpre_b], w=[xn])
                for half in range(2):
                    pb = psr.get()
                    for j in range(8):
                        kc = half * 8 + j
                        P.op("pe", lambda e, pb=pb, xn=xn, kc=kc, j=j: e.transpose(
                            out=bf(pb)[:, j * 128:(j + 1) * 128], in_=xn.t[:, kc * 128:(kc + 1) * 128], identity=identb.t[:, :]),
                            r=[xn, identb], w=[pb], acc=(j > 0))
                    P.op("act", lambda e, pb=pb, hT=hT, half=half, tt=tt: e.activation(
                        out=hT.t[:, half * 8:half * 8 + 8, tt * 128:(tt + 1) * 128],
                        in_=bf(pb).rearrange("p (j t) -> p j t", j=8), func=AF.Copy), r=[pb], w=[hT], acc=True)
                xh = xn_r.get()
                P.op("act", lambda e, xt=xt, xh=xh: e.activation(out=xh.t[:, :], in_=xt.t[:, :], func=AF.Copy), r=[xt], w=[xh])
                for half in range(2):
                    pb = psr.get()
                    for j in range(8):
                        kc = half * 8 + j
                        P.op("pe", lambda e, pb=pb, xh=xh, kc=kc, j=j: e.transpose(
                            out=bf(pb)[:, j * 128:(j + 1) * 128], in_=xh.t[:, kc * 128:(kc + 1) * 128], identity=identb.t[:, :]),
                            r=[xh, identb], w=[pb], acc=(j > 0))
                    P.op("act", lambda e, pb=pb, half=half, tt=tt: e.activation(
                        out=xT.t[:, half * 8:half * 8 + 8, tt * 128:(tt + 1) * 128],
                        in_=bf(pb).rearrange("p (j t) -> p j t", j=8), func=AF.Copy), r=[pb], w=[xT], acc=True)
            if DEBUG.get('p4_stop') == 'A1':
                P.barrier()
                P.pop_scope()
                continue
            for br, (wsrc, goff) in enumerate(((wab, 0), (wbb, 16))):
                for r_ in range(2):
                    row0 = tb * 4096 + r_ * 2048 + br * 1024
                    P.op("sp", lambda e, r_=r_, row0=row0, oab=oab: e.dma_start(
                        out=oab.t[:, r_ * 8:(r_ + 1) * 8, :],
                        in_=ogs.t[row0:row0 + 1024, :].rearrange("(k p) t -> p k t", p=128)),
                        r=[ogs], w=[oab], acc=(r_ > 0), dma=True)
                for q4 in range(4):
                    wgt = wtile(wgb, goff * 128 + q4 * 512, 512)
                    G4 = G4_r.get()
                    for c4 in range(4):
                        pb = psr.get()
                        for kc in range(16):
                            P.op("pe", lambda e, pb=pb, wgt=wgt, c4=c4, kc=kc, hT=hT: e.matmul(
                                pb.t[:, 0:512], lhsT=wv(wgt, 512)[:, kc, c4 * 128:(c4 + 1) * 128], rhs=hT.t[:, kc, :],
                                start=(kc == 0), stop=(kc == 15)), r=[wgt, hT], w=[pb], acc=(kc > 0))
                        gch = goff + q4 * 4 + c4
                        P.op("act", lambda e, pb=pb, G4=G4, c4=c4, gch=gch: e.activation(
                            out=G4.t[:, c4, :], in_=pb.t[:, :], func=AF.Sigmoid, bias=pv.t[:, gch:gch + 1]),
                            r=[pb, pv], w=[G4], acc=(c4 > 0))
                    wbt = wtile(wsrc, q4 * 512, 512)
                    for c4 in range(4):
                        cc = q4 * 4 + c4
                        pb = psr.get()
                        for kc in range(16):
                            P.op("pe", lambda e, pb=pb, wbt=wbt, c4=c4, kc=kc, oab=oab: e.matmul(
                                pb.t[:, 0:512], lhsT=wv(wbt, 512)[:, kc, c4 * 128:(c4 + 1) * 128], rhs=oab.t[:, kc, :],
                                start=(kc == 0), stop=(kc == 15)), r=[wbt, oab], w=[pb], acc=(kc > 0))
                        if br == 0:
                            P.op("dve", lambda e, pb=pb, G4=G4, c4=c4, cc=cc, mixA=mixA: e.tensor_tensor(
                                out=mixA.t[:, cc, :], in0=pb.t[:, :], in1=G4.t[:, c4, :], op=ALU.mult),
                                r=[pb, G4], w=[mixA], acc=True)
                        else:
                            tmp = tmp_r.get()
                            P.op("dve", lambda e, pb=pb, G4=G4, c4=c4, tmp=tmp: e.tensor_tensor(
                                out=tmp.t[:, :], in0=pb.t[:, :], in1=G4.t[:, c4, :], op=ALU.mult), r=[pb, G4], w=[tmp])
                            P.op("pool", lambda e, tmp=tmp, cc=cc, mixA=mixA, mixb=mixb: e.tensor_tensor(
                                out=mixb.t[:, cc, :], in0=tmp.t[:, :], in1=mixA.t[:, cc, :], op=ALU.add),
                                r=[tmp, mixA], w=[mixb], acc=True)
            if DEBUG.get('p4_stop') == 'A2':
                P.barrier()
                P.pop_scope()
                continue
            for q4 in range(4):
                wot = wtile(wob, q4 * 512, 512)
                for c4 in range(4):
                    cc = q4 * 4 + c4
                    pb = psr.get()
                    for kc in range(16):
                        P.op("pe", lambda e, pb=pb, wot=wot, c4=c4, kc=kc, mixb=mixb: e.matmul(
                            pb.t[:, 0:512], lhsT=wv(wot, 512)[:, kc, c4 * 128:(c4 + 1) * 128], rhs=mixb.t[:, kc, :],
                            start=(kc == 0), stop=(kc == 15)), r=[wot, mixb], w=[pb], acc=(kc > 0))
                    P.op("dve", lambda e, pb=pb, cc=cc, mixA=mixA: e.tensor_copy(out=mixA.t[:, cc, :], in_=pb.t[:, :]),
                         r=[pb], w=[mixA], acc=True)
                    P.op("act", lambda e, cc=cc, mixA=mixA: e.activation(out=sqb.t[:, cc, :], in_=mixA.t[:, cc, :], func=AF.Square),
                         r=[mixA], w=[sqb], acc=True)
            rs = colsum_rstd(sqb, 0)
            for cc in range(16):
                tmp = tmp_r.get()
                P.op("dve", lambda e, tmp=tmp, cc=cc, rs=rs, mixA=mixA: e.scalar_tensor_tensor(
                    out=tmp.t[:, :], in0=mixA.t[:, cc, :], scalar=pv.t[:, 32 + cc:33 + cc], in1=rs.t[:, :], op0=ALU.mult, op1=ALU.mult),
                    r=[mixA, pv, rs], w=[tmp])
                P.op("act", lambda e, tmp=tmp, cc=cc: e.activation(out=N1.t[:, cc, :], in_=tmp.t[:, :], func=AF.Copy), r=[tmp], w=[N1], acc=True)
                P.op("pool", lambda e, tmp=tmp, cc=cc: e.tensor_tensor(out=xT.t[:, cc, :], in0=tmp.t[:, :], in1=xT.t[:, cc, :], op=ALU.add),
                     r=[tmp, xT], w=[xT], acc=True)
            P.barrier()
            P.pop_scope()
            if DEBUG.get("p4_stop") == "A":
                continue
            P.push_scope()
            h2T = P.sb("h2T_%d" % tb, [128, 16, 512], BF16)
            FT = P.sb("FT_%d" % tb, [128, 44, 512], BF16)
            yo = P.sb("yo_%d" % tb, [128, 16, 512], BF16)
            for cc in range(16):
                P.op("act", lambda e, cc=cc: e.activation(out=sqb.t[:, cc, :], in_=xT.t[:, cc, :], func=AF.Square), r=[xT], w=[sqb], acc=True)
            rs = colsum_rstd(sqb, 0)
            for cc in range(16):
                P.op("dve", lambda e, cc=cc, rs=rs, h2T=h2T: e.scalar_tensor_tensor(
                    out=h2T.t[:, cc, :], in0=xT.t[:, cc, :], scalar=pv.t[:, 48 + cc:49 + cc], in1=rs.t[:, :], op0=ALU.mult, op1=ALU.mult),
                    r=[xT, pv, rs], w=[h2T], acc=True)
            for i in range(11):
                wgt = wtile(wf1b, i * 512, 512)
                wut = wtile(wf1b, 5632 + i * 512, 512)
                for c4 in range(4):
                    pg = psr.get()
                    for kc in range(16):
                        P.op("pe", lambda e, pg=pg, wgt=wgt, c4=c4, kc=kc, h2T=h2T: e.matmul(
                            pg.t[:, 0:512], lhsT=wv(wgt, 512)[:, kc, c4 * 128:(c4 + 1) * 128], rhs=h2T.t[:, kc, :],
                            start=(kc == 0), stop=(kc == 15)), r=[wgt, h2T], w=[pg], acc=(kc > 0))
                    pu = psr.get()
                    for kc in range(16):
                        P.op("pe", lambda e, pu=pu, wut=wut, c4=c4, kc=kc, h2T=h2T: e.matmul(
                            pu.t[:, 0:512], lhsT=wv(wut, 512)[:, kc, c4 * 128:(c4 + 1) * 128], rhs=h2T.t[:, kc, :],
                            start=(kc == 0), stop=(kc == 15)), r=[wut, h2T], w=[pu], acc=(kc > 0))
                    tmp = tmp_r.get()
                    P.op("act", lambda e, pg=pg, tmp=tmp: e.activation(out=tmp.t[:, :], in_=pg.t[:, :], func=AF.Silu), r=[pg], w=[tmp])
                    P.op("dve", lambda e, pu=pu, tmp=tmp, i=i, c4=c4, FT=FT: e.tensor_tensor(
                        out=FT.t[:, i * 4 + c4, :], in0=pu.t[:, :], in1=tmp.t[:, :], op=ALU.mult), r=[pu, tmp], w=[FT], acc=True)
            for cc in range(16):
                w2t = wtile(wf2b, cc * 128, 128, kcn=44)
                pb = psr.get()
                for kc in range(44):
                    P.op("pe", lambda e, pb=pb, w2t=w2t, kc=kc, FT=FT: e.matmul(
                        pb.t[:, 0:512], lhsT=wv(w2t, 128, 44)[:, kc, :], rhs=FT.t[:, kc, :],
                        start=(kc == 0), stop=(kc == 43)), r=[w2t, FT], w=[pb], acc=(kc > 0))
                P.op("dve", lambda e, pb=pb, cc=cc, yo=yo: e.tensor_copy(out=yo.t[:, cc, :], in_=pb.t[:, :]), r=[pb], w=[yo], acc=True)
                P.op("act", lambda e, cc=cc, yo=yo: e.activation(out=sqb.t[:, cc, :], in_=yo.t[:, cc, :], func=AF.Square),
                     r=[yo], w=[sqb], acc=True)
            rs = colsum_rstd(sqb, 0)
            for cc in range(16):
                tmp = tmp_r.get()
                P.op("dve", lambda e, tmp=tmp, cc=cc, rs=rs, yo=yo: e.scalar_tensor_tensor(
                    out=tmp.t[:, :], in0=yo.t[:, cc, :], scalar=pv.t[:, 64 + cc:65 + cc], in1=rs.t[:, :], op0=ALU.mult, op1=ALU.mult),
                    r=[yo, pv, rs], w=[tmp])
                P.op("pool", lambda e, tmp=tmp, cc=cc, yo=yo: e.tensor_tensor(out=yo.t[:, cc, :], in0=tmp.t[:, :], in1=N1.t[:, cc, :], op=ALU.add),
                     r=[tmp, N1], w=[yo], acc=True)
            for tt in range(4):
                r0 = tb * 512 + tt * 128
                xt = xt_r.get()
                P.op("sp", lambda e, xt=xt, r0=r0: e.dma_start(out=xt.t[:, :], in_=xo_in.t[r0:r0 + 128, :]), w=[xt], dma=True)
                for half in range(2):
                    pb = psr.get()
                    for j in range(8):
                        kc = half * 8 + j
                        P.op("pe", lambda e, pb=pb, kc=kc, j=j, tt=tt, yo=yo: e.transpose(
                            out=bf(pb)[:, j * 128:(j + 1) * 128], in_=yo.t[:, kc, tt * 128:(tt + 1) * 128], identity=identb.t[:, :]),
                            r=[yo, identb], w=[pb], acc=(j > 0))
                    P.op("dve", lambda e, pb=pb, xt=xt, half=half: e.tensor_tensor(
                        out=xt.t[:, half * 1024:(half + 1) * 1024], in0=bf(pb)[:, :], in1=xt.t[:, half * 1024:(half + 1) * 1024], op=ALU.add),
                        r=[pb, xt], w=[xt], acc=True)
                P.op("sp", lambda e, xt=xt, r0=r0: e.dma_start(out=out_t.t[r0:r0 + 128, :], in_=xt.t[:, :]), r=[xt], w=[out_t], acc=True, dma=True)
            P.barrier()
            P.pop_scope()

    P.push_scope()
    phase4()
    P.barrier()
    P.pop_scope()

    if stop_after <= 4:
        P.emit()
        return nc, P, dbg

    P.emit()
    return nc, P, dbg


def host_inputs(inputs, nsb=8, ncore=8):
    x = np.asarray(inputs["x"], np.float32)
    w_in = np.asarray(inputs["w_in"], np.float32)[0]
    half = 64
    inv = (10000.0 ** (-np.arange(half, dtype=np.float32) / half)).astype(np.float32)
    ang = np.arange(S, dtype=np.float32)[None, :] * inv[:, None]
    cosT = np.concatenate([np.cos(ang), np.cos(ang)], 0).astype(np.float32)
    sinT = np.concatenate([np.sin(ang), np.sin(ang)], 0).astype(np.float32)
    rmat = np.zeros((128, 128), np.float32)
    for dp in range(64):
        rmat[dp + 64, dp] = -1.0
        rmat[dp, dp + 64] = 1.0
    ident = np.eye(128, dtype=np.float32)
    tri = np.arange(64)
    mSU = np.tile((tri[:, None] < tri[None, :]).astype(np.float32), (1, 8))
    mIU = np.tile((tri[:, None] <= tri[None, :]).astype(np.float32), (1, 8))
    mSL = np.tile((tri[:, None] > tri[None, :]).astype(np.float32), (1, 8))
    I8c = np.tile(np.eye(64, dtype=np.float32), (1, 8))
    scanm = np.ones((128, 512), np.float32)
    scanm[:, 0::64] = 0.0
    bones = np.zeros((128, 128), np.float32)
    bones[0:64, 0:64] = 1.0
    bones[64:128, 64:128] = 1.0
    maps = []
    S_att = nsb * 1024
    SO_ = S_att // 2
    g1 = lambda k: np.asarray(inputs[k], np.float32)[0]
    pvec = np.concatenate([g1("b_gate").reshape(32, 128).T, g1("g_mix_post").reshape(16, 128).T,
                           g1("g_ffn_pre").reshape(16, 128).T, g1("g_ffn_post").reshape(16, 128).T], 1)
    pvec = np.ascontiguousarray(pvec, np.float32)
    wg = np.ascontiguousarray(w_in[:, 12736:])
    for c in range(ncore):
        b, hh = c // 2, c % 2
        qs = slice(hh * 1024, (hh + 1) * 1024)
        cols = np.concatenate([
            np.arange(0, 2048)[qs], np.arange(2048, 4096)[qs], np.arange(4096, 6144)[qs],
            6144 + np.arange(0, 2048)[qs], 6144 + 2048 + np.arange(0, 2048)[qs], 6144 + 4096 + np.arange(0, 2048)[qs],
            6144 + 6144 + np.arange(0, 448)])
        m = {
            "x": np.ascontiguousarray(x[b, 0:S_att]),
            "xo": np.ascontiguousarray(x[b, hh * SO_:(hh + 1) * SO_]),
            "wg": wg, "wa": g1("w_branch_a"), "wb": g1("w_branch_b"), "wo": g1("w_out"),
            "wf1": g1("w_ffn_in"), "wf2": g1("w_ffn_out"), "pvec": pvec,
            "w1": np.ascontiguousarray(w_in[:, cols]),
            "gpre": np.ascontiguousarray(np.asarray(inputs["g_mix_pre"], np.float32)[0][None, :]),
            "cosT": cosT, "sinT": sinT, "rmat": rmat, "ident": ident,
            "lqk": np.ascontiguousarray(np.concatenate([np.asarray(inputs["da_lambda_q"], np.float32)[0].reshape(-1),
                                                        np.asarray(inputs["da_lambda_k"], np.float32)[0].reshape(-1)])[None, :]),
            "gsub": np.ascontiguousarray(np.asarray(inputs["da_subln_g"], np.float32)[0][None, :]),
            "rwp": rw_params(inputs, hh), "mul": rw_mul(inputs),
            "wup": np.ascontiguousarray(np.asarray(inputs["rw_w_up"], np.float32)[0][:, qs]),
            "aup": np.ascontiguousarray(np.asarray(inputs["rw_a_up"], np.float32)[0][:, qs]),
            "gup": np.ascontiguousarray(np.asarray(inputs["rw_g_up"], np.float32)[0][:, qs]),
            "mSU": mSU, "mIU": mIU, "mSL": mSL, "I8": I8c, "scanm": scanm, "bones": bones,
        }
        maps.append(m)
    return maps


def rw_params(inputs, hh):
    g = lambda k: np.asarray(inputs[k], np.float32)[0].reshape(-1)
    mu = g("rw_mu")
    cols = hh * 1024 + np.arange(1024)
    vecs = [mu[cols], mu[2048 + cols], mu[4096 + cols], g("rw_w0")[cols], g("rw_a0")[cols], g("rw_k_k")[cols],
            g("rw_k_a")[cols], g("rw_r_k")[cols], g("rw_ln_w")[cols], g("rw_ln_b")[cols]]
    a = np.stack(vecs, -1).reshape(8, 128, 10).transpose(1, 0, 2)
    return np.ascontiguousarray(a)


def rw_mul(inputs):
    mu = np.asarray(inputs["rw_mu"], np.float32)[0]
    a = np.zeros((128, 4), np.float32)
    a[0:96, 0] = mu[6144:6240]
    a[0:96, 1] = mu[6240:6336]
    a[:, 2] = mu[6336:6464]
    a[:, 3] = mu[6464:6592]
    return a


def kernel(**inputs):
    nc, P, dbg = build(nsb=8, ncore=8)
    maps = host_inputs(inputs, nsb=8, ncore=8)
    res = run_bass_kernel_spmd(nc, maps, core_ids=list(range(8)))
    out = np.zeros((4, S, D), np.float32)
    for c in range(8):
        b, hh = c // 2, c % 2
        out[b, hh * SO:(hh + 1) * SO] = res.results[c]["out"]
    return out
```
